# Optimizing a Trainium2 kernel written in Bass

```python
import math
import jax
import jax.numpy as jnp
from jax import lax
import numpy as np

D_MODEL = 1024
BATCH = 32
SEQ = 256
DEPTH = 2
DEC_BATCH = 2
DEC_SEQ = 1024
PAST_LEN = 256

GRID_W = 64
POOL_WINDOWS = (2, 4, 8, 16)
POOL_GROUPS = 4
POOL_CH = D_MODEL // 16
POOL_WIDTH = POOL_GROUPS * POOL_CH
FNET_HEADS = 4
FNET_CH = D_MODEL // 16
FNET_WIDTH = FNET_HEADS * FNET_CH
ATTN_HEADS = 4
QK_DIM = D_MODEL // 16
V_DIM = 2 * QK_DIM
Q_WIDTH = ATTN_HEADS * 2 * QK_DIM
ATTN_WIDTH = ATTN_HEADS * V_DIM
IN_WIDTH = POOL_WIDTH + FNET_WIDTH + 2 * Q_WIDTH + ATTN_WIDTH
MIX_WIDTH = POOL_WIDTH + FNET_WIDTH + ATTN_WIDTH
IN_SPLITS = (POOL_WIDTH, POOL_WIDTH + FNET_WIDTH, POOL_WIDTH + FNET_WIDTH + Q_WIDTH, POOL_WIDTH + FNET_WIDTH + 2 * Q_WIDTH)
D_FF = 2816
N_MOD = 9
ROPE_BASE = 10000.0
ROPE_AXIS_DIM = QK_DIM // 2
Q_BLOCK = 128
ATTN_SCALE = QK_DIM ** -0.5
EPS = 1e-6

kernel_name = 'hybrid_pool_fourier_diffattn_prefix_step'


def rmsnorm(x, g):
    xf = x.astype(jnp.float32)
    y = xf * lax.rsqrt(jnp.mean(xf * xf, axis=-1, keepdims=True) + EPS)
    return (y * g.astype(jnp.float32)).astype(x.dtype)


def adaln_params(cond, ada_w, ada_b):
    m = jax.nn.silu(cond) @ ada_w + ada_b
    return m.reshape(cond.shape[:-1] + (N_MOD, D_MODEL))


def mod_norm(x, g, mod, s):
    return rmsnorm(x, g) * (1 + mod[..., 3 * s + 1, :]) + mod[..., 3 * s, :]


def swiglu(h, wi, wo):
    a, b = jnp.split(h @ wi, 2, axis=-1)
    return (jax.nn.silu(a) * b) @ wo


def multiscale_pool(p, pool_w, pool_scale):
    B, L, _ = p.shape
    pf = p.reshape(B, L, POOL_GROUPS, POOL_CH).astype(jnp.float32)
    cs = jnp.concatenate([jnp.zeros_like(pf[:, :1]), jnp.cumsum(pf, axis=1)], axis=1)
    t = np.arange(L)
    outs = []
    for g, w in enumerate(POOL_WINDOWS):
        lo = np.clip(t - w // 2, 0, L)
        hi = np.clip(t + w // 2, 0, L)
        cnt = (hi - lo).astype(np.float32)
        mean = (cs[:, hi, g] - cs[:, lo, g]) / cnt[None, :, None]
        outs.append(mean - pf[:, :, g])
    d = jnp.stack(outs, axis=2).astype(p.dtype)
    y = jnp.einsum('blgc,gcd->blgd', d, pool_w).reshape(B, L, POOL_WIDTH)
    return y * pool_scale


def fourier_mix(f, fnet_w):
    B, L, _ = f.shape
    fg = f.reshape(B, L, FNET_HEADS, FNET_CH).astype(jnp.float32)
    z = jnp.fft.fft2(fg, axes=(1, 3), norm='ortho').real.astype(f.dtype)
    return jnp.einsum('blgc,gcd->blgd', z, fnet_w).reshape(B, L, FNET_WIDTH)


def axial_rope_tables(L):
    rows = L // GRID_W
    row = jnp.repeat(jnp.arange(rows, dtype=jnp.float32), GRID_W)
    col = jnp.tile(jnp.arange(GRID_W, dtype=jnp.float32), rows)
    inv = 1.0 / (ROPE_BASE ** (jnp.arange(0, ROPE_AXIS_DIM, 2, dtype=jnp.float32) / ROPE_AXIS_DIM))
    ang_r = row[:, None] * inv[None, :]
    ang_c = col[:, None] * inv[None, :]
    return (jnp.cos(ang_r), jnp.sin(ang_r), jnp.cos(ang_c), jnp.sin(ang_c))


def _rotate(x, cos, sin):
    cos = cos[:, None, None, :]
    sin = sin[:, None, None, :]
    x1, x2 = jnp.split(x, 2, axis=-1)
    return jnp.concatenate([x1 * cos - x2 * sin, x2 * cos + x1 * sin], axis=-1)


def apply_axial_rope(x, tables):
    cos_r, sin_r, cos_c, sin_c = tables
    xf = x.astype(jnp.float32)
    xr = _rotate(xf[..., :ROPE_AXIS_DIM], cos_r, sin_r)
    xc = _rotate(xf[..., ROPE_AXIS_DIM:], cos_c, sin_c)
    return jnp.concatenate([xr, xc], axis=-1).astype(x.dtype)


def diff_attention(q, k, v, lam):
    B, Lq = q.shape[0], q.shape[1]
    nb = Lq // Q_BLOCK
    qb = jnp.moveaxis(q.reshape((B, nb, Q_BLOCK) + q.shape[2:]), 1, 0)

    def one_block(qblk):
        s = jnp.einsum('bqhcd,bkhcd->bchqk', qblk, k, preferred_element_type=jnp.float32) * ATTN_SCALE
        p = jax.nn.softmax(s, axis=-1)
        w = p[:, 0] - lam * p[:, 1]
        return jnp.einsum('bhqk,bkhd->bqhd', w.astype(v.dtype), v)

    o = lax.map(one_block, qb)
    return jnp.moveaxis(o, 0, 1).reshape((B, Lq) + v.shape[2:])


def token_mix(h, ctx_k, ctx_v, lam, lam_init, w_in, w_out, q_norm_g, k_norm_g, attn_out_g, pool_w, pool_scale, fnet_w):
    B, L, _ = h.shape
    u = h @ w_in
    p, f, q, k, v = jnp.split(u, IN_SPLITS, axis=-1)
    q = rmsnorm(q.reshape(B, L, ATTN_HEADS, 2, QK_DIM), q_norm_g)
    k = rmsnorm(k.reshape(B, L, ATTN_HEADS, 2, QK_DIM), k_norm_g)
    v = v.reshape(B, L, ATTN_HEADS, V_DIM)
    if ctx_k is None:
        o = diff_attention(q, k, v, lam)
    else:
        tables = axial_rope_tables(L)
        q = apply_axial_rope(q, tables)
        k = apply_axial_rope(k, tables)
        ck = ctx_k.reshape(ctx_k.shape[:3] + (2, QK_DIM)).astype(k.dtype)
        k_all = jnp.concatenate([ck, k], axis=1)
        v_all = jnp.concatenate([ctx_v.astype(v.dtype), v], axis=1)
        o = diff_attention(q, k_all, v_all, lam)
    o = rmsnorm(o, attn_out_g) * (1.0 - lam_init)
    mixed = jnp.concatenate([multiscale_pool(p, pool_w, pool_scale), fourier_mix(f, fnet_w), o.reshape(B, L, ATTN_WIDTH)], axis=-1)
    return mixed @ w_out, k.reshape(B, L, ATTN_HEADS, 2 * QK_DIM), v


def trunk_layer(x, mod, ctx_k, ctx_v, lam, lam_init, norm_g, ffn1_wi, ffn1_wo, ffn2_wi, ffn2_wo, w_in, w_out, q_norm_g, k_norm_g, attn_out_g, pool_w, pool_scale, fnet_w):
    h = mod_norm(x, norm_g[0], mod, 0)
    x = x + 0.5 * mod[..., 2, :] * swiglu(h, ffn1_wi, ffn1_wo)
    h = mod_norm(x, norm_g[1], mod, 1)
    y, k, v = token_mix(h, ctx_k, ctx_v, lam, lam_init, w_in, w_out, q_norm_g, k_norm_g, attn_out_g, pool_w, pool_scale, fnet_w)
    x = x + mod[..., 5, :] * y
    h = mod_norm(x, norm_g[2], mod, 2)
    x = x + 0.5 * mod[..., 8, :] * swiglu(h, ffn2_wi, ffn2_wo)
    return x, k, v


def setup_inputs(seed: int = 0) -> dict:
    key = jax.random.key(seed)
    ks = jax.random.split(key, 25)

    def nrm(i, shape, s):
        return jax.random.normal(ks[i], shape, jnp.float32) * s

    return {
        'x_prompt': nrm(0, (BATCH, SEQ, D_MODEL), 1.0),
        'x_sample': nrm(1, (DEC_BATCH, DEC_SEQ, D_MODEL), 1.0),
        'cache_k': nrm(2, (DEC_BATCH, DEPTH, PAST_LEN, ATTN_HEADS, 2 * QK_DIM), 1.0),
        'cache_v': nrm(3, (DEC_BATCH, DEPTH, PAST_LEN, ATTN_HEADS, V_DIM), 1.0),
        'c': nrm(4, (DEC_BATCH, D_MODEL), 1.0),
        'c_ctx': nrm(5, (D_MODEL,), 1.0),
        'norm_g': 1.0 + nrm(6, (DEPTH, 3, D_MODEL), 0.1),
        'ada_w': nrm(7, (DEPTH, D_MODEL, N_MOD * D_MODEL), 0.5 * D_MODEL ** -0.5),
        'ada_b': nrm(8, (DEPTH, N_MOD * D_MODEL), 0.02),
        'ffn1_wi': nrm(9, (DEPTH, D_MODEL, 2 * D_FF), D_MODEL ** -0.5),
        'ffn1_wo': nrm(10, (DEPTH, D_FF, D_MODEL), D_FF ** -0.5),
        'ffn2_wi': nrm(11, (DEPTH, D_MODEL, 2 * D_FF), D_MODEL ** -0.5),
        'ffn2_wo': nrm(12, (DEPTH, D_FF, D_MODEL), D_FF ** -0.5),
        'w_in': nrm(13, (DEPTH, D_MODEL, IN_WIDTH), D_MODEL ** -0.5),
        'w_out': nrm(14, (DEPTH, MIX_WIDTH, D_MODEL), MIX_WIDTH ** -0.5),
        'q_norm_g': 1.0 + nrm(15, (DEPTH, QK_DIM), 0.1),
        'k_norm_g': 1.0 + nrm(16, (DEPTH, QK_DIM), 0.1),
        'lam_q1': nrm(17, (DEPTH, QK_DIM), 0.1),
        'lam_k1': nrm(18, (DEPTH, QK_DIM), 0.1),
        'lam_q2': nrm(19, (DEPTH, QK_DIM), 0.1),
        'lam_k2': nrm(20, (DEPTH, QK_DIM), 0.1),
        'attn_out_g': 1.0 + nrm(21, (DEPTH, V_DIM), 0.1),
        'pool_w': nrm(22, (DEPTH, POOL_GROUPS, POOL_CH, POOL_CH), POOL_CH ** -0.5),
        'pool_scale': 1.0 + nrm(23, (DEPTH, POOL_WIDTH), 0.1),
        'fnet_w': nrm(24, (DEPTH, FNET_HEADS, FNET_CH, FNET_CH), FNET_CH ** -0.5),
    }


def reference(x_prompt, x_sample, cache_k, cache_v, c, c_ctx, norm_g, ada_w, ada_b, ffn1_wi, ffn1_wo, ffn2_wi, ffn2_wo, w_in, w_out, q_norm_g, k_norm_g, lam_q1, lam_k1, lam_q2, lam_k2, attn_out_g, pool_w, pool_scale, fnet_w):
    xp = x_prompt
    xs = x_sample
    ks_new = []
    vs_new = []
    for l in range(DEPTH):
        lam_init = 0.8 - 0.6 * math.exp(-0.3 * l)
        lam = (jnp.exp(jnp.sum(lam_q1[l].astype(jnp.float32) * lam_k1[l].astype(jnp.float32)))
               - jnp.exp(jnp.sum(lam_q2[l].astype(jnp.float32) * lam_k2[l].astype(jnp.float32))) + lam_init)
        lw = (norm_g[l], ffn1_wi[l], ffn1_wo[l], ffn2_wi[l], ffn2_wo[l], w_in[l], w_out[l],
              q_norm_g[l], k_norm_g[l], attn_out_g[l], pool_w[l], pool_scale[l], fnet_w[l])
        mod_ctx = adaln_params(c_ctx, ada_w[l], ada_b[l])
        mod_lat = adaln_params(c, ada_w[l], ada_b[l])[:, None]
        xp, k_l, v_l = trunk_layer(xp, mod_ctx, None, None, lam, lam_init, *lw)
        ks_new.append(k_l)
        vs_new.append(v_l)
        xs, _, _ = trunk_layer(xs, mod_lat, cache_k[:, l], cache_v[:, l], lam, lam_init, *lw)
    new_cache_k = jnp.stack(ks_new, axis=1)
    new_cache_v = jnp.stack(vs_new, axis=1)
    return (xp, xs, new_cache_k, new_cache_v)
```

```python
import math
import numpy as np
from contextlib import ExitStack
import concourse.bass as bass
import concourse.mybir as mybir
from concourse.bass_utils import run_bass_kernel_spmd

F32 = mybir.dt.float32
BF16 = mybir.dt.bfloat16
AF = mybir.ActivationFunctionType
ALU = mybir.AluOpType
AX = mybir.AxisListType

D = 1024
DEPTH = 2
T = 1280
TILES = [(0, 512, 0), (512, 512, 0), (1024, 256, 1)]
DFF = 2816
NF = 22
EPS = 1e-6
SCALE = 64 ** -0.5
NBUF = 5
ENGS = ('sp', 'act', 'pool', 'dve', 'pe')
PAYC = 3584


class Prog:
    def __init__(self, nc, stack, dry=False):
        self.nc = nc
        self.stack = stack
        self.dry = dry
        self.thunks = {e: [] for e in ENGS}
        self.semh = {}
        self.val = {}
        self.seen = {e: {} for e in ENGS}

    def sem(self, name):
        if name not in self.val:
            if not self.dry:
                self.semh[name] = self.stack.enter_context(self.nc.semaphore(name))
            self.val[name] = 0
        return name

    def _waits(self, eng, toks):
        ws = []
        for t in toks:
            if t is None:
                continue
            name, v = t
            if self.seen[eng].get(name, 0) < v:
                self.seen[eng][name] = v
                ws.append((name, v))
        return ws

    def op(self, eng, fn, deps=(), sig=True):
        if eng in ('act', 'dve') and self.val.get('p_' + eng, 0) > 0:
            deps = list(deps) + [('p_' + eng, self.val['p_' + eng])]
        ws = self._waits(eng, deps)
        tok = None
        if sig:
            name = self.sem('p_' + eng)
            self.val[name] += 1
            tok = (name, self.val[name])
        self.thunks[eng].append((ws, fn, tok, 1))
        return tok

    def dma(self, eng, out, in_, chan, deps=()):
        ws = self._waits(eng, deps)
        name = self.sem(chan)
        self.val[name] += 16
        tok = (name, self.val[name])
        self.thunks[eng].append((ws, lambda e: e.dma_start(out=out, in_=in_), tok, 16))
        return tok

    def cc(self, fn, chan, deps=()):
        ws = self._waits('pool', deps)
        name = self.sem(chan)
        self.val[name] += 1
        tok = (name, self.val[name])
        self.thunks['pool'].append((ws, fn, tok, 'cc'))
        return tok

    def wait_only(self, eng, deps):
        ws = self._waits(eng, deps)
        if ws:
            self.thunks[eng].append((ws, None, None, 0))

    def barrier(self, skip=('out',), skip_w=False):
        if skip_w:
            skip = tuple(skip) + tuple('w%d' % i for i in range(NBUF))
        toks = [(n, v) for n, v in self.val.items() if v > 0 and n not in skip]
        for e in ENGS:
            self.wait_only(e, toks)

    def emit(self):
        with self.nc.Block() as block:
            def run(engname):
                def f(e):
                    for ws, fn, tok, inc in self.thunks[engname]:
                        for (n, v) in ws:
                            e.wait_ge(self.semh[n], v)
                        if fn is None:
                            continue
                        ins = fn(e)
                        if tok is not None:
                            if inc == 'cc':
                                ins.then_inc(self.semh[tok[0]])
                            else:
                                ins.then_inc(self.semh[tok[0]], inc)
                return f
            block.sync(run('sp'))
            block.scalar(run('act'))
            block.gpsimd(run('pool'))
            block.vector(run('dve'))
            block.tensor(run('pe'))


class Rot:
    def __init__(self, aps):
        self.aps = aps
        self.i = 0
        self.free = [[] for _ in aps]
        self.busy = {}

    def get(self, skip_busy=False):
        idx = self.i % len(self.aps)
        if skip_busy:
            for _ in range(len(self.aps)):
                if not self.busy.get(idx, False):
                    break
                self.i += 1
                idx = self.i % len(self.aps)
        self.i += 1
        assert not self.busy.get(idx, False), ('rotating buffer reused before release', idx)
        self.busy[idx] = True
        toks = self.free[idx]
        self.free[idx] = []
        return idx, self.aps[idx], toks

    def rel(self, idx, tok):
        self.free[idx].append(tok)
        self.busy[idx] = False


class WRing:
    def __init__(self, P, wbuf, plan=None):
        self.P = P
        self.wbuf = wbuf
        self.collect = plan is None
        self.plan = [] if plan is None else plan
        self.free = [[] for _ in range(NBUF)]
        self.rec = 0
        self.cur = 0
        self.tok = {}
        self.donef = {}

    def _record(self, j):
        s = j % NBUF
        key, dmas = self.plan[j]
        deps = self.free[s]
        self.free[s] = []
        tok = None
        for (dstf, src) in dmas:
            tok = self.P.dma('pool', dstf(self.wbuf[:, s, :]), src, 'w%d' % s, deps=deps)
        self.tok[j] = tok

    def _advance(self):
        while self.rec < len(self.plan) and (self.rec < NBUF or self.donef.get(self.rec - NBUF)):
            self._record(self.rec)
            self.rec += 1

    def use(self, key, dmas):
        i = self.cur
        self.cur += 1
        if self.collect:
            self.plan.append((key, dmas))
            return i, self.wbuf[:, i % NBUF, :], None
        assert self.plan[i][0] == key, (self.plan[i][0], key)
        self._advance()
        assert i < self.rec, (i, self.rec, key)
        return i, self.wbuf[:, i % NBUF, :], self.tok[i]

    def prestart(self):
        if not self.collect:
            self._advance()

    def done(self, i, tok):
        self.free[i % NBUF].append(tok)
        self.donef[i] = True
        if not self.collect:
            self._advance()


def build_nc():
    nc = bass.Bass("TRN2", target_bir_lowering=False)

    def din(name, shape):
        return nc.dram_tensor(name, list(shape), F32, kind="ExternalInput").ap()

    def dout(name, shape):
        return nc.dram_tensor(name, list(shape), F32, kind="ExternalOutput").ap()

    d = {}
    d['xp'] = din('xp', [1024, D]); d['xs'] = din('xs', [256, D])
    d['ck'] = din('ck', [DEPTH, 256, 512]); d['cv'] = din('cv', [DEPTH, 256, 512])
    d['cond'] = din('cond', [2, D])
    d['norm_g'] = din('norm_g', [DEPTH, 3, D])
    d['ada_w'] = din('ada_w', [DEPTH, D, 9 * D]); d['ada_b'] = din('ada_b', [DEPTH, 9 * D])
    d['ffn1_wi'] = din('ffn1_wi', [DEPTH, D, 2 * DFF]); d['ffn1_wo'] = din('ffn1_wo', [DEPTH, DFF, D])
    d['ffn2_wi'] = din('ffn2_wi', [DEPTH, D, 2 * DFF]); d['ffn2_wo'] = din('ffn2_wo', [DEPTH, DFF, D])
    d['w_in'] = din('w_in', [DEPTH, D, 2048]); d['w_out'] = din('w_out', [DEPTH, D, D])
    for nm in ('q_norm_g', 'k_norm_g', 'lam_q1', 'lam_k1', 'lam_q2', 'lam_k2'):
        d[nm] = din(nm, [DEPTH, 64])
    d['attn_out_g'] = din('attn_out_g', [DEPTH, 128])
    d['pool_w'] = din('pool_w', [DEPTH, 4, 64, 64]); d['fnet_w'] = din('fnet_w', [DEPTH, 4, 64, 64])
    d['pool_scale'] = din('pool_scale', [DEPTH, 256])
    d['ident'] = din('c_ident', [128, 128]); d['prot'] = din('c_prot', [128, 128])
    d['ropec'] = din('c_ropec', [128, 256]); d['ropes'] = din('c_ropes', [128, 256])
    d['cs'] = din('c_cs', [128, 256]); d['dftp'] = din('c_dftp', [256, 512]); d['poolp'] = din('c_poolp', [256, 1024])
    d['dfts'] = din('c_dfts', [1024, 512]); d['pools'] = din('c_pools', [1024, 1024])
    d['yp'] = dout('yp', [1024, D]); d['ys'] = dout('ys', [256, D])
    d['nk'] = dout('nk', [DEPTH, 1024, 512]); d['nv'] = dout('nv', [DEPTH, 1024, 512])
    ag_in = [nc.dram_tensor('ag_in%d' % l, [128, PAYC], BF16).ap() for l in range(DEPTH)]
    ag_out = [nc.dram_tensor('ag_out%d' % l, [512, PAYC], BF16).ap() for l in range(DEPTH)]

    with ExitStack() as st:
        def sb(name, shape, dt):
            return st.enter_context(nc.sbuf_tensor(name, shape, dt))

        xT = sb('xT', [128, 8, T], F32)
        xn = sb('xn', [128, 8, T], BF16)
        ARENA = 29776
        arena = sb('arena', [128, ARENA], BF16)
        wbuf = sb('wbuf', [128, NBUF, 4096], BF16)
        ident = sb('ident', [128, 128], F32)
        ones_f = sb('ones_f', [128, 128], F32)
        ones_b = sb('ones_b', [128, 128], BF16)
        bones_b = sb('bones_b', [128, 128], BF16)
        zeros_b = sb('zeros_b', [128, 128], BF16)
        prot = sb('prot', [128, 128], F32)
        ropec = sb('ropec', [128, 256], F32)
        ropes = sb('ropes', [128, 256], F32)
        csmat = sb('csmat', [128, 256], BF16)
        dftp = sb('dftp', [128, 2, 512], BF16)
        poolp = sb('poolp', [128, 2, 1024], BF16)
        pwbd = sb('pwbd', [128, DEPTH, 2, 128], BF16)
        fwbd = sb('fwbd', [128, DEPTH, 2, 128], BF16)
        vecrows = sb('vecrows', [128, DEPTH, 128], F32)
        vcols = sb('vcols', [128, DEPTH, 128], F32)
        condrows = sb('condrows', [16, 128], F32)
        condsil = sb('condsil', [16, 128], F32)
        scT = sb('scT', [128, 8, 2], BF16)
        modT = sb('modT', [128, DEPTH, 72, 2], F32)
        Atab = sb('Atab', [128, DEPTH, 3, 2, 8], F32)
        Gtab = sb('Gtab', [128, DEPTH, 3, 2, 8], F32)
        sctab = sb('sctab', [128, DEPTH, 8], F32)
        epsT = sb('epsT', [128, 1], F32)
        rstd_t = sb('rstd_t', [128, 2, 512], F32)
        sqc_t = sb('sqc_t', [128, 2, 512], BF16)
        sa_t = sb('sa_t', [128, 2, 512], F32)
        stg_t = sb('stg_t', [128, 2, 1024], F32)
        sm_t = sb('sm_t', [128, 2, 256], BF16)
        ot = sb('ot', [128, 6, 2, 128], F32)
        os_t = sb('os_t', [128, 6, 8], F32)
        rp_t = sb('rp_t', [128, 2, 2, 256], F32)
        lam_t = sb('lam_t', [128, 8], F32)

        psb = [st.enter_context(nc.psum_tensor('ps%d' % i, [128, 512], F32)) for i in range(8)]
        modps = psb[7]

        hv = arena[:, 0:NF * T].rearrange("p (f t) -> p f t", f=NF)
        SCR0 = ARENA - 12288
        tscr = arena[:, SCR0:SCR0 + 8192].bitcast(F32).rearrange("p (k t) -> p k t", k=8)
        sq = arena[:, SCR0 + 8192:SCR0 + 12288].rearrange("p (k t) -> p k t", k=8)
        qnT = arena[:, 0:5120].rearrange("p (h t) -> p h t", h=4)
        R0 = 5120
        knT = arena[:, R0:R0 + 4096].rearrange("p (h t) -> p h t", h=4)
        v_aug = arena[:, R0 + 4096:R0 + 8256].rearrange("p (i h e) -> p i h e", i=8, h=4)
        peo_p = arena[:, R0 + 8256:R0 + 12352].rearrange("p (i e j c) -> p i e j c", i=8, e=2, j=2)
        AB_p = arena[:, R0 + 12352:R0 + 16448].rearrange("p (i j c) -> p i j c", i=8, j=2)
        khat_r = arena[:, R0 + 16448:R0 + 18496].bitcast(F32).rearrange("p (r t) -> p r t", r=2)
        kT_all = arena[:, R0:R0 + 5120].rearrange("p (h t) -> p h t", h=4)
        v_all = arena[:, R0 + 5120:R0 + 10320].rearrange("p (i h e) -> p i h e", i=10, h=4)
        peo_s = arena[:, R0 + 10320:R0 + 14416].rearrange("p (i e j c) -> p i e j c", i=8, e=2, j=2)
        AB_s = arena[:, R0 + 14416:R0 + 18512].rearrange("p (i j c) -> p i j c", i=8, j=2)
        F0 = R0 + 18512
        fT = arena[:, F0:F0 + 2560].rearrange("p (j t) -> p j t", j=2)
        pay = arena[:, F0 + 2560:F0 + 2560 + PAYC]
        E_p = arena[:, F0:F0 + 4096].rearrange("p (r c x) -> p r c x", r=4, c=2)
        E_s = arena[:, F0:F0 + 5120].rearrange("p (r x) -> p r x", r=4)

        def emit_all(P, W):
            PS = Rot([psb[i] for i in range(8)])
            _psget = PS.get
            PS.get = lambda: _psget(skip_busy=True)
            ns = {'bank': [None, None, None], 'tok': [None, None, None], 'ready': False}
            SQC = Rot([sqc_t[:, i, :] for i in range(2)])
            STG = Rot([stg_t[:, i, :] for i in range(2)])
            SA = Rot([sa_t[:, i, :] for i in range(2)])
            RS = Rot([rstd_t[:, i, :] for i in range(2)])
            SM = Rot([sm_t[:, i, :] for i in range(2)])
            OT = Rot([(ot[:, i], os_t[:, i, :]) for i in range(6)])
            RP = Rot([rp_t[:, i] for i in range(2)])
            KH = Rot([khat_r[:, i, :] for i in range(2)])
            EP = Rot([E_p[:, i] for i in range(4)])
            ES = Rot([E_s[:, i, :] for i in range(4)])

            def mm(out, lhsT, rhs, start, stop, deps=(), sig=False):
                return P.op('pe', lambda e: e.matmul(out, lhsT=lhsT, rhs=rhs, start=start, stop=stop), deps=deps, sig=sig)

            def tr(out, in_, idn, deps=(), sig=True):
                return P.op('pe', lambda e: e.transpose(out=out, in_=in_, identity=idn), deps=deps, sig=sig)

            def act(out, in_, func, deps=(), bias=None, scale=None, accum=None):
                kw = {}
                if bias is not None:
                    kw['bias'] = bias
                if scale is not None:
                    kw['scale'] = scale
                if accum is not None:
                    kw['accum_out'] = accum
                return P.op('act', lambda e: e.activation(out=out, in_=in_, func=func, **kw), deps=deps)

            def amul(out, in_, m, deps=()):
                return P.op('act', lambda e: e.mul(out=out, in_=in_, mul=m), deps=deps)

            def acopy(out, in_, deps=()):
                return P.op('act', lambda e: e.copy(out=out, in_=in_), deps=deps)

            def vcopy(out, in_, deps=()):
                return P.op('dve', lambda e: e.tensor_copy(out=out, in_=in_), deps=deps)

            def tt(out, in0, in1, op, deps=()):
                return P.op('dve', lambda e: e.tensor_tensor(out=out, in0=in0, in1=in1, op=op), deps=deps)

            def tsc(out, in0, s1, op0, s2=None, op1=None, deps=()):
                if op1 is None:
                    return P.op('dve', lambda e: e.tensor_scalar(out=out, in0=in0, scalar1=s1, scalar2=None, op0=op0), deps=deps)
                return P.op('dve', lambda e: e.tensor_scalar(out=out, in0=in0, scalar1=s1, scalar2=s2, op0=op0, op1=op1), deps=deps)

            def stt(out, in0, scalar, in1, op0, op1, deps=()):
                return P.op('dve', lambda e: e.scalar_tensor_tensor(out=out, in0=in0, scalar=scalar, in1=in1, op0=op0, op1=op1), deps=deps)

            def recip(out, in_, deps=()):
                return P.op('dve', lambda e: e.reciprocal(out=out, in_=in_), deps=deps)

            def rsum(out, in_, deps=()):
                return P.op('dve', lambda e: e.reduce_sum(out=out, in_=in_, axis=AX.X), deps=deps)

            def memset(ap, v, deps=()):
                return P.op('dve', lambda e: e.memset(ap, v), deps=deps)

            cp_flip = [0]

            def evac(out, in_, deps=()):
                cp_flip[0] ^= 1
                return (acopy if cp_flip[0] else vcopy)(out, in_, deps)

            t_id = P.dma('sp', ident[:], d['ident'], 'cst0')
            xst = [arena[:, i * 2048:(i + 1) * 2048].bitcast(F32) for i in range(10)]
            t_lds = []
            for tti in range(10):
                src = d['xp'][tti * 128:(tti + 1) * 128, :] if tti < 8 else d['xs'][(tti - 8) * 128:(tti - 7) * 128, :]
                t_lds.append(P.dma('sp' if tti % 2 == 0 else 'act', xst[tti], src, 'xl%d' % tti))
            for tti in range(10):
                stg = xst[tti]
                for half in range(2):
                    pi, ps, pfr = PS.get()
                    for q in range(4):
                        kc = half * 4 + q
                        t = tr(ps[:, q * 128:(q + 1) * 128], stg[:, kc * 128:(kc + 1) * 128], ident[:],
                               deps=[t_lds[tti], t_id] + pfr, sig=(q == 3))
                    tcp = evac(xT[:, half * 4:half * 4 + 4, tti * 128:(tti + 1) * 128],
                               ps[:].rearrange("p (q t) -> p q t", q=4), deps=[t])
                    PS.rel(pi, tcp)

            P.dma('sp', prot[:], d['prot'], 'cst')
            P.dma('sp', ropec[:], d['ropec'], 'cst')
            P.dma('sp', ropes[:], d['ropes'], 'cst')
            P.dma('sp', condrows[:], d['cond'].rearrange("j (k c) -> (j k) c", c=128), 'cst')
            t_m = [memset(vecrows[:], 0.0), memset(pwbd[:], 0.0), memset(fwbd[:], 0.0)]
            memset(ones_f[:], 1.0); memset(ones_b[:], 1.0); memset(bones_b[:], 0.0); memset(zeros_b[:], 0.0)
            memset(bones_b[0:64, 0:64], 1.0); memset(bones_b[64:128, 64:128], 1.0)
            memset(epsT[:], EPS)
            P.dma('pool', csmat[:], d['cs'], 'cst2')
            P.dma('pool', dftp[:], d['dftp'].rearrange("(i p) c -> p i c", p=128), 'cst2')
            P.dma('pool', poolp[:], d['poolp'].rearrange("(i p) c -> p i c", p=128), 'cst2')
            P.wait_only('pool', t_m)
            P.wait_only('sp', t_m)
            for l in range(DEPTH):
                for g in range(4):
                    r0 = (g % 2) * 64
                    P.dma('pool', pwbd[r0:r0 + 64, l, g // 2, r0:r0 + 64], d['pool_w'][l, g], 'cst2')
                    P.dma('pool', fwbd[r0:r0 + 64, l, g // 2, r0:r0 + 64], d['fnet_w'][l, g], 'cst2')
            W.prestart()
            for l in range(DEPTH):
                P.dma('sp', vecrows[0:72, l, :], d['ada_b'][l].rearrange("(r c) -> r c", c=128), 'cst')
                P.dma('sp', vecrows[72:96, l, :], d['norm_g'][l].rearrange("s (k c) -> (s k) c", c=128), 'cst')
                P.dma('sp', vecrows[96:98, l, :], d['pool_scale'][l].rearrange("(r c) -> r c", c=128), 'cst')
                P.dma('sp', vecrows[98:99, l, 0:64], d['q_norm_g'][l:l + 1, :], 'cst')
                P.dma('sp', vecrows[98:99, l, 64:128], d['q_norm_g'][l:l + 1, :], 'cst')
                P.dma('sp', vecrows[99:100, l, 0:64], d['k_norm_g'][l:l + 1, :], 'cst')
                P.dma('sp', vecrows[99:100, l, 64:128], d['k_norm_g'][l:l + 1, :], 'cst')
                P.dma('sp', vecrows[100:101, l, :], d['attn_out_g'][l:l + 1, :], 'cst')
                for i, nm in enumerate(('lam_q1', 'lam_k1', 'lam_q2', 'lam_k2')):
                    P.dma('sp', vecrows[101 + i:102 + i, l, 0:64], d[nm][l:l + 1, :], 'cst')
            P.barrier(skip_w=True)
            t = act(condsil[:], condrows[:], AF.Silu)
            pi, ps, fr = PS.get()
            t = tr(ps[:, 0:16], condsil[:], ident[0:16, 0:16], deps=[t] + fr)
            t = vcopy(scT[:].rearrange("p k j -> p j k"), ps[:, 0:16].rearrange("p (j k) -> p j k", j=2), deps=[t])
            PS.rel(pi, t)
            for l in range(DEPTH):
                lam_init = 0.8 - 0.6 * math.exp(-0.3 * l)
                pi, ps, fr = PS.get()
                t = tr(ps[:, 0:128], vecrows[:, l, :], ident[:], deps=fr)
                t = vcopy(vcols[:, l, :], ps[:, 0:128], deps=[t])
                t_vc = t
                PS.rel(pi, t)
                t1 = tt(lam_t[:, 0:1], vcols[:, l, 101:102], vcols[:, l, 102:103], ALU.mult, deps=[t])
                t2 = tt(lam_t[:, 1:2], vcols[:, l, 103:104], vcols[:, l, 104:105], ALU.mult, deps=[t])
                pi, ps, fr = PS.get()
                t = mm(ps[:, 0:2], ones_f[:], lam_t[:, 0:2], True, True, deps=[t1, t2] + fr, sig=True)
                t = act(lam_t[:, 2:4], ps[:, 0:2], AF.Exp, deps=[t])
                PS.rel(pi, t)
                t = tsc(lam_t[:, 4:5], lam_t[:, 3:4], -lam_init, ALU.add, deps=[t])
                t = tt(sctab[:, l, 3:4], lam_t[:, 4:5], lam_t[:, 2:3], ALU.subtract, deps=[t])
                tsc(sctab[:, l, 2:3], vcols[:, l, 100:101], (1.0 - lam_init), ALU.mult, deps=[t_vc])
            P.barrier(skip_w=True)

            ada_state = {'next': [0, 0], 'rd': [None] * 18, 'tab': {}}

            def ada_block(l, b):
                i, ws, tw = W.use(('ada', l, b), [(lambda s: s.rearrange("p (k c) -> p k c", k=8),
                                                   d['ada_w'][l, :, b * 512:(b + 1) * 512].rearrange("(k p) c -> p k c", p=128))])
                wv = ws.rearrange("p (k c) -> p k c", k=8)
                t = None
                pi, ps, fr = PS.get()
                for q in range(4):
                    for k in range(8):
                        t = mm(ps[:, 2 * q:2 * q + 2], wv[:, k, q * 128:(q + 1) * 128], scT[:, k, :], k == 0, k == 7,
                               deps=[tw] + fr, sig=(q == 3 and k == 7))
                W.done(i, t)
                te = tt(modT[:, l, 4 * b:4 * b + 4, :], ps[:, 0:8].rearrange("p (c j) -> p c j", j=2),
                        vcols[:, l, 4 * b:4 * b + 4].unsqueeze(2).to_broadcast([128, 4, 2]), ALU.add, deps=[t])
                PS.rel(pi, te)
                ada_state['rd'][b] = te
                if b % 6 == 3:
                    s = b // 6
                    for cond in range(2):
                        te = stt(Atab[:, l, s, cond, :], modT[:, l, (3 * s + 1) * 8:(3 * s + 2) * 8, cond], 1.0,
                                 vcols[:, l, 72 + s * 8:72 + (s + 1) * 8], ALU.add, ALU.mult, deps=[te])
                    ada_state['tab'][(l, s)] = te
                if b % 6 == 5:
                    s = b // 6
                    for cond in range(2):
                        te = tsc(Gtab[:, l, s, cond, :], modT[:, l, (3 * s + 2) * 8:(3 * s + 3) * 8, cond],
                                 (1.0 if s == 1 else 0.5), ALU.mult, deps=[te])

            def pump_ada(l, upto):
                while ada_state['next'][l] < upto:
                    ada_block(l, ada_state['next'][l])
                    ada_state['next'][l] += 1

            def ns_accum(dc, ti, c0, n, t_u):
                si, sqb, sfr = SQC.get()
                t_q = act(sqb[:, :n], xT[:, dc, c0:c0 + n], AF.Square, deps=[t_u] + sfr)

                def later():
                    fr = []
                    if dc == 0:
                        bi, bap, fr = PS.get()
                        ns['bank'][ti] = (bi, bap)
                    t_m = mm(ns['bank'][ti][1][:, :n], ones_b[:], sqb[:, :n], dc == 0, dc == 7, deps=[t_q] + fr, sig=True)
                    SQC.rel(si, t_m)
                    if dc == 7:
                        ns['tok'][ti] = t_m
                return later

            def norm(l, s, t_in):
                if not ns['ready']:
                    t_ss = None
                    for ti, (c0, n, cond) in enumerate(TILES):
                        t_sq = act(sq[:, :, :n], xT[:, :, c0:c0 + n], AF.Square, deps=list(t_in) + [t_ss])
                        bi, bap, bfr = PS.get()
                        ns['bank'][ti] = (bi, bap)
                        for k in range(8):
                            t_ss = mm(bap[:, :n], ones_b[:], sq[:, k, :n], k == 0, k == 7, deps=[t_sq] + bfr, sig=(k == 7))
                        ns['tok'][ti] = t_ss
                pump_ada(l, 6 * s + 4)
                t_tab = ada_state['tab'][(l, s)]
                toks = []
                for ti, (c0, n, cond) in enumerate(TILES):
                    t_ss = ns['tok'][ti]
                    bi, bap = ns['bank'][ti]
                    ri, rs, rfr = RS.get()
                    t_sd = act(rs[:, :n], bap[:, :n], AF.Ln, deps=[t_ss] + rfr, bias=epsT[:, 0:1], scale=1.0 / D)
                    PS.rel(bi, t_sd)
                    t_r = act(rs[:, :n], rs[:, :n], AF.Exp, deps=[t_sd], scale=-0.5)
                    t_t = tt(tscr[:, :, :n], xT[:, :, c0:c0 + n], rs[:, :n].unsqueeze(1).to_broadcast([128, 8, n]), ALU.mult,
                             deps=[t_r] + list(t_in))
                    RS.rel(ri, t_t)
                    for k in range(5):
                        t_x = act(xn[:, k, c0:c0 + n], tscr[:, k, :n], AF.Identity, deps=[t_t, t_tab],
                                  scale=Atab[:, l, s, cond, k:k + 1], bias=modT[:, l, 3 * s * 8 + k, cond:cond + 1])
                    for k in range(5, 8):
                        t_y = tsc(xn[:, k, c0:c0 + n], tscr[:, k, :n], Atab[:, l, s, cond, k:k + 1], ALU.mult,
                                  s2=modT[:, l, 3 * s * 8 + k, cond:cond + 1], op1=ALU.add, deps=[t_t, t_tab])
                    toks.append(t_x)
                    toks.append(t_y)
                ns['ready'] = False
                return toks

            def ffn(l, s, t_in, ada_l=None, hoist=True):
                wi = d['ffn1_wi'] if s == 0 else d['ffn2_wi']
                wo = d['ffn1_wo'] if s == 0 else d['ffn2_wo']
                xtok = norm(l, s, t_in)
                t_h = None
                t_u = None
                for gi in range(11):
                    wsrc = wi[l].rearrange("(k p) (a f) -> p k a f", p=128, a=2)
                    dmas = []
                    for a in range(2):
                        dmas.append(((lambda s_, a=a: s_.rearrange("p (k a c) -> p k a c", k=8, a=2)[:, :, a, :]),
                                     wsrc[:, :, a, gi * 256:(gi + 1) * 256]))
                    i, ws, tw = W.use(('wi', l, s, gi), dmas)
                    wv = ws.rearrange("p (k a c) -> p k a c", k=8, a=2)
                    last = None
                    for q in range(2):
                        f = gi * 2 + q
                        for ti, (c0, n, cond) in enumerate(TILES):
                            ia, pa, fa = PS.get()
                            for k in range(8):
                                ta = mm(pa[:, :n], wv[:, k, 0, q * 128:(q + 1) * 128], xn[:, k, c0:c0 + n], k == 0, k == 7,
                                        deps=[tw, xtok[2 * ti], xtok[2 * ti + 1]] + fa, sig=(k == 7))
                            ib, pb, fb = PS.get()
                            for k in range(8):
                                tb = mm(pb[:, :n], wv[:, k, 1, q * 128:(q + 1) * 128], xn[:, k, c0:c0 + n], k == 0, k == 7,
                                        deps=[tw] + fb, sig=(k == 7))
                            si, sa, sfr = SA.get()
                            t_s = act(sa[:, :n], pa[:, :n], AF.Silu, deps=[ta] + sfr)
                            PS.rel(ia, t_s)
                            t_h = tt(hv[:, f, c0:c0 + n], sa[:, :n], pb[:, :n], ALU.mult, deps=[t_s, tb])
                            PS.rel(ib, t_h)
                            SA.rel(si, t_h)
                            last = tb
                    W.done(i, last)
                    if ada_l is not None:
                        ll, lo, hi = ada_l
                        pump_ada(ll, min(hi, ada_state['next'][ll] + 2))
                pend = [None]
                for dc in range(8):
                    wsrc = wo[l, :, dc * 128:(dc + 1) * 128].rearrange("(f p) c -> p f c", p=128)
                    dmas = []
                    for (f0, f1) in ((0, 8), (8, 15), (15, 22)):
                        dmas.append(((lambda s_, f0=f0, f1=f1: s_[:, 0:NF * 128].rearrange("p (f c) -> p f c", f=NF)[:, f0:f1, :]),
                                     wsrc[:, f0:f1, :]))
                    i, ws, tw = W.use(('wo', l, s, dc), dmas)
                    wv = ws[:, 0:NF * 128].rearrange("p (f c) -> p f c", f=NF)
                    t = None
                    for ti, (c0, n, cond) in enumerate(TILES):
                        ip, ps, fr = PS.get()
                        for f in range(NF):
                            t = mm(ps[:, :n], wv[:, f, :], hv[:, f, c0:c0 + n], f == 0, f == NF - 1, deps=[tw, t_h] + fr, sig=(f == NF - 1))
                        if pend[0] is not None:
                            pend[0]()
                            pend[0] = None
                        t_u = stt(xT[:, dc, c0:c0 + n], ps[:, :n], Gtab[:, l, s, cond, dc:dc + 1], xT[:, dc, c0:c0 + n],
                                  ALU.mult, ALU.add, deps=[t])
                        PS.rel(ip, t_u)
                        if hoist:
                            pend[0] = ns_accum(dc, ti, c0, n, t_u)
                    W.done(i, t)
                if pend[0] is not None:
                    pend[0]()
                if hoist:
                    ns['ready'] = True
                return [t_u]

            def qk_chain_a(ps, n, deps):
                si, sqb, sfr = SA.get()
                sqv = sqb.bitcast(BF16)[:, 0:n]
                t_q = act(sqv, ps[:, :n], AF.Square, deps=deps + sfr)
                pi2, ps2, fr2 = PS.get()
                t_m = mm(ps2[:, :n], bones_b[:], sqv, True, True, deps=[t_q] + fr2, sig=True)
                SA.rel(si, t_m)
                return pi2, ps2, t_m

            def qk_chain_b(ps, n, gcol, out_ap, pi2, ps2, t_m, deps):
                ri, rs, rfr = RS.get()
                t_sd = act(rs[:, :n], ps2[:, :n], AF.Ln, deps=[t_m] + rfr, bias=epsT[:, 0:1], scale=1.0 / 64)
                PS.rel(pi2, t_sd)
                t_r = act(rs[:, :n], rs[:, :n], AF.Exp, deps=[t_sd], scale=-0.5)
                t_o = stt(out_ap, ps[:, :n], gcol, rs[:, :n], ALU.mult, ALU.mult, deps=[t_r] + deps)
                RS.rel(ri, t_o)
                return t_o

            def rope(src, out_ap, deps):
                pi, ps, fr = PS.get()
                t_p = mm(ps[:, 0:256], prot[:], src, True, True, deps=deps + fr, sig=True)
                ri, rp, rfr = RP.get()
                t1 = tt(rp[:, 0, :], src, ropec[:], ALU.mult, deps=deps + rfr)
                t2 = tt(rp[:, 1, :], ps[:, 0:256], ropes[:], ALU.mult, deps=[t_p])
                PS.rel(pi, t2)
                t3 = tt(out_ap, rp[:, 0, :], rp[:, 1, :], ALU.add, deps=[t1, t2])
                RP.rel(ri, t3)
                return t3

            def pipeline(units):
                n = len(units)
                ns = max(len(u) for u in units)
                for step in range(n + ns - 1):
                    for sg in range(ns):
                        ui = step - sg
                        if 0 <= ui < n and sg < len(units[ui]):
                            units[ui][sg]()

            def attn_scores(kT, kbase, nkc, qbase, h, Eslots):
                E = []
                for c in range(2):
                    lo = c * 64
                    ei, ev, efr = Eslots[c]
                    tE = None
                    for g0 in range(0, nkc, 2):
                        pi, ps, fr = PS.get()
                        for j in range(2):
                            kc = g0 + j
                            t = mm(ps[:, j * 256:(j + 1) * 256], kT[lo:lo + 64, h, kbase + kc * 128:kbase + (kc + 1) * 128],
                                   qnT[lo:lo + 64, h, qbase:qbase + 256], True, True, deps=fr, sig=(j == 1))
                        tE = act(ev[:, g0 * 256:(g0 + 2) * 256], ps[:, 0:512], AF.Exp, deps=[t] + efr, scale=SCALE)
                        PS.rel(pi, tE)
                    E.append((ev, tE))
                return E

            def attn_pv_a(l, nkc, vsrc, E):
                qs = []
                t_pv = None
                for qt in range(2):
                    pi, ps, fr = PS.get()
                    for c in range(2):
                        ev, tE = E[c]
                        for kc in range(nkc):
                            t_pv = mm(ps[:, c * 256:c * 256 + 129], ev[:, kc * 256 + qt * 128:kc * 256 + (qt + 1) * 128],
                                      vsrc(kc), kc == 0, kc == nkc - 1, deps=[tE] + fr, sig=(c == 1 and kc == nkc - 1))
                    oi, (ob, osb), ofr = OT.get()
                    qs.append({'pi': pi, 'ps': ps, 'oi': oi, 'ob': ob, 'osb': osb, 'ofr': ofr, 'tpv': t_pv})
                for q in qs:
                    psv = q['ps'][:].rearrange("p (c x) -> p c x", c=2)
                    q['t'] = recip(q['osb'][:, 0:2], psv[:, :, 128], deps=[q['tpv']] + q['ofr'])
                for q in qs:
                    q['t0'] = tsc(q['ob'][:, 0, :], q['ps'][:, 0:128], q['osb'][:, 0:1], ALU.mult, deps=[q['t']])
                for q in qs:
                    q['t1'] = tsc(q['ob'][:, 1, :], q['ps'][:, 256:384], q['osb'][:, 1:2], ALU.mult, deps=[q['t']])
                    PS.rel(q['pi'], q['t1'])
                for q in qs:
                    q['t'] = stt(q['ob'][:, 1, :], q['ob'][:, 1, :], sctab[:, l, 3:4], q['ob'][:, 0, :], ALU.mult, ALU.add,
                                 deps=[q['t0'], q['t1']])
                return t_pv, qs

            def attn_pv_b(qs):
                for q in qs:
                    q['t'] = act(q['ob'][:, 0, :], q['ob'][:, 1, :], AF.Square, deps=[q['t']], accum=q['osb'][:, 3:4])
                for q in qs:
                    q['t'] = act(q['osb'][:, 4:5], q['osb'][:, 3:4], AF.Ln, deps=[q['t']], bias=epsT[:, 0:1], scale=1.0 / 128)
                for q in qs:
                    q['t'] = act(q['osb'][:, 5:6], q['osb'][:, 4:5], AF.Exp, deps=[q['t']], scale=-0.5)
                for q in qs:
                    q['t'] = tsc(q['ob'][:, 0, :], q['ob'][:, 1, :], q['osb'][:, 5:6], ALU.mult, deps=[q['t']])
                return [(q['oi'], q['ob'], q['t']) for q in qs]

            def attn_finish(l, h, mix_c0, res):
                for qt, (oi, ob, t) in enumerate(res):
                    pi2, ps2, fr2 = PS.get()
                    t = tr(ps2[:, 0:128], ob[:, 0, :], ident[:], deps=[t] + fr2)
                    OT.rel(oi, t)
                    t = amul(xn[:, 4 + h, mix_c0 + qt * 128:mix_c0 + (qt + 1) * 128], ps2[:, 0:128], sctab[:, l, 2:3], deps=[t])
                    PS.rel(pi2, t)

            def mixing(l, t_in):
                xtok = norm(l, 1, t_in)
                P.wait_only('pe', xtok)
                lnk = d['nk'][l]
                lnv = d['nv'][l]
                t_z = []
                for tti in range(8):
                    t_z.append(vcopy(v_aug[:, tti, :, 128:130], ones_b[:, 0:8].rearrange("p (h e) -> p h e", h=4)))
                    t_z.append(vcopy(peo_p[:, tti, 0, :, 64:128], zeros_b[:, 0:128].rearrange("p (j c) -> p j c", j=2)))
                    t_z.append(vcopy(peo_p[:, tti, 1, :, 0:64], zeros_b[:, 0:128].rearrange("p (j c) -> p j c", j=2)))
                pt = {}
                i, ws, tw = W.use(('win', l, 0), [(lambda s_: s_.rearrange("p (k c) -> p k c", k=8),
                                                   d['w_in'][l, :, 0:512].rearrange("(k p) c -> p k c", p=128))])
                wv = ws.rearrange("p (k c) -> p k c", k=8)
                for tti in range(10):
                    pi, ps, fr = PS.get()
                    for k in range(8):
                        t = mm(ps[:, 0:256], xn[:, k, tti * 128:(tti + 1) * 128], wv[:, k, 0:256], k == 0, k == 7, deps=[tw] + fr, sig=(k == 7))
                    if tti < 8:
                        psv = ps[:, 0:256].rearrange("p (j e c) -> p j e c", j=2, e=2)
                        t1 = vcopy(peo_p[:, tti, 0, :, 0:64], psv[:, :, 0, :], deps=[t] + t_z)
                        t2 = vcopy(peo_p[:, tti, 1, :, 64:128], psv[:, :, 1, :], deps=[t] + t_z)
                        PS.rel(pi, t1); PS.rel(pi, t2)
                    else:
                        t1 = evac(pay[:, 2048 + (tti - 8) * 256:2048 + (tti - 7) * 256], ps[:, 0:256], deps=[t])
                        PS.rel(pi, t1)
                last = t
                for c2 in range(2):
                    for (c0, n, cond) in TILES:
                        pi, ps, fr = PS.get()
                        for k in range(8):
                            t = mm(ps[:, :n], wv[:, k, 256 + c2 * 128:256 + (c2 + 1) * 128], xn[:, k, c0:c0 + n], k == 0, k == 7,
                                   deps=[tw] + fr, sig=(k == 7))
                        t1 = evac(fT[:, c2, c0:c0 + n], ps[:, :n], deps=[t])
                        PS.rel(pi, t1)
                W.done(i, t)
                t_fT = t1
                cgw = {}
                for cg in (1, 2):
                    i, ws, tw = W.use(('win', l, cg), [(lambda s_: s_.rearrange("p (k c) -> p k c", k=8),
                                                        d['w_in'][l, :, cg * 512:(cg + 1) * 512].rearrange("(k p) c -> p k c", p=128))])
                    cgw[cg] = {'i': i, 'wv': ws.rearrange("p (k c) -> p k c", k=8), 'tw': tw, 'last': None}

                def qk_unit(cg, c0, n, cond, h):
                    u = {}
                    gcol = vcols[:, l, 98:99] if cg == 1 else vcols[:, l, 99:100]
                    wv = cgw[cg]['wv']
                    tw = cgw[cg]['tw']

                    def s0():
                        pi, ps, fr = PS.get()
                        for k in range(8):
                            t = mm(ps[:, :n], wv[:, k, h * 128:(h + 1) * 128], xn[:, k, c0:c0 + n], k == 0, k == 7,
                                   deps=[tw] + fr, sig=(k == 7))
                        cgw[cg]['last'] = t
                        u['pi'], u['ps'], u['t'] = pi, ps, t

                    def s1():
                        u['c'] = qk_chain_a(u['ps'], n, [u['t']])

                    def s1b():
                        pi, ps = u['pi'], u['ps']
                        pi2, ps2, t_m = u['c']
                        if cond == 0 and cg == 1:
                            t_o = qk_chain_b(ps, n, gcol, qnT[:, h, c0:c0 + n], pi2, ps2, t_m, [])
                            PS.rel(pi, t_o)
                        else:
                            ki, kh, kfr = KH.get()
                            t_o = qk_chain_b(ps, n, gcol, kh[:, :n], pi2, ps2, t_m, kfr)
                            PS.rel(pi, t_o)
                            u['ki'], u['kh'], u['t_o'] = ki, kh, t_o
                            if cond == 0:
                                u['t_c'] = vcopy(knT[:, h, c0:c0 + n], kh[:, :n], deps=[t_o])

                    def s2():
                        if cond == 0 and cg == 1:
                            return
                        ki, kh, t_o = u['ki'], u['kh'], u['t_o']
                        if cond == 0:
                            pi2, ps2, fr2 = PS.get()
                            for sub in range(4):
                                t_t = tr(ps2[:, sub * 128:(sub + 1) * 128], kh[:, sub * 128:(sub + 1) * 128], ident[:],
                                         deps=[t_o] + fr2, sig=(sub == 3))
                            KH.rel(ki, t_t); KH.rel(ki, u['t_c'])
                            si, stg, sfr = STG.get()
                            t_e = vcopy(stg[:, 0:512], ps2[:, 0:512], deps=[t_t] + sfr)
                            PS.rel(pi2, t_e)
                            t_d = P.dma('sp', lnk[c0:c0 + 512, h * 128:(h + 1) * 128].rearrange("(s p) e -> p s e", p=128),
                                        stg[:, 0:512].rearrange("p (s e) -> p s e", s=4), 'st%d' % si, deps=[t_e])
                            STG.rel(si, t_d)
                        else:
                            dst = qnT[:, h, c0:c0 + n] if cg == 1 else pay[:, h * 256:(h + 1) * 256]
                            t_r = rope(kh[:, :n], dst, [t_o])
                            KH.rel(ki, t_r)
                    return [s0, s1, s1b, s2]

                units = []
                for cg in (1, 2):
                    for (c0, n, cond) in TILES:
                        for h in range(4):
                            units.append(qk_unit(cg, c0, n, cond, h))
                pipeline(units)
                for cg in (1, 2):
                    W.done(cgw[cg]['i'], cgw[cg]['last'])
                i, ws, tw = W.use(('win', l, 3), [(lambda s_: s_.rearrange("p (k c) -> p k c", k=8),
                                                   d['w_in'][l, :, 1536:2048].rearrange("(k p) c -> p k c", p=128))])
                wv = ws.rearrange("p (k c) -> p k c", k=8)
                for tti in range(10):
                    pi, ps, fr = PS.get()
                    for k in range(8):
                        t = mm(ps[:, 0:512], xn[:, k, tti * 128:(tti + 1) * 128], wv[:, k, :], k == 0, k == 7, deps=[tw] + fr, sig=(k == 7))
                    if tti < 8:
                        si, stg, sfr = STG.get()
                        t1 = acopy(stg[:, 0:512], ps[:, 0:512], deps=[t] + sfr)
                        t2 = acopy(v_aug[:, tti, :, 0:128], ps[:, 0:512].rearrange("p (h e) -> p h e", h=4), deps=[t])
                        PS.rel(pi, t1); PS.rel(pi, t2)
                        t_d = P.dma('sp', lnv[tti * 128:(tti + 1) * 128, :], stg[:, 0:512], 'st%d' % si, deps=[t1])
                        STG.rel(si, t_d)
                        pt['out'] = pt.get('out', []) + [t_d]
                    else:
                        t1 = evac(pay[:, 1024 + (tti - 8) * 512:1024 + (tti - 7) * 512], ps[:, 0:512], deps=[t])
                        PS.rel(pi, t1)
                W.done(i, t)
                for tti in range(10):
                    for j in range(2):
                        pi, ps, fr = PS.get()
                        t = mm(ps[:, 0:256], fT[:, j, tti * 128:(tti + 1) * 128], csmat[:], True, True, deps=[t_fT] + fr, sig=True)
                        if tti < 8:
                            t1 = evac(AB_p[:, tti, j, :], ps[:, 0:256], deps=[t])
                        else:
                            o0 = 2560 + ((tti - 8) * 2 + j) * 256
                            t1 = evac(pay[:, o0:o0 + 256], ps[:, 0:256], deps=[t])
                        PS.rel(pi, t1)
                P.barrier()
                t_pd = P.dma('pool', ag_in[l], pay, 'ccin')
                t_cc = P.cc(lambda e: e.collective_compute("AllGather", ALU.bypass, replica_groups=[[0, 1, 2, 3], [4, 5, 6, 7]],
                                                           ins=[ag_in[l].opt()], outs=[ag_out[l].opt()]), 'cc', deps=[t_pd])
                P.wait_only('act', [t_pd])
                P.wait_only('dve', [t_pd])


                def mix_pool_fourier(l, ntile, peo, AB, Mv, Dv, c0):
                    for j in range(2):
                        pi, ps, fr = PS.get()
                        cnt = 0
                        for i_ in range(ntile):
                            for eo in range(2):
                                t = mm(ps[:, 0:256], peo[:, i_, eo, j, :], Mv(i_, 2 * j + eo), cnt == 0, cnt == 2 * ntile - 1,
                                       deps=fr, sig=(cnt == 2 * ntile - 1))
                                cnt += 1
                        si, sm, sfr = SM.get()
                        t1 = vcopy(sm[:, 0:256], ps[:, 0:256], deps=[t] + sfr)
                        PS.rel(pi, t1)
                        pi2, ps2, fr2 = PS.get()
                        t = mm(ps2[:, 0:256], pwbd[:, l, j, :], sm[:, 0:256], True, True, deps=[t1] + fr2, sig=True)
                        SM.rel(si, t)
                        t1 = amul(xn[:, j, c0:c0 + 256], ps2[:, 0:256], vcols[:, l, 96 + j:97 + j], deps=[t])
                        PS.rel(pi2, t1)
                    for j in range(2):
                        pi, ps, fr = PS.get()
                        cnt = 0
                        for i_ in range(ntile):
                            for cs_ in range(2):
                                t = mm(ps[:, 0:256], AB[:, i_, j, cs_ * 128:(cs_ + 1) * 128], Dv(i_)[:, cs_ * 256:(cs_ + 1) * 256],
                                       cnt == 0, cnt == 2 * ntile - 1, deps=fr, sig=(cnt == 2 * ntile - 1))
                                cnt += 1
                        si, sm, sfr = SM.get()
                        t1 = acopy(sm[:, 0:256], ps[:, 0:256], deps=[t] + sfr)
                        PS.rel(pi, t1)
                        pi2, ps2, fr2 = PS.get()
                        t = mm(ps2[:, 0:256], fwbd[:, l, j, :], sm[:, 0:256], True, True, deps=[t1] + fr2, sig=True)
                        SM.rel(si, t)
                        t1 = vcopy(xn[:, 2 + j, c0:c0 + 256], ps2[:, 0:256], deps=[t])
                        PS.rel(pi2, t1)
                    return t

                units = []
                for b in range(4):
                    c0 = b * 256
                    mix_pool_fourier(l, 2, peo_p[:, 2 * b:2 * b + 2], AB_p[:, 2 * b:2 * b + 2],
                                     lambda i_, g: poolp[:, i_, g * 256:(g + 1) * 256], lambda i_: dftp[:, i_, :], c0)
                for b in range(4):
                    for h in range(4):
                        def mk(b=b, h=h):
                            u = {}
                            c0 = b * 256

                            def s0():
                                ei, ev, efr = EP.get()
                                u['ei'] = ei
                                u['E'] = attn_scores(knT, c0, 2, c0, h, [(ei, ev[:, c, :], efr) for c in range(2)])

                            def s1():
                                tpv, qs = attn_pv_a(l, 2, lambda kc: v_aug[:, 2 * b + kc, h, 0:129], u['E'])
                                EP.rel(u['ei'], tpv)
                                u['qs'] = qs

                            def s1b():
                                u['res'] = attn_pv_b(u['qs'])

                            def s2():
                                attn_finish(l, h, c0, u['res'])
                            return [s0, s1, s1b, s2]
                        units.append(mk())
                pipeline(units)
                P.barrier()
                t_z2 = []
                for tti in range(10):
                    t_on2 = vcopy(v_all[:, tti, :, 128:130], ones_b[:, 0:8].rearrange("p (h e) -> p h e", h=4))
                    t_z2.append(t_on2)
                for tti in range(8):
                    t_z2.append(vcopy(peo_s[:, tti, 0, :, 64:128], zeros_b[:, 0:128].rearrange("p (j c) -> p j c", j=2)))
                    t_z2.append(vcopy(peo_s[:, tti, 1, :, 0:64], zeros_b[:, 0:128].rearrange("p (j c) -> p j c", j=2)))
                P.wait_only('sp', [t_on2] + t_z2 + [t_cc])
                P.wait_only('pool', [t_on2])
                for i_ in range(2):
                    t_cv = P.dma('pool', v_all[:, i_, :, 0:128], d['cv'][l][i_ * 128:(i_ + 1) * 128, :].rearrange("p (h e) -> p h e", h=4), 'ccin')
                si, stg, sfr = STG.get()
                t_ck = P.dma('sp', stg.rearrange("p (i c) -> p i c", i=2), d['ck'][l].rearrange("(i p) c -> p i c", p=128), 'st%d' % si, deps=sfr)
                for i_ in range(2):
                    pi, ps, fr = PS.get()
                    for h in range(4):
                        t = tr(ps[:, h * 128:(h + 1) * 128], stg[:, i_ * 512 + h * 128:i_ * 512 + (h + 1) * 128], ident[:],
                               deps=[t_ck] + fr, sig=(h == 3))
                    t1 = evac(kT_all[:, :, i_ * 128:(i_ + 1) * 128], ps[:].rearrange("p (h t) -> p h t", h=4), deps=[t])
                    PS.rel(pi, t1)
                STG.rel(si, t)
                gl = []
                ago = ag_out[l]
                P.wait_only('act', [t_on2] + t_z2 + [t_cc])
                for r in range(4):
                    rows = ago[r * 128:(r + 1) * 128, :]
                    gq = 'sp' if r % 2 == 0 else 'act'
                    gl.append(P.dma(gq, kT_all[:, :, 256 + r * 256:256 + (r + 1) * 256],
                                    rows[:, 0:1024].rearrange("p (h t) -> p h t", h=4), 'gld_' + gq))
                    gl.append(P.dma(gq, AB_s[:, 2 * r:2 * r + 2, :, :].rearrange("p i j c -> p (i j c)"), rows[:, 2560:3584], 'gld_' + gq))
                    for i_ in range(2):
                        gl.append(P.dma(gq, v_all[:, 2 + 2 * r + i_, :, 0:128],
                                        rows[:, 1024 + i_ * 512:1024 + (i_ + 1) * 512].rearrange("p (h e) -> p h e", h=4), 'gld_' + gq))
                        pv_ = rows[:, 2048 + i_ * 256:2048 + (i_ + 1) * 256].rearrange("p (j e c) -> p j e c", j=2, e=2)
                        gl.append(P.dma(gq, peo_s[:, 2 * r + i_, 0, :, 0:64], pv_[:, :, 0, :], 'gld_' + gq))
                        gl.append(P.dma(gq, peo_s[:, 2 * r + i_, 1, :, 64:128], pv_[:, :, 1, :], 'gld_' + gq))
                pend = [None]
                wo_st = {'i': [], 'wv': [], 'tw': [], 'last': None}
                for hf in range(2):
                    i, ws, tw = W.use(('wout', l, hf), [(lambda s_: s_.rearrange("p (k c) -> p k c", k=8),
                                                         d['w_out'][l, :, hf * 512:(hf + 1) * 512].rearrange("(k p) c -> p k c", p=128))])
                    wo_st['i'].append(i); wo_st['wv'].append(ws.rearrange("p (k c) -> p k c", k=8)); wo_st['tw'].append(tw)

                def w_out_part(tis):
                    t_u = None
                    for hf in range(2):
                        wv = wo_st['wv'][hf]
                        tw = wo_st['tw'][hf]
                        for q in range(4):
                            dc = hf * 4 + q
                            for ti in tis:
                                c0, n, cond = TILES[ti]
                                pi, ps, fr = PS.get()
                                for k in range(8):
                                    t = mm(ps[:, :n], wv[:, k, q * 128:(q + 1) * 128], xn[:, k, c0:c0 + n], k == 0, k == 7, deps=[tw] + fr, sig=(k == 7))
                                wo_st['last'] = t
                                if pend[0] is not None:
                                    pend[0]()
                                    pend[0] = None
                                t_u = stt(xT[:, dc, c0:c0 + n], ps[:, :n], Gtab[:, l, 1, cond, dc:dc + 1], xT[:, dc, c0:c0 + n],
                                          ALU.mult, ALU.add, deps=[t])
                                PS.rel(pi, t_u)
                                pend[0] = ns_accum(dc, ti, c0, n, t_u)
                    return t_u

                w_out_part([0, 1])
                P.barrier()
                i0, ws0, tw0 = W.use(('pools', l, 0), [(lambda s_: s_.rearrange("p (k c) -> p k c", k=8),
                                                        d['pools'][:, 0:512].rearrange("(k p) c -> p k c", p=128))])
                i1, ws1, tw1 = W.use(('pools', l, 1), [(lambda s_: s_.rearrange("p (k c) -> p k c", k=8),
                                                        d['pools'][:, 512:1024].rearrange("(k p) c -> p k c", p=128))])
                i2, ws2, tw2 = W.use(('dfts', l), [(lambda s_: s_.rearrange("p (k c) -> p k c", k=8),
                                                    d['dfts'].rearrange("(k p) c -> p k c", p=128))])
                P.wait_only('pe', [tw0, tw1, tw2])
                pm = [ws0.rearrange("p (k c) -> p k c", k=8), ws1.rearrange("p (k c) -> p k c", k=8)]
                dm = ws2.rearrange("p (k c) -> p k c", k=8)
                tl = mix_pool_fourier(l, 8, peo_s, AB_s, lambda i_, g: pm[g // 2][:, i_, (g % 2) * 256:(g % 2 + 1) * 256],
                                      lambda i_: dm[:, i_, :], 1024)
                W.done(i0, tl); W.done(i1, tl); W.done(i2, tl)
                def samp_unit(h, qt):
                    u = {}
                    q0 = 1024 + qt * 128

                    def s0():
                        E = []
                        rels = []
                        for c in range(2):
                            lo = c * 64
                            ei, ev, efr = ES.get()
                            rels.append(ei)
                            tE = None
                            for g0 in (0, 4, 8):
                                nk = min(4, 10 - g0)
                                pi, ps, fr = PS.get()
                                for j in range(nk):
                                    kc = g0 + j
                                    t = mm(ps[:, j * 128:(j + 1) * 128], kT_all[lo:lo + 64, h, kc * 128:(kc + 1) * 128],
                                           qnT[lo:lo + 64, h, q0:q0 + 128], True, True, deps=fr, sig=(j == nk - 1))
                                tE = act(ev[:, g0 * 128:(g0 + nk) * 128], ps[:, 0:nk * 128], AF.Exp, deps=[t] + efr, scale=SCALE)
                                PS.rel(pi, tE)
                            E.append((ev, tE))
                        u['E'], u['rels'] = E, rels

                    def s1():
                        pi, ps, fr = PS.get()
                        t_pv = None
                        for c in range(2):
                            ev, tE = u['E'][c]
                            for kc in range(10):
                                t_pv = mm(ps[:, c * 256:c * 256 + 129], ev[:, kc * 128:(kc + 1) * 128], v_all[:, kc, h, 0:129],
                                          kc == 0, kc == 9, deps=[tE] + fr, sig=(c == 1 and kc == 9))
                        for ei in u['rels']:
                            ES.rel(ei, t_pv)
                        oi, (ob, osb), ofr = OT.get()
                        q = {'pi': pi, 'ps': ps, 'oi': oi, 'ob': ob, 'osb': osb}
                        psv = ps[:].rearrange("p (c x) -> p c x", c=2)
                        q['t'] = recip(osb[:, 0:2], psv[:, :, 128], deps=[t_pv] + ofr)
                        q['t0'] = tsc(ob[:, 0, :], ps[:, 0:128], osb[:, 0:1], ALU.mult, deps=[q['t']])
                        q['t1'] = tsc(ob[:, 1, :], ps[:, 256:384], osb[:, 1:2], ALU.mult, deps=[q['t']])
                        PS.rel(pi, q['t1'])
                        q['t'] = stt(ob[:, 1, :], ob[:, 1, :], sctab[:, l, 3:4], ob[:, 0, :], ALU.mult, ALU.add, deps=[q['t0'], q['t1']])
                        u['qs'] = [q]

                    def s1b():
                        u['res'] = attn_pv_b(u['qs'])

                    def s2():
                        attn_finish(l, h, q0, u['res'])
                    return [s0, s1, s1b, s2]

                units = []
                for h in range(4):
                    for qt in range(2):
                        units.append(samp_unit(h, qt))
                pipeline(units)
                P.barrier()
                t_u = w_out_part([2])
                for hf in range(2):
                    W.done(wo_st['i'][hf], wo_st['last'])
                if pend[0] is not None:
                    pend[0]()
                    pend[0] = None
                ns['ready'] = True
                return [t_u]

            P.barrier(skip_w=True)
            pump_ada(0, 4)
            t_in = []
            for l in range(DEPTH):
                t_in = ffn(l, 0, t_in, ada_l=(l, 6, 18))
                pump_ada(l, 18)
                t_in = mixing(l, t_in)
                t_in = ffn(l, 2, t_in, ada_l=((l + 1, 0, 18) if l + 1 < DEPTH else None), hoist=(l + 1 < DEPTH))
                if l + 1 < DEPTH:
                    pump_ada(l + 1, 18)
            P.barrier()
            outs = []
            for tti in range(10):
                dst = d['yp'][tti * 128:(tti + 1) * 128, :] if tti < 8 else d['ys'][(tti - 8) * 128:(tti - 7) * 128, :]
                si, stg, sfr = STG.get()
                te = None
                for half in range(2):
                    pi, ps, pfr = PS.get()
                    for q in range(4):
                        kc = half * 4 + q
                        t = tr(ps[:, q * 128:(q + 1) * 128], xT[:, kc, tti * 128:(tti + 1) * 128], ident[:], deps=pfr, sig=(q == 3))
                    te = evac(stg[:, half * 512:(half + 1) * 512], ps[:, 0:512], deps=[t] + sfr)
                    PS.rel(pi, te)
                    outs.append(te)
                t_d = P.dma('sp', dst, stg, 'st%d' % si, deps=outs[-2:])
                STG.rel(si, t_d)
            P.barrier()

        P1 = Prog(nc, st, dry=True)
        W1 = WRing(P1, wbuf, None)
        emit_all(P1, W1)
        P2 = Prog(nc, st, dry=False)
        W2 = WRing(P2, wbuf, W1.plan)
        emit_all(P2, W2)
        P2.emit()
    return nc


def _consts():
    c = {}
    c['c_ident'] = np.eye(128, dtype=np.float32)
    pm = np.zeros((128, 128), np.float32)
    for i in range(128):
        if (i % 32) < 16:
            pm[i, i + 16] = -1.0
        else:
            pm[i, i - 16] = 1.0
    c['c_prot'] = np.ascontiguousarray(pm.T)
    cc = np.arange(64)
    ang = 2 * np.pi * np.outer(cc, cc) / 64.0
    C64 = np.cos(ang); S64 = np.sin(ang)
    cs = np.zeros((128, 256), np.float64)
    for hh in range(2):
        cs[hh * 64:(hh + 1) * 64, hh * 64:(hh + 1) * 64] = C64
        cs[hh * 64:(hh + 1) * 64, 128 + hh * 64:128 + (hh + 1) * 64] = S64
    c['c_cs'] = cs.astype(np.float32)

    def dft(L, cols):
        l_ = np.arange(L)[:, None].astype(np.float64)
        lp = np.asarray(cols)[None, :].astype(np.float64)
        a = 2 * np.pi * ((l_ * lp) % L) / L
        s = 1.0 / math.sqrt(64.0 * L)
        return np.concatenate([s * np.cos(a), -s * np.sin(a)], axis=1).astype(np.float32)

    def poolm(L, cols):
        out = np.zeros((L, 4, len(cols)), np.float64)
        for g, w in enumerate((2, 4, 8, 16)):
            for ci, t in enumerate(cols):
                lo = min(max(t - w // 2, 0), L); hi = min(max(t + w // 2, 0), L)
                out[lo:hi, g, ci] += 1.0 / (hi - lo)
                out[t, g, ci] -= 1.0
        return out.reshape(L, 4 * len(cols)).astype(np.float32)

    c['c_dftp'] = dft(256, np.arange(256))
    c['c_poolp'] = poolm(256, list(range(256)))
    per_rank = []
    inv = 1.0 / (10000.0 ** (np.arange(0, 32, 2, dtype=np.float32) / 32.0))
    for r in range(4):
        cols = np.arange(r * 256, (r + 1) * 256)
        pr = {'c_dfts': dft(1024, cols), 'c_pools': poolm(1024, list(cols))}
        row = (cols // 64).astype(np.float32); col = (cols % 64).astype(np.float32)
        rc = np.zeros((128, 256), np.float32); rs = np.zeros((128, 256), np.float32)
        for p in range(128):
            dd = p % 64
            if dd < 32:
                a = row * inv[dd % 16]
            else:
                a = col * inv[(dd - 32) % 16]
            rc[p] = np.cos(a.astype(np.float32)); rs[p] = np.sin(a.astype(np.float32))
        pr['c_ropec'] = rc; pr['c_ropes'] = rs
        per_rank.append(pr)
    return c, per_rank


_NC_CACHE = {}


def kernel(**inputs):
    inp = {k: np.ascontiguousarray(np.asarray(v)) for k, v in inputs.items()}
    if 'nc' not in _NC_CACHE:
        _NC_CACHE['nc'] = build_nc()
    nc = _NC_CACHE['nc']
    consts, per_rank = _consts()
    shared = {}
    for nm in ('norm_g', 'ada_w', 'ada_b', 'ffn1_wi', 'ffn1_wo', 'ffn2_wi', 'ffn2_wo', 'w_in', 'w_out', 'q_norm_g', 'k_norm_g',
               'lam_q1', 'lam_k1', 'lam_q2', 'lam_k2', 'attn_out_g', 'pool_w', 'fnet_w', 'pool_scale'):
        shared[nm] = inp[nm].astype(np.float32, copy=False)
    shared.update(consts)
    in_maps = []
    for c in range(8):
        bs, r = c // 4, c % 4
        m = dict(shared)
        m['xp'] = inp['x_prompt'][4 * c:4 * c + 4].reshape(1024, D)
        m['xs'] = inp['x_sample'][bs, r * 256:(r + 1) * 256, :]
        m['ck'] = inp['cache_k'][bs].reshape(DEPTH, 256, 512)
        m['cv'] = inp['cache_v'][bs].reshape(DEPTH, 256, 512)
        m['cond'] = np.stack([inp['c_ctx'], inp['c'][bs]], axis=0)
        m.update(per_rank[r])
        in_maps.append({k: np.ascontiguousarray(v, dtype=np.float32) for k, v in m.items()})
    res = run_bass_kernel_spmd(nc, in_maps, core_ids=list(range(8)))
    R = res.results
    y_prompt = np.concatenate([np.asarray(R[c]['yp']).reshape(4, 256, D) for c in range(8)], axis=0)
    y_sample = np.stack([np.concatenate([np.asarray(R[bs * 4 + r]['ys']) for r in range(4)], axis=0) for bs in range(2)], axis=0)
    nk = np.concatenate([np.asarray(R[c]['nk']).reshape(DEPTH, 4, 256, 4, 128).transpose(1, 0, 2, 3, 4) for c in range(8)], axis=0)
    nv = np.concatenate([np.asarray(R[c]['nv']).reshape(DEPTH, 4, 256, 4, 128).transpose(1, 0, 2, 3, 4) for c in range(8)], axis=0)
    return (y_prompt.astype(np.float32), y_sample.astype(np.float32),
            np.ascontiguousarray(nk, dtype=np.float32), np.ascontiguousarray(nv, dtype=np.float32))
```

```python
import math
import numpy as np
from contextlib import ExitStack
import concourse.bass as bass
import concourse.mybir as mybir
from concourse.bass_utils import run_bass_kernel_spmd

F32 = mybir.dt.float32
BF16 = mybir.dt.bfloat16
AF = mybir.ActivationFunctionType
ALU = mybir.AluOpType
AX = mybir.AxisListType

D = 1024
DEPTH = 2
T = 1280
TILES = [(0, 512, 0), (512, 512, 0), (1024, 256, 1)]
DFF = 2816
NF = 22
EPS = 1e-6
SCALE = 64 ** -0.5
NBUF = 5
ENGS = ('sp', 'act', 'pool', 'dve', 'pe')
PAYC = 3584


class Prog:
    def __init__(self, nc, stack, dry=False):
        self.nc = nc
        self.stack = stack
        self.dry = dry
        self.thunks = {e: [] for e in ENGS}
        self.semh = {}
        self.val = {}
        self.seen = {e: {} for e in ENGS}

    def sem(self, name):
        if name not in self.val:
            if not self.dry:
                self.semh[name] = self.stack.enter_context(self.nc.semaphore(name))
            self.val[name] = 0
        return name

    def _waits(self, eng, toks):
        ws = []
        for t in toks:
            if t is None:
                continue
            name, v = t
            if self.seen[eng].get(name, 0) < v:
                self.seen[eng][name] = v
                ws.append((name, v))
        return ws

    def op(self, eng, fn, deps=(), sig=True):
        if eng in ('act', 'dve') and self.val.get('p_' + eng, 0) > 0:
            deps = list(deps) + [('p_' + eng, self.val['p_' + eng])]
        ws = self._waits(eng, deps)
        tok = None
        if sig:
            name = self.sem('p_' + eng)
            self.val[name] += 1
            tok = (name, self.val[name])
        self.thunks[eng].append((ws, fn, tok, 1))
        return tok

    def dma(self, eng, out, in_, chan, deps=()):
        ws = self._waits(eng, deps)
        name = self.sem(chan)
        self.val[name] += 16
        tok = (name, self.val[name])
        self.thunks[eng].append((ws, lambda e: e.dma_start(out=out, in_=in_), tok, 16))
        return tok

    def cc(self, fn, chan, deps=()):
        ws = self._waits('pool', deps)
        name = self.sem(chan)
        self.val[name] += 1
        tok = (name, self.val[name])
        self.thunks['pool'].append((ws, fn, tok, 'cc'))
        return tok

    def wait_only(self, eng, deps):
        ws = self._waits(eng, deps)
        if ws:
            self.thunks[eng].append((ws, None, None, 0))

    def barrier(self, skip=('out',), skip_w=False):
        if skip_w:
            skip = tuple(skip) + tuple('w%d' % i for i in range(NBUF))
        toks = [(n, v) for n, v in self.val.items() if v > 0 and n not in skip]
        for e in ENGS:
            self.wait_only(e, toks)

    def emit(self):
        with self.nc.Block() as block:
            def run(engname):
                def f(e):
                    for ws, fn, tok, inc in self.thunks[engname]:
                        for (n, v) in ws:
                            e.wait_ge(self.semh[n], v)
                        if fn is None:
                            continue
                        ins = fn(e)
                        if tok is not None:
                            if inc == 'cc':
                                ins.then_inc(self.semh[tok[0]])
                            else:
                                ins.then_inc(self.semh[tok[0]], inc)
                return f
            block.sync(run('sp'))
            block.scalar(run('act'))
            block.gpsimd(run('pool'))
            block.vector(run('dve'))
            block.tensor(run('pe'))


class Rot:
    def __init__(self, aps):
        self.aps = aps
        self.i = 0
        self.free = [[] for _ in aps]
        self.busy = {}

    def get(self, skip_busy=False):
        idx = self.i % len(self.aps)
        if skip_busy:
            for _ in range(len(self.aps)):
                if not self.busy.get(idx, False):
                    break
                self.i += 1
                idx = self.i % len(self.aps)
        self.i += 1
        assert not self.busy.get(idx, False), ('rotating buffer reused before release', idx)
        self.busy[idx] = True
        toks = self.free[idx]
        self.free[idx] = []
        return idx, self.aps[idx], toks

    def rel(self, idx, tok):
        self.free[idx].append(tok)
        self.busy[idx] = False


class WRing:
    def __init__(self, P, wbuf, plan=None):
        self.P = P
        self.wbuf = wbuf
        self.collect = plan is None
        self.plan = [] if plan is None else plan
        self.free = [[] for _ in range(NBUF)]
        self.rec = 0
        self.cur = 0
        self.tok = {}
        self.donef = {}

    def _record(self, j):
        s = j % NBUF
        key, dmas = self.plan[j]
        deps = self.free[s]
        self.free[s] = []
        tok = None
        for (dstf, src) in dmas:
            tok = self.P.dma('pool', dstf(self.wbuf[:, s, :]), src, 'w%d' % s, deps=deps)
        self.tok[j] = tok

    def _advance(self):
        while self.rec < len(self.plan) and (self.rec < NBUF or self.donef.get(self.rec - NBUF)):
            self._record(self.rec)
            self.rec += 1

    def use(self, key, dmas):
        i = self.cur
        self.cur += 1
        if self.collect:
            self.plan.append((key, dmas))
            return i, self.wbuf[:, i % NBUF, :], None
        assert self.plan[i][0] == key, (self.plan[i][0], key)
        self._advance()
        assert i < self.rec, (i, self.rec, key)
        return i, self.wbuf[:, i % NBUF, :], self.tok[i]

    def prestart(self):
        if not self.collect:
            self._advance()

    def done(self, i, tok):
        self.free[i % NBUF].append(tok)
        self.donef[i] = True
        if not self.collect:
            self._advance()


def build_nc():
    nc = bass.Bass("TRN2", target_bir_lowering=False)

    def din(name, shape):
        return nc.dram_tensor(name, list(shape), F32, kind="ExternalInput").ap()

    def dout(name, shape):
        return nc.dram_tensor(name, list(shape), F32, kind="ExternalOutput").ap()

    d = {}
    d['xp'] = din('xp', [1024, D]); d['xs'] = din('xs', [256, D])
    d['ck'] = din('ck', [DEPTH, 256, 512]); d['cv'] = din('cv', [DEPTH, 256, 512])
    d['cond'] = din('cond', [2, D])
    d['norm_g'] = din('norm_g', [DEPTH, 3, D])
    d['ada_w'] = din('ada_w', [DEPTH, D, 9 * D]); d['ada_b'] = din('ada_b', [DEPTH, 9 * D])
    d['ffn1_wi'] = din('ffn1_wi', [DEPTH, D, 2 * DFF]); d['ffn1_wo'] = din('ffn1_wo', [DEPTH, DFF, D])
    d['ffn2_wi'] = din('ffn2_wi', [DEPTH, D, 2 * DFF]); d['ffn2_wo'] = din('ffn2_wo', [DEPTH, DFF, D])
    d['w_in'] = din('w_in', [DEPTH, D, 2048]); d['w_out'] = din('w_out', [DEPTH, D, D])
    for nm in ('q_norm_g', 'k_norm_g', 'lam_q1', 'lam_k1', 'lam_q2', 'lam_k2'):
        d[nm] = din(nm, [DEPTH, 64])
    d['attn_out_g'] = din('attn_out_g', [DEPTH, 128])
    d['pool_w'] = din('pool_w', [DEPTH, 4, 64, 64]); d['fnet_w'] = din('fnet_w', [DEPTH, 4, 64, 64])
    d['pool_scale'] = din('pool_scale', [DEPTH, 256])
    d['ident'] = din('c_ident', [128, 128]); d['prot'] = din('c_prot', [128, 128])
    d['ropec'] = din('c_ropec', [128, 256]); d['ropes'] = din('c_ropes', [128, 256])
    d['cs'] = din('c_cs', [128, 256]); d['dftp'] = din('c_dftp', [256, 512]); d['poolp'] = din('c_poolp', [256, 1024])
    d['dfts'] = din('c_dfts', [1024, 512]); d['pools'] = din('c_pools', [1024, 1024])
    d['yp'] = dout('yp', [1024, D]); d['ys'] = dout('ys', [256, D])
    d['nk'] = dout('nk', [DEPTH, 1024, 512]); d['nv'] = dout('nv', [DEPTH, 1024, 512])
    ag_in = [nc.dram_tensor('ag_in%d' % l, [128, PAYC], BF16).ap() for l in range(DEPTH)]
    ag_out = [nc.dram_tensor('ag_out%d' % l, [512, PAYC], BF16).ap() for l in range(DEPTH)]

    with ExitStack() as st:
        def sb(name, shape, dt):
            return st.enter_context(nc.sbuf_tensor(name, shape, dt))

        xT = sb('xT', [128, 8, T], F32)
        xn = sb('xn', [128, 8, T], BF16)
        ARENA = 29776
        arena = sb('arena', [128, ARENA], BF16)
        wbuf = sb('wbuf', [128, NBUF, 4096], BF16)
        ident = sb('ident', [128, 128], F32)
        ones_f = sb('ones_f', [128, 128], F32)
        ones_b = sb('ones_b', [128, 128], BF16)
        bones_b = sb('bones_b', [128, 128], BF16)
        zeros_b = sb('zeros_b', [128, 128], BF16)
        prot = sb('prot', [128, 128], F32)
        ropec = sb('ropec', [128, 256], F32)
        ropes = sb('ropes', [128, 256], F32)
        csmat = sb('csmat', [128, 256], BF16)
        dftp = sb('dftp', [128, 2, 512], BF16)
        poolp = sb('poolp', [128, 2, 1024], BF16)
        pwbd = sb('pwbd', [128, DEPTH, 2, 128], BF16)
        fwbd = sb('fwbd', [128, DEPTH, 2, 128], BF16)
        vecrows = sb('vecrows', [128, DEPTH, 128], F32)
        vcols = sb('vcols', [128, DEPTH, 128], F32)
        condrows = sb('condrows', [16, 128], F32)
        condsil = sb('condsil', [16, 128], F32)
        scT = sb('scT', [128, 8, 2], BF16)
        modT = sb('modT', [128, DEPTH, 72, 2], F32)
        Atab = sb('Atab', [128, DEPTH, 3, 2, 8], F32)
        Gtab = sb('Gtab', [128, DEPTH, 3, 2, 8], F32)
        sctab = sb('sctab', [128, DEPTH, 8], F32)
        epsT = sb('epsT', [128, 1], F32)
        rstd_t = sb('rstd_t', [128, 2, 512], F32)
        sqc_t = sb('sqc_t', [128, 2, 512], BF16)
        sa_t = sb('sa_t', [128, 2, 512], F32)
        stg_t = sb('stg_t', [128, 2, 1024], F32)
        sm_t = sb('sm_t', [128, 2, 256], BF16)
        ot = sb('ot', [128, 6, 2, 128], F32)
        os_t = sb('os_t', [128, 6, 8], F32)
        rp_t = sb('rp_t', [128, 2, 2, 256], F32)
        lam_t = sb('lam_t', [128, 8], F32)

        psb = [st.enter_context(nc.psum_tensor('ps%d' % i, [128, 512], F32)) for i in range(8)]
        modps = psb[7]

        hv = arena[:, 0:NF * T].rearrange("p (f t) -> p f t", f=NF)
        SCR0 = ARENA - 12288
        tscr = arena[:, SCR0:SCR0 + 8192].bitcast(F32).rearrange("p (k t) -> p k t", k=8)
        sq = arena[:, SCR0 + 8192:SCR0 + 12288].rearrange("p (k t) -> p k t", k=8)
        qnT = arena[:, 0:5120].rearrange("p (h t) -> p h t", h=4)
        R0 = 5120
        knT = arena[:, R0:R0 + 4096].rearrange("p (h t) -> p h t", h=4)
        v_aug = arena[:, R0 + 4096:R0 + 8256].rearrange("p (i h e) -> p i h e", i=8, h=4)
        peo_p = arena[:, R0 + 8256:R0 + 12352].rearrange("p (i e j c) -> p i e j c", i=8, e=2, j=2)
        AB_p = arena[:, R0 + 12352:R0 + 16448].rearrange("p (i j c) -> p i j c", i=8, j=2)
        khat_r = arena[:, R0 + 16448:R0 + 18496].bitcast(F32).rearrange("p (r t) -> p r t", r=2)
        kT_all = arena[:, R0:R0 + 5120].rearrange("p (h t) -> p h t", h=4)
        v_all = arena[:, R0 + 5120:R0 + 10320].rearrange("p (i h e) -> p i h e", i=10, h=4)
        peo_s = arena[:, R0 + 10320:R0 + 14416].rearrange("p (i e j c) -> p i e j c", i=8, e=2, j=2)
        AB_s = arena[:, R0 + 14416:R0 + 18512].rearrange("p (i j c) -> p i j c", i=8, j=2)
        F0 = R0 + 18512
        fT = arena[:, F0:F0 + 2560].rearrange("p (j t) -> p j t", j=2)
        pay = arena[:, F0 + 2560:F0 + 2560 + PAYC]
        E_p = arena[:, F0:F0 + 4096].rearrange("p (r c x) -> p r c x", r=4, c=2)
        E_s = arena[:, F0:F0 + 5120].rearrange("p (r x) -> p r x", r=4)

        def emit_all(P, W):
            PS = Rot([psb[i] for i in range(8)])
            _psget = PS.get
            PS.get = lambda: _psget(skip_busy=True)
            ns = {'bank': [None, None, None], 'tok': [None, None, None], 'ready': False}
            SQC = Rot([sqc_t[:, i, :] for i in range(2)])
            STG = Rot([stg_t[:, i, :] for i in range(2)])
            SA = Rot([sa_t[:, i, :] for i in range(2)])
            RS = Rot([rstd_t[:, i, :] for i in range(2)])
            SM = Rot([sm_t[:, i, :] for i in range(2)])
            OT = Rot([(ot[:, i], os_t[:, i, :]) for i in range(6)])
            RP = Rot([rp_t[:, i] for i in range(2)])
            KH = Rot([khat_r[:, i, :] for i in range(2)])
            EP = Rot([E_p[:, i] for i in range(4)])
            ES = Rot([E_s[:, i, :] for i in range(4)])

            def mm(out, lhsT, rhs, start, stop, deps=(), sig=False):
                return P.op('pe', lambda e: e.matmul(out, lhsT=lhsT, rhs=rhs, start=start, stop=stop), deps=deps, sig=sig)

            def tr(out, in_, idn, deps=(), sig=True):
                return P.op('pe', lambda e: e.transpose(out=out, in_=in_, identity=idn), deps=deps, sig=sig)

            def act(out, in_, func, deps=(), bias=None, scale=None, accum=None):
                kw = {}
                if bias is not None:
                    kw['bias'] = bias
                if scale is not None:
                    kw['scale'] = scale
                if accum is not None:
                    kw['accum_out'] = accum
                return P.op('act', lambda e: e.activation(out=out, in_=in_, func=func, **kw), deps=deps)

            def amul(out, in_, m, deps=()):
                return P.op('act', lambda e: e.mul(out=out, in_=in_, mul=m), deps=deps)

            def acopy(out, in_, deps=()):
                return P.op('act', lambda e: e.copy(out=out, in_=in_), deps=deps)

            def vcopy(out, in_, deps=()):
                return P.op('dve', lambda e: e.tensor_copy(out=out, in_=in_), deps=deps)

            def tt(out, in0, in1, op, deps=()):
                return P.op('dve', lambda e: e.tensor_tensor(out=out, in0=in0, in1=in1, op=op), deps=deps)

            def tsc(out, in0, s1, op0, s2=None, op1=None, deps=()):
                if op1 is None:
                    return P.op('dve', lambda e: e.tensor_scalar(out=out, in0=in0, scalar1=s1, scalar2=None, op0=op0), deps=deps)
                return P.op('dve', lambda e: e.tensor_scalar(out=out, in0=in0, scalar1=s1, scalar2=s2, op0=op0, op1=op1), deps=deps)

            def stt(out, in0, scalar, in1, op0, op1, deps=()):
                return P.op('dve', lambda e: e.scalar_tensor_tensor(out=out, in0=in0, scalar=scalar, in1=in1, op0=op0, op1=op1), deps=deps)

            def recip(out, in_, deps=()):
                return P.op('dve', lambda e: e.reciprocal(out=out, in_=in_), deps=deps)

            def rsum(out, in_, deps=()):
                return P.op('dve', lambda e: e.reduce_sum(out=out, in_=in_, axis=AX.X), deps=deps)

            def memset(ap, v, deps=()):
                return P.op('dve', lambda e: e.memset(ap, v), deps=deps)

            cp_flip = [0]

            def evac(out, in_, deps=()):
                cp_flip[0] ^= 1
                return (acopy if cp_flip[0] else vcopy)(out, in_, deps)

            t_id = P.dma('sp', ident[:], d['ident'], 'cst0')
            xst = [arena[:, i * 2048:(i + 1) * 2048].bitcast(F32) for i in range(10)]
            t_lds = []
            for tti in range(10):
                src = d['xp'][tti * 128:(tti + 1) * 128, :] if tti < 8 else d['xs'][(tti - 8) * 128:(tti - 7) * 128, :]
                t_lds.append(P.dma('sp' if tti % 2 == 0 else 'act', xst[tti], src, 'xl%d' % tti))
            for tti in range(10):
                stg = xst[tti]
                for half in range(2):
                    pi, ps, pfr = PS.get()
                    for q in range(4):
                        kc = half * 4 + q
                        t = tr(ps[:, q * 128:(q + 1) * 128], stg[:, kc * 128:(kc + 1) * 128], ident[:],
                               deps=[t_lds[tti], t_id] + pfr, sig=(q == 3))
                    tcp = evac(xT[:, half * 4:half * 4 + 4, tti * 128:(tti + 1) * 128],
                               ps[:].rearrange("p (q t) -> p q t", q=4), deps=[t])
                    PS.rel(pi, tcp)

            P.dma('sp', prot[:], d['prot'], 'cst')
            P.dma('sp', ropec[:], d['ropec'], 'cst')
            P.dma('sp', ropes[:], d['ropes'], 'cst')
            P.dma('sp', condrows[:], d['cond'].rearrange("j (k c) -> (j k) c", c=128), 'cst')
            t_m = [memset(vecrows[:], 0.0), memset(pwbd[:], 0.0), memset(fwbd[:], 0.0)]
            memset(ones_f[:], 1.0); memset(ones_b[:], 1.0); memset(bones_b[:], 0.0); memset(zeros_b[:], 0.0)
            memset(bones_b[0:64, 0:64], 1.0); memset(bones_b[64:128, 64:128], 1.0)
            memset(epsT[:], EPS)
            P.dma('pool', csmat[:], d['cs'], 'cst2')
            P.dma('pool', dftp[:], d['dftp'].rearrange("(i p) c -> p i c", p=128), 'cst2')
            P.dma('pool', poolp[:], d['poolp'].rearrange("(i p) c -> p i c", p=128), 'cst2')
            P.wait_only('pool', t_m)
            P.wait_only('sp', t_m)
            for l in range(DEPTH):
                for g in range(4):
                    r0 = (g % 2) * 64
                    P.dma('pool', pwbd[r0:r0 + 64, l, g // 2, r0:r0 + 64], d['pool_w'][l, g], 'cst2')
                    P.dma('pool', fwbd[r0:r0 + 64, l, g // 2, r0:r0 + 64], d['fnet_w'][l, g], 'cst2')
            W.prestart()
            for l in range(DEPTH):
                P.dma('sp', vecrows[0:72, l, :], d['ada_b'][l].rearrange("(r c) -> r c", c=128), 'cst')
                P.dma('sp', vecrows[72:96, l, :], d['norm_g'][l].rearrange("s (k c) -> (s k) c", c=128), 'cst')
                P.dma('sp', vecrows[96:98, l, :], d['pool_scale'][l].rearrange("(r c) -> r c", c=128), 'cst')
                P.dma('sp', vecrows[98:99, l, 0:64], d['q_norm_g'][l:l + 1, :], 'cst')
                P.dma('sp', vecrows[98:99, l, 64:128], d['q_norm_g'][l:l + 1, :], 'cst')
                P.dma('sp', vecrows[99:100, l, 0:64], d['k_norm_g'][l:l + 1, :], 'cst')
                P.dma('sp', vecrows[99:100, l, 64:128], d['k_norm_g'][l:l + 1, :], 'cst')
                P.dma('sp', vecrows[100:101, l, :], d['attn_out_g'][l:l + 1, :], 'cst')
                for i, nm in enumerate(('lam_q1', 'lam_k1', 'lam_q2', 'lam_k2')):
                    P.dma('sp', vecrows[101 + i:102 + i, l, 0:64], d[nm][l:l + 1, :], 'cst')
            P.barrier(skip_w=True)
            t = act(condsil[:], condrows[:], AF.Silu)
            pi, ps, fr = PS.get()
            t = tr(ps[:, 0:16], condsil[:], ident[0:16, 0:16], deps=[t] + fr)
            t = vcopy(scT[:].rearrange("p k j -> p j k"), ps[:, 0:16].rearrange("p (j k) -> p j k", j=2), deps=[t])
            PS.rel(pi, t)
            for l in range(DEPTH):
                lam_init = 0.8 - 0.6 * math.exp(-0.3 * l)
                pi, ps, fr = PS.get()
                t = tr(ps[:, 0:128], vecrows[:, l, :], ident[:], deps=fr)
                t = vcopy(vcols[:, l, :], ps[:, 0:128], deps=[t])
                t_vc = t
                PS.rel(pi, t)
                t1 = tt(lam_t[:, 0:1], vcols[:, l, 101:102], vcols[:, l, 102:103], ALU.mult, deps=[t])
                t2 = tt(lam_t[:, 1:2], vcols[:, l, 103:104], vcols[:, l, 104:105], ALU.mult, deps=[t])
                pi, ps, fr = PS.get()
                t = mm(ps[:, 0:2], ones_f[:], lam_t[:, 0:2], True, True, deps=[t1, t2] + fr, sig=True)
                t = act(lam_t[:, 2:4], ps[:, 0:2], AF.Exp, deps=[t])
                PS.rel(pi, t)
                t = tsc(lam_t[:, 4:5], lam_t[:, 3:4], -lam_init, ALU.add, deps=[t])
                t = tt(sctab[:, l, 3:4], lam_t[:, 4:5], lam_t[:, 2:3], ALU.subtract, deps=[t])
                tsc(sctab[:, l, 2:3], vcols[:, l, 100:101], (1.0 - lam_init), ALU.mult, deps=[t_vc])
            P.barrier(skip_w=True)

            ada_state = {'next': [0, 0], 'rd': [None] * 18, 'tab': {}}

            def ada_block(l, b):
                i, ws, tw = W.use(('ada', l, b), [(lambda s: s.rearrange("p (k c) -> p k c", k=8),
                                                   d['ada_w'][l, :, b * 512:(b + 1) * 512].rearrange("(k p) c -> p k c", p=128))])
                wv = ws.rearrange("p (k c) -> p k c", k=8)
                t = None
                pi, ps, fr = PS.get()
                for q in range(4):
                    for k in range(8):
                        t = mm(ps[:, 2 * q:2 * q + 2], wv[:, k, q * 128:(q + 1) * 128], scT[:, k, :], k == 0, k == 7,
                               deps=[tw] + fr, sig=(q == 3 and k == 7))
                W.done(i, t)
                te = tt(modT[:, l, 4 * b:4 * b + 4, :], ps[:, 0:8].rearrange("p (c j) -> p c j", j=2),
                        vcols[:, l, 4 * b:4 * b + 4].unsqueeze(2).to_broadcast([128, 4, 2]), ALU.add, deps=[t])
                PS.rel(pi, te)
                ada_state['rd'][b] = te
                if b % 6 == 3:
                    s = b // 6
                    for cond in range(2):
                        te = stt(Atab[:, l, s, cond, :], modT[:, l, (3 * s + 1) * 8:(3 * s + 2) * 8, cond], 1.0,
                                 vcols[:, l, 72 + s * 8:72 + (s + 1) * 8], ALU.add, ALU.mult, deps=[te])
                    ada_state['tab'][(l, s)] = te
                if b % 6 == 5:
                    s = b // 6
                    for cond in range(2):
                        te = tsc(Gtab[:, l, s, cond, :], modT[:, l, (3 * s + 2) * 8:(3 * s + 3) * 8, cond],
                                 (1.0 if s == 1 else 0.5), ALU.mult, deps=[te])

            def pump_ada(l, upto):
                while ada_state['next'][l] < upto:
                    ada_block(l, ada_state['next'][l])
                    ada_state['next'][l] += 1

            def ns_accum(dc, ti, c0, n, t_u):
                si, sqb, sfr = SQC.get()
                t_q = act(sqb[:, :n], xT[:, dc, c0:c0 + n], AF.Square, deps=[t_u] + sfr)

                def later():
                    fr = []
                    if dc == 0:
                        bi, bap, fr = PS.get()
                        ns['bank'][ti] = (bi, bap)
                    t_m = mm(ns['bank'][ti][1][:, :n], ones_b[:], sqb[:, :n], dc == 0, dc == 7, deps=[t_q] + fr, sig=True)
                    SQC.rel(si, t_m)
                    if dc == 7:
                        ns['tok'][ti] = t_m
                return later

            def norm(l, s, t_in):
                if not ns['ready']:
                    t_ss = None
                    for ti, (c0, n, cond) in enumerate(TILES):
                        t_sq = act(sq[:, :, :n], xT[:, :, c0:c0 + n], AF.Square, deps=list(t_in) + [t_ss])
                        bi, bap, bfr = PS.get()
                        ns['bank'][ti] = (bi, bap)
                        for k in range(8):
                            t_ss = mm(bap[:, :n], ones_b[:], sq[:, k, :n], k == 0, k == 7, deps=[t_sq] + bfr, sig=(k == 7))
                        ns['tok'][ti] = t_ss
                pump_ada(l, 6 * s + 4)
                t_tab = ada_state['tab'][(l, s)]
                toks = []
                for ti, (c0, n, cond) in enumerate(TILES):
                    t_ss = ns['tok'][ti]
                    bi, bap = ns['bank'][ti]
                    ri, rs, rfr = RS.get()
                    t_sd = act(rs[:, :n], bap[:, :n], AF.Ln, deps=[t_ss] + rfr, bias=epsT[:, 0:1], scale=1.0 / D)
                    PS.rel(bi, t_sd)
                    t_r = act(rs[:, :n], rs[:, :n], AF.Exp, deps=[t_sd], scale=-0.5)
                    t_t = tt(tscr[:, :, :n], xT[:, :, c0:c0 + n], rs[:, :n].unsqueeze(1).to_broadcast([128, 8, n]), ALU.mult,
                             deps=[t_r] + list(t_in))
                    RS.rel(ri, t_t)
                    for k in range(5):
                        t_x = act(xn[:, k, c0:c0 + n], tscr[:, k, :n], AF.Identity, deps=[t_t, t_tab],
                                  scale=Atab[:, l, s, cond, k:k + 1], bias=modT[:, l, 3 * s * 8 + k, cond:cond + 1])
                    for k in range(5, 8):
                        t_y = tsc(xn[:, k, c0:c0 + n], tscr[:, k, :n], Atab[:, l, s, cond, k:k + 1], ALU.mult,
                                  s2=modT[:, l, 3 * s * 8 + k, cond:cond + 1], op1=ALU.add, deps=[t_t, t_tab])
                    toks.append(t_x)
                    toks.append(t_y)
                ns['ready'] = False
                return toks

            def ffn(l, s, t_in, ada_l=None, hoist=True):
                wi = d['ffn1_wi'] if s == 0 else d['ffn2_wi']
                wo = d['ffn1_wo'] if s == 0 else d['ffn2_wo']
                xtok = norm(l, s, t_in)
                t_h = None
                t_u = None
                for gi in range(11):
                    wsrc = wi[l].rearrange("(k p) (a f) -> p k a f", p=128, a=2)
                    dmas = []
                    for a in range(2):
                        dmas.append(((lambda s_, a=a: s_.rearrange("p (k a c) -> p k a c", k=8, a=2)[:, :, a, :]),
                                     wsrc[:, :, a, gi * 256:(gi + 1) * 256]))
                    i, ws, tw = W.use(('wi', l, s, gi), dmas)
                    wv = ws.rearrange("p (k a c) -> p k a c", k=8, a=2)
                    last = None
                    for q in range(2):
                        f = gi * 2 + q
                        for ti, (c0, n, cond) in enumerate(TILES):
                            ia, pa, fa = PS.get()
                            for k in range(8):
                                ta = mm(pa[:, :n], wv[:, k, 0, q * 128:(q + 1) * 128], xn[:, k, c0:c0 + n], k == 0, k == 7,
                                        deps=[tw, xtok[2 * ti], xtok[2 * ti + 1]] + fa, sig=(k == 7))
                            ib, pb, fb = PS.get()
                            for k in range(8):
                                tb = mm(pb[:, :n], wv[:, k, 1, q * 128:(q + 1) * 128], xn[:, k, c0:c0 + n], k == 0, k == 7,
                                        deps=[tw] + fb, sig=(k == 7))
                            si, sa, sfr = SA.get()
                            t_s = act(sa[:, :n], pa[:, :n], AF.Silu, deps=[ta] + sfr)
                            PS.rel(ia, t_s)
                            t_h = tt(hv[:, f, c0:c0 + n], sa[:, :n], pb[:, :n], ALU.mult, deps=[t_s, tb])
                            PS.rel(ib, t_h)
                            SA.rel(si, t_h)
                            last = tb
                    W.done(i, last)
                    if ada_l is not None:
                        ll, lo, hi = ada_l
                        pump_ada(ll, min(hi, ada_state['next'][ll] + 2))
                pend = [None]
                for dc in range(8):
                    wsrc = wo[l, :, dc * 128:(dc + 1) * 128].rearrange("(f p) c -> p f c", p=128)
                    dmas = []
                    for (f0, f1) in ((0, 8), (8, 15), (15, 22)):
                        dmas.append(((lambda s_, f0=f0, f1=f1: s_[:, 0:NF * 128].rearrange("p (f c) -> p f c", f=NF)[:, f0:f1, :]),
                                     wsrc[:, f0:f1, :]))
                    i, ws, tw = W.use(('wo', l, s, dc), dmas)
                    wv = ws[:, 0:NF * 128].rearrange("p (f c) -> p f c", f=NF)
                    t = None
                    for ti, (c0, n, cond) in enumerate(TILES):
                        ip, ps, fr = PS.get()
                        for f in range(NF):
                            t = mm(ps[:, :n], wv[:, f, :], hv[:, f, c0:c0 + n], f == 0, f == NF - 1, deps=[tw, t_h] + fr, sig=(f == NF - 1))
                        if pend[0] is not None:
                            pend[0]()
                            pend[0] = None
                        t_u = stt(xT[:, dc, c0:c0 + n], ps[:, :n], Gtab[:, l, s, cond, dc:dc + 1], xT[:, dc, c0:c0 + n],
                                  ALU.mult, ALU.add, deps=[t])
                        PS.rel(ip, t_u)
                        if hoist:
                            pend[0] = ns_accum(dc, ti, c0, n, t_u)
                    W.done(i, t)
                if pend[0] is not None:
                    pend[0]()
                if hoist:
                    ns['ready'] = True
                return [t_u]

            def qk_chain_a(ps, n, deps):
                si, sqb, sfr = SA.get()
                sqv = sqb.bitcast(BF16)[:, 0:n]
                t_q = act(sqv, ps[:, :n], AF.Square, deps=deps + sfr)
                pi2, ps2, fr2 = PS.get()
                t_m = mm(ps2[:, :n], bones_b[:], sqv, True, True, deps=[t_q] + fr2, sig=True)
                SA.rel(si, t_m)
                return pi2, ps2, t_m

            def qk_chain_b(ps, n, gcol, out_ap, pi2, ps2, t_m, deps):
                ri, rs, rfr = RS.get()
                t_sd = act(rs[:, :n], ps2[:, :n], AF.Ln, deps=[t_m] + rfr, bias=epsT[:, 0:1], scale=1.0 / 64)
                PS.rel(pi2, t_sd)
                t_r = act(rs[:, :n], rs[:, :n], AF.Exp, deps=[t_sd], scale=-0.5)
                t_o = stt(out_ap, ps[:, :n], gcol, rs[:, :n], ALU.mult, ALU.mult, deps=[t_r] + deps)
                RS.rel(ri, t_o)
                return t_o

            def rope(src, out_ap, deps):
                pi, ps, fr = PS.get()
                t_p = mm(ps[:, 0:256], prot[:], src, True, True, deps=deps + fr, sig=True)
                ri, rp, rfr = RP.get()
                t1 = tt(rp[:, 0, :], src, ropec[:], ALU.mult, deps=deps + rfr)
                t2 = tt(rp[:, 1, :], ps[:, 0:256], ropes[:], ALU.mult, deps=[t_p])
                PS.rel(pi, t2)
                t3 = tt(out_ap, rp[:, 0, :], rp[:, 1, :], ALU.add, deps=[t1, t2])
                RP.rel(ri, t3)
                return t3

            def pipeline(units):
                n = len(units)
                ns = max(len(u) for u in units)
                for step in range(n + ns - 1):
                    for sg in range(ns):
                        ui = step - sg
                        if 0 <= ui < n and sg < len(units[ui]):
                            units[ui][sg]()

            def attn_scores(kT, kbase, nkc, qbase, h, Eslots):
                E = []
                for c in range(2):
                    lo = c * 64
                    ei, ev, efr = Eslots[c]
                    tE = None
                    for g0 in range(0, nkc, 2):
                        pi, ps, fr = PS.get()
                        for j in range(2):
                            kc = g0 + j
                            t = mm(ps[:, j * 256:(j + 1) * 256], kT[lo:lo + 64, h, kbase + kc * 128:kbase + (kc + 1) * 128],
                                   qnT[lo:lo + 64, h, qbase:qbase + 256], True, True, deps=fr, sig=(j == 1))
                        tE = act(ev[:, g0 * 256:(g0 + 2) * 256], ps[:, 0:512], AF.Exp, deps=[t] + efr, scale=SCALE)
                        PS.rel(pi, tE)
                    E.append((ev, tE))
                return E

            def attn_pv_a(l, nkc, vsrc, E):
                qs = []
                t_pv = None
                for qt in range(2):
                    pi, ps, fr = PS.get()
                    for c in range(2):
                        ev, tE = E[c]
                        for kc in range(nkc):
                            t_pv = mm(ps[:, c * 256:c * 256 + 129], ev[:, kc * 256 + qt * 128:kc * 256 + (qt + 1) * 128],
                                      vsrc(kc), kc == 0, kc == nkc - 1, deps=[tE] + fr, sig=(c == 1 and kc == nkc - 1))
                    oi, (ob, osb), ofr = OT.get()
                    qs.append({'pi': pi, 'ps': ps, 'oi': oi, 'ob': ob, 'osb': osb, 'ofr': ofr, 'tpv': t_pv})
                for q in qs:
                    psv = q['ps'][:].rearrange("p (c x) -> p c x", c=2)
                    q['t'] = recip(q['osb'][:, 0:2], psv[:, :, 128], deps=[q['tpv']] + q['ofr'])
                for q in qs:
                    q['t0'] = tsc(q['ob'][:, 0, :], q['ps'][:, 0:128], q['osb'][:, 0:1], ALU.mult, deps=[q['t']])
                for q in qs:
                    q['t1'] = tsc(q['ob'][:, 1, :], q['ps'][:, 256:384], q['osb'][:, 1:2], ALU.mult, deps=[q['t']])
                    PS.rel(q['pi'], q['t1'])
                for q in qs:
                    q['t'] = stt(q['ob'][:, 1, :], q['ob'][:, 1, :], sctab[:, l, 3:4], q['ob'][:, 0, :], ALU.mult, ALU.add,
                                 deps=[q['t0'], q['t1']])
                return t_pv, qs

            def attn_pv_b(qs):
                for q in qs:
                    q['t'] = act(q['ob'][:, 0, :], q['ob'][:, 1, :], AF.Square, deps=[q['t']], accum=q['osb'][:, 3:4])
                if len(qs) == 2 and qs[1]['oi'] == qs[0]['oi'] + 1:
                    o2 = os_t[:, qs[0]['oi']:qs[0]['oi'] + 2, :]
                    t = act(o2[:, :, 4:5], o2[:, :, 3:4], AF.Ln, deps=[qs[0]['t'], qs[1]['t']], bias=epsT[:, 0:1], scale=1.0 / 128)
                    t = act(o2[:, :, 5:6], o2[:, :, 4:5], AF.Exp, deps=[t], scale=-0.5)
                    for q in qs:
                        q['t'] = t
                else:
                    for q in qs:
                        q['t'] = act(q['osb'][:, 4:5], q['osb'][:, 3:4], AF.Ln, deps=[q['t']], bias=epsT[:, 0:1], scale=1.0 / 128)
                    for q in qs:
                        q['t'] = act(q['osb'][:, 5:6], q['osb'][:, 4:5], AF.Exp, deps=[q['t']], scale=-0.5)
                for q in qs:
                    q['t'] = tsc(q['ob'][:, 0, :], q['ob'][:, 1, :], q['osb'][:, 5:6], ALU.mult, deps=[q['t']])
                return [(q['oi'], q['ob'], q['t']) for q in qs]

            def attn_finish(l, h, mix_c0, res):
                nq = len(res)
                pi2, ps2, fr2 = PS.get()
                t = None
                for qt, (oi, ob, t_on) in enumerate(res):
                    t = tr(ps2[:, qt * 128:(qt + 1) * 128], ob[:, 0, :], ident[:], deps=[t_on] + fr2, sig=True)
                    OT.rel(oi, t)
                t = amul(xn[:, 4 + h, mix_c0:mix_c0 + nq * 128], ps2[:, 0:nq * 128], sctab[:, l, 2:3], deps=[t])
                PS.rel(pi2, t)

            def mixing(l, t_in):
                xtok = norm(l, 1, t_in)
                P.wait_only('pe', xtok)
                lnk = d['nk'][l]
                lnv = d['nv'][l]
                t_z = []
                for tti in range(8):
                    t_z.append(vcopy(v_aug[:, tti, :, 128:130], ones_b[:, 0:8].rearrange("p (h e) -> p h e", h=4)))
                    t_z.append(vcopy(peo_p[:, tti, 0, :, 64:128], zeros_b[:, 0:128].rearrange("p (j c) -> p j c", j=2)))
                    t_z.append(vcopy(peo_p[:, tti, 1, :, 0:64], zeros_b[:, 0:128].rearrange("p (j c) -> p j c", j=2)))
                pt = {}
                i, ws, tw = W.use(('win', l, 0), [(lambda s_: s_.rearrange("p (k c) -> p k c", k=8),
                                                   d['w_in'][l, :, 0:512].rearrange("(k p) c -> p k c", p=128))])
                wv = ws.rearrange("p (k c) -> p k c", k=8)
                for tti in range(10):
                    pi, ps, fr = PS.get()
                    for k in range(8):
                        t = mm(ps[:, 0:256], xn[:, k, tti * 128:(tti + 1) * 128], wv[:, k, 0:256], k == 0, k == 7, deps=[tw] + fr, sig=(k == 7))
                    if tti < 8:
                        psv = ps[:, 0:256].rearrange("p (j e c) -> p j e c", j=2, e=2)
                        t1 = vcopy(peo_p[:, tti, 0, :, 0:64], psv[:, :, 0, :], deps=[t] + t_z)
                        t2 = vcopy(peo_p[:, tti, 1, :, 64:128], psv[:, :, 1, :], deps=[t] + t_z)
                        PS.rel(pi, t1); PS.rel(pi, t2)
                    else:
                        t1 = evac(pay[:, 2048 + (tti - 8) * 256:2048 + (tti - 7) * 256], ps[:, 0:256], deps=[t])
                        PS.rel(pi, t1)
                last = t
                for c2 in range(2):
                    for (c0, n, cond) in TILES:
                        pi, ps, fr = PS.get()
                        for k in range(8):
                            t = mm(ps[:, :n], wv[:, k, 256 + c2 * 128:256 + (c2 + 1) * 128], xn[:, k, c0:c0 + n], k == 0, k == 7,
                                   deps=[tw] + fr, sig=(k == 7))
                        t1 = evac(fT[:, c2, c0:c0 + n], ps[:, :n], deps=[t])
                        PS.rel(pi, t1)
                W.done(i, t)
                t_fT = t1
                cgw = {}
                for cg in (1, 2):
                    i, ws, tw = W.use(('win', l, cg), [(lambda s_: s_.rearrange("p (k c) -> p k c", k=8),
                                                        d['w_in'][l, :, cg * 512:(cg + 1) * 512].rearrange("(k p) c -> p k c", p=128))])
                    cgw[cg] = {'i': i, 'wv': ws.rearrange("p (k c) -> p k c", k=8), 'tw': tw, 'last': None}

                def qk_unit(cg, c0, n, cond, h):
                    u = {}
                    gcol = vcols[:, l, 98:99] if cg == 1 else vcols[:, l, 99:100]
                    wv = cgw[cg]['wv']
                    tw = cgw[cg]['tw']

                    def s0():
                        pi, ps, fr = PS.get()
                        for k in range(8):
                            t = mm(ps[:, :n], wv[:, k, h * 128:(h + 1) * 128], xn[:, k, c0:c0 + n], k == 0, k == 7,
                                   deps=[tw] + fr, sig=(k == 7))
                        cgw[cg]['last'] = t
                        u['pi'], u['ps'], u['t'] = pi, ps, t

                    def s1():
                        u['c'] = qk_chain_a(u['ps'], n, [u['t']])

                    def s1b():
                        pi, ps = u['pi'], u['ps']
                        pi2, ps2, t_m = u['c']
                        if cond == 0 and cg == 1:
                            t_o = qk_chain_b(ps, n, gcol, qnT[:, h, c0:c0 + n], pi2, ps2, t_m, [])
                            PS.rel(pi, t_o)
                        else:
                            ki, kh, kfr = KH.get()
                            t_o = qk_chain_b(ps, n, gcol, kh[:, :n], pi2, ps2, t_m, kfr)
                            PS.rel(pi, t_o)
                            u['ki'], u['kh'], u['t_o'] = ki, kh, t_o
                            if cond == 0:
                                u['t_c'] = vcopy(knT[:, h, c0:c0 + n], kh[:, :n], deps=[t_o])

                    def s2():
                        if cond == 0 and cg == 1:
                            return
                        ki, kh, t_o = u['ki'], u['kh'], u['t_o']
                        if cond == 0:
                            pi2, ps2, fr2 = PS.get()
                            for sub in range(4):
                                t_t = tr(ps2[:, sub * 128:(sub + 1) * 128], kh[:, sub * 128:(sub + 1) * 128], ident[:],
                                         deps=[t_o] + fr2, sig=(sub == 3))
                            KH.rel(ki, t_t); KH.rel(ki, u['t_c'])
                            si, stg, sfr = STG.get()
                            t_e = vcopy(stg[:, 0:512], ps2[:, 0:512], deps=[t_t] + sfr)
                            PS.rel(pi2, t_e)
                            t_d = P.dma('sp', lnk[c0:c0 + 512, h * 128:(h + 1) * 128].rearrange("(s p) e -> p s e", p=128),
                                        stg[:, 0:512].rearrange("p (s e) -> p s e", s=4), 'st%d' % si, deps=[t_e])
                            STG.rel(si, t_d)
                        else:
                            dst = qnT[:, h, c0:c0 + n] if cg == 1 else pay[:, h * 256:(h + 1) * 256]
                            t_r = rope(kh[:, :n], dst, [t_o])
                            KH.rel(ki, t_r)
                    return [s0, s1, s1b, s2]

                units = []
                for cg in (1, 2):
                    for (c0, n, cond) in TILES:
                        for h in range(4):
                            units.append(qk_unit(cg, c0, n, cond, h))
                pipeline(units)
                for cg in (1, 2):
                    W.done(cgw[cg]['i'], cgw[cg]['last'])
                i, ws, tw = W.use(('win', l, 3), [(lambda s_: s_.rearrange("p (k c) -> p k c", k=8),
                                                   d['w_in'][l, :, 1536:2048].rearrange("(k p) c -> p k c", p=128))])
                wv = ws.rearrange("p (k c) -> p k c", k=8)
                for tti in range(10):
                    pi, ps, fr = PS.get()
                    for k in range(8):
                        t = mm(ps[:, 0:512], xn[:, k, tti * 128:(tti + 1) * 128], wv[:, k, :], k == 0, k == 7, deps=[tw] + fr, sig=(k == 7))
                    if tti < 8:
                        si, stg, sfr = STG.get()
                        t1 = acopy(stg[:, 0:512], ps[:, 0:512], deps=[t] + sfr)
                        t2 = acopy(v_aug[:, tti, :, 0:128], ps[:, 0:512].rearrange("p (h e) -> p h e", h=4), deps=[t])
                        PS.rel(pi, t1); PS.rel(pi, t2)
                        t_d = P.dma('sp', lnv[tti * 128:(tti + 1) * 128, :], stg[:, 0:512], 'st%d' % si, deps=[t1])
                        STG.rel(si, t_d)
                        pt['out'] = pt.get('out', []) + [t_d]
                    else:
                        t1 = evac(pay[:, 1024 + (tti - 8) * 512:1024 + (tti - 7) * 512], ps[:, 0:512], deps=[t])
                        PS.rel(pi, t1)
                W.done(i, t)
                for tti in range(10):
                    for j in range(2):
                        pi, ps, fr = PS.get()
                        t = mm(ps[:, 0:256], fT[:, j, tti * 128:(tti + 1) * 128], csmat[:], True, True, deps=[t_fT] + fr, sig=True)
                        if tti < 8:
                            t1 = evac(AB_p[:, tti, j, :], ps[:, 0:256], deps=[t])
                        else:
                            o0 = 2560 + ((tti - 8) * 2 + j) * 256
                            t1 = evac(pay[:, o0:o0 + 256], ps[:, 0:256], deps=[t])
                        PS.rel(pi, t1)
                P.barrier()
                t_pd = P.dma('pool', ag_in[l], pay, 'ccin')
                t_cc = P.cc(lambda e: e.collective_compute("AllGather", ALU.bypass, replica_groups=[[0, 1, 2, 3], [4, 5, 6, 7]],
                                                           ins=[ag_in[l].opt()], outs=[ag_out[l].opt()]), 'cc', deps=[t_pd])
                P.wait_only('act', [t_pd])
                P.wait_only('dve', [t_pd])


                def mix_pool_fourier(l, ntile, peo, AB, Mv, Dv, c0):
                    for j in range(2):
                        pi, ps, fr = PS.get()
                        cnt = 0
                        for i_ in range(ntile):
                            for eo in range(2):
                                t = mm(ps[:, 0:256], peo[:, i_, eo, j, :], Mv(i_, 2 * j + eo), cnt == 0, cnt == 2 * ntile - 1,
                                       deps=fr, sig=(cnt == 2 * ntile - 1))
                                cnt += 1
                        si, sm, sfr = SM.get()
                        t1 = vcopy(sm[:, 0:256], ps[:, 0:256], deps=[t] + sfr)
                        PS.rel(pi, t1)
                        pi2, ps2, fr2 = PS.get()
                        t = mm(ps2[:, 0:256], pwbd[:, l, j, :], sm[:, 0:256], True, True, deps=[t1] + fr2, sig=True)
                        SM.rel(si, t)
                        t1 = amul(xn[:, j, c0:c0 + 256], ps2[:, 0:256], vcols[:, l, 96 + j:97 + j], deps=[t])
                        PS.rel(pi2, t1)
                    for j in range(2):
                        pi, ps, fr = PS.get()
                        cnt = 0
                        for i_ in range(ntile):
                            for cs_ in range(2):
                                t = mm(ps[:, 0:256], AB[:, i_, j, cs_ * 128:(cs_ + 1) * 128], Dv(i_)[:, cs_ * 256:(cs_ + 1) * 256],
                                       cnt == 0, cnt == 2 * ntile - 1, deps=fr, sig=(cnt == 2 * ntile - 1))
                                cnt += 1
                        si, sm, sfr = SM.get()
                        t1 = acopy(sm[:, 0:256], ps[:, 0:256], deps=[t] + sfr)
                        PS.rel(pi, t1)
                        pi2, ps2, fr2 = PS.get()
                        t = mm(ps2[:, 0:256], fwbd[:, l, j, :], sm[:, 0:256], True, True, deps=[t1] + fr2, sig=True)
                        SM.rel(si, t)
                        t1 = vcopy(xn[:, 2 + j, c0:c0 + 256], ps2[:, 0:256], deps=[t])
                        PS.rel(pi2, t1)
                    return t

                units = []
                for b in range(4):
                    c0 = b * 256
                    mix_pool_fourier(l, 2, peo_p[:, 2 * b:2 * b + 2], AB_p[:, 2 * b:2 * b + 2],
                                     lambda i_, g: poolp[:, i_, g * 256:(g + 1) * 256], lambda i_: dftp[:, i_, :], c0)
                for b in range(4):
                    for h in range(4):
                        def mk(b=b, h=h):
                            u = {}
                            c0 = b * 256

                            def s0():
                                ei, ev, efr = EP.get()
                                u['ei'] = ei
                                u['E'] = attn_scores(knT, c0, 2, c0, h, [(ei, ev[:, c, :], efr) for c in range(2)])

                            def s1():
                                tpv, qs = attn_pv_a(l, 2, lambda kc: v_aug[:, 2 * b + kc, h, 0:129], u['E'])
                                EP.rel(u['ei'], tpv)
                                u['qs'] = qs

                            def s1b():
                                u['res'] = attn_pv_b(u['qs'])

                            def s2():
                                attn_finish(l, h, c0, u['res'])
                            return [s0, s1, s1b, s2]
                        units.append(mk())
                pipeline(units)
                P.barrier()
                t_z2 = []
                for tti in range(10):
                    t_on2 = vcopy(v_all[:, tti, :, 128:130], ones_b[:, 0:8].rearrange("p (h e) -> p h e", h=4))
                    t_z2.append(t_on2)
                for tti in range(8):
                    t_z2.append(vcopy(peo_s[:, tti, 0, :, 64:128], zeros_b[:, 0:128].rearrange("p (j c) -> p j c", j=2)))
                    t_z2.append(vcopy(peo_s[:, tti, 1, :, 0:64], zeros_b[:, 0:128].rearrange("p (j c) -> p j c", j=2)))
                P.wait_only('sp', [t_on2] + t_z2 + [t_cc])
                P.wait_only('pool', [t_on2])
                for i_ in range(2):
                    t_cv = P.dma('pool', v_all[:, i_, :, 0:128], d['cv'][l][i_ * 128:(i_ + 1) * 128, :].rearrange("p (h e) -> p h e", h=4), 'ccin')
                si, stg, sfr = STG.get()
                t_ck = P.dma('sp', stg.rearrange("p (i c) -> p i c", i=2), d['ck'][l].rearrange("(i p) c -> p i c", p=128), 'st%d' % si, deps=sfr)
                for i_ in range(2):
                    pi, ps, fr = PS.get()
                    for h in range(4):
                        t = tr(ps[:, h * 128:(h + 1) * 128], stg[:, i_ * 512 + h * 128:i_ * 512 + (h + 1) * 128], ident[:],
                               deps=[t_ck] + fr, sig=(h == 3))
                    t1 = evac(kT_all[:, :, i_ * 128:(i_ + 1) * 128], ps[:].rearrange("p (h t) -> p h t", h=4), deps=[t])
                    PS.rel(pi, t1)
                STG.rel(si, t)
                gl = []
                ago = ag_out[l]
                P.wait_only('act', [t_on2] + t_z2 + [t_cc])
                for r in range(4):
                    rows = ago[r * 128:(r + 1) * 128, :]
                    gq = 'sp' if r % 2 == 0 else 'act'
                    gl.append(P.dma(gq, kT_all[:, :, 256 + r * 256:256 + (r + 1) * 256],
                                    rows[:, 0:1024].rearrange("p (h t) -> p h t", h=4), 'gld_' + gq))
                    gl.append(P.dma(gq, AB_s[:, 2 * r:2 * r + 2, :, :].rearrange("p i j c -> p (i j c)"), rows[:, 2560:3584], 'gld_' + gq))
                    for i_ in range(2):
                        gl.append(P.dma(gq, v_all[:, 2 + 2 * r + i_, :, 0:128],
                                        rows[:, 1024 + i_ * 512:1024 + (i_ + 1) * 512].rearrange("p (h e) -> p h e", h=4), 'gld_' + gq))
                        pv_ = rows[:, 2048 + i_ * 256:2048 + (i_ + 1) * 256].rearrange("p (j e c) -> p j e c", j=2, e=2)
                        gl.append(P.dma(gq, peo_s[:, 2 * r + i_, 0, :, 0:64], pv_[:, :, 0, :], 'gld_' + gq))
                        gl.append(P.dma(gq, peo_s[:, 2 * r + i_, 1, :, 64:128], pv_[:, :, 1, :], 'gld_' + gq))
                pend = [None]
                wo_st = {'i': [], 'wv': [], 'tw': [], 'last': None}
                for hf in range(2):
                    i, ws, tw = W.use(('wout', l, hf), [(lambda s_: s_.rearrange("p (k c) -> p k c", k=8),
                                                         d['w_out'][l, :, hf * 512:(hf + 1) * 512].rearrange("(k p) c -> p k c", p=128))])
                    wo_st['i'].append(i); wo_st['wv'].append(ws.rearrange("p (k c) -> p k c", k=8)); wo_st['tw'].append(tw)

                def w_out_part(tis):
                    t_u = None
                    for hf in range(2):
                        wv = wo_st['wv'][hf]
                        tw = wo_st['tw'][hf]
                        for q in range(4):
                            dc = hf * 4 + q
                            for ti in tis:
                                c0, n, cond = TILES[ti]
                                pi, ps, fr = PS.get()
                                for k in range(8):
                                    t = mm(ps[:, :n], wv[:, k, q * 128:(q + 1) * 128], xn[:, k, c0:c0 + n], k == 0, k == 7, deps=[tw] + fr, sig=(k == 7))
                                wo_st['last'] = t
                                if pend[0] is not None:
                                    pend[0]()
                                    pend[0] = None
                                t_u = stt(xT[:, dc, c0:c0 + n], ps[:, :n], Gtab[:, l, 1, cond, dc:dc + 1], xT[:, dc, c0:c0 + n],
                                          ALU.mult, ALU.add, deps=[t])
                                PS.rel(pi, t_u)
                                pend[0] = ns_accum(dc, ti, c0, n, t_u)
                    return t_u

                w_out_part([0, 1])
                P.barrier()
                i0, ws0, tw0 = W.use(('pools', l, 0), [(lambda s_: s_.rearrange("p (k c) -> p k c", k=8),
                                                        d['pools'][:, 0:512].rearrange("(k p) c -> p k c", p=128))])
                i1, ws1, tw1 = W.use(('pools', l, 1), [(lambda s_: s_.rearrange("p (k c) -> p k c", k=8),
                                                        d['pools'][:, 512:1024].rearrange("(k p) c -> p k c", p=128))])
                i2, ws2, tw2 = W.use(('dfts', l), [(lambda s_: s_.rearrange("p (k c) -> p k c", k=8),
                                                    d['dfts'].rearrange("(k p) c -> p k c", p=128))])
                P.wait_only('pe', [tw0, tw1, tw2])
                pm = [ws0.rearrange("p (k c) -> p k c", k=8), ws1.rearrange("p (k c) -> p k c", k=8)]
                dm = ws2.rearrange("p (k c) -> p k c", k=8)
                tl = mix_pool_fourier(l, 8, peo_s, AB_s, lambda i_, g: pm[g // 2][:, i_, (g % 2) * 256:(g % 2 + 1) * 256],
                                      lambda i_: dm[:, i_, :], 1024)
                W.done(i0, tl); W.done(i1, tl); W.done(i2, tl)
                def samp_unit(h, qt):
                    u = {}
                    q0 = 1024 + qt * 128

                    def s0():
                        E = []
                        rels = []
                        for c in range(2):
                            lo = c * 64
                            ei, ev, efr = ES.get()
                            rels.append(ei)
                            tE = None
                            for g0 in (0, 4, 8):
                                nk = min(4, 10 - g0)
                                pi, ps, fr = PS.get()
                                for j in range(nk):
                                    kc = g0 + j
                                    t = mm(ps[:, j * 128:(j + 1) * 128], kT_all[lo:lo + 64, h, kc * 128:(kc + 1) * 128],
                                           qnT[lo:lo + 64, h, q0:q0 + 128], True, True, deps=fr, sig=(j == nk - 1))
                                tE = act(ev[:, g0 * 128:(g0 + nk) * 128], ps[:, 0:nk * 128], AF.Exp, deps=[t] + efr, scale=SCALE)
                                PS.rel(pi, tE)
                            E.append((ev, tE))
                        u['E'], u['rels'] = E, rels

                    def s1():
                        pi, ps, fr = PS.get()
                        t_pv = None
                        for c in range(2):
                            ev, tE = u['E'][c]
                            for kc in range(10):
                                t_pv = mm(ps[:, c * 256:c * 256 + 129], ev[:, kc * 128:(kc + 1) * 128], v_all[:, kc, h, 0:129],
                                          kc == 0, kc == 9, deps=[tE] + fr, sig=(c == 1 and kc == 9))
                        for ei in u['rels']:
                            ES.rel(ei, t_pv)
                        oi, (ob, osb), ofr = OT.get()
                        q = {'pi': pi, 'ps': ps, 'oi': oi, 'ob': ob, 'osb': osb}
                        psv = ps[:].rearrange("p (c x) -> p c x", c=2)
                        q['t'] = recip(osb[:, 0:2], psv[:, :, 128], deps=[t_pv] + ofr)
                        q['t0'] = tsc(ob[:, 0, :], ps[:, 0:128], osb[:, 0:1], ALU.mult, deps=[q['t']])
                        q['t1'] = tsc(ob[:, 1, :], ps[:, 256:384], osb[:, 1:2], ALU.mult, deps=[q['t']])
                        PS.rel(pi, q['t1'])
                        q['t'] = stt(ob[:, 1, :], ob[:, 1, :], sctab[:, l, 3:4], ob[:, 0, :], ALU.mult, ALU.add, deps=[q['t0'], q['t1']])
                        u['qs'] = [q]

                    def s1b():
                        u['res'] = attn_pv_b(u['qs'])

                    def s2():
                        attn_finish(l, h, q0, u['res'])
                    return [s0, s1, s1b, s2]

                units = []
                for h in range(4):
                    for qt in range(2):
                        units.append(samp_unit(h, qt))
                pipeline(units)
                P.barrier()
                t_u = w_out_part([2])
                for hf in range(2):
                    W.done(wo_st['i'][hf], wo_st['last'])
                if pend[0] is not None:
                    pend[0]()
                    pend[0] = None
                ns['ready'] = True
                return [t_u]

            P.barrier(skip_w=True)
            pump_ada(0, 4)
            t_in = []
            for l in range(DEPTH):
                t_in = ffn(l, 0, t_in, ada_l=(l, 6, 18))
                pump_ada(l, 18)
                t_in = mixing(l, t_in)
                t_in = ffn(l, 2, t_in, ada_l=((l + 1, 0, 18) if l + 1 < DEPTH else None), hoist=(l + 1 < DEPTH))
                if l + 1 < DEPTH:
                    pump_ada(l + 1, 18)
            P.barrier()
            outs = []
            for tti in range(10):
                dst = d['yp'][tti * 128:(tti + 1) * 128, :] if tti < 8 else d['ys'][(tti - 8) * 128:(tti - 7) * 128, :]
                si, stg, sfr = STG.get()
                te = None
                for half in range(2):
                    pi, ps, pfr = PS.get()
                    for q in range(4):
                        kc = half * 4 + q
                        t = tr(ps[:, q * 128:(q + 1) * 128], xT[:, kc, tti * 128:(tti + 1) * 128], ident[:], deps=pfr, sig=(q == 3))
                    te = evac(stg[:, half * 512:(half + 1) * 512], ps[:, 0:512], deps=[t] + sfr)
                    PS.rel(pi, te)
                    outs.append(te)
                t_d = P.dma('sp', dst, stg, 'st%d' % si, deps=outs[-2:])
                STG.rel(si, t_d)
            P.barrier()

        P1 = Prog(nc, st, dry=True)
        W1 = WRing(P1, wbuf, None)
        emit_all(P1, W1)
        P2 = Prog(nc, st, dry=False)
        W2 = WRing(P2, wbuf, W1.plan)
        emit_all(P2, W2)
        P2.emit()
    return nc


def _consts():
    c = {}
    c['c_ident'] = np.eye(128, dtype=np.float32)
    pm = np.zeros((128, 128), np.float32)
    for i in range(128):
        if (i % 32) < 16:
            pm[i, i + 16] = -1.0
        else:
            pm[i, i - 16] = 1.0
    c['c_prot'] = np.ascontiguousarray(pm.T)
    cc = np.arange(64)
    ang = 2 * np.pi * np.outer(cc, cc) / 64.0
    C64 = np.cos(ang); S64 = np.sin(ang)
    cs = np.zeros((128, 256), np.float64)
    for hh in range(2):
        cs[hh * 64:(hh + 1) * 64, hh * 64:(hh + 1) * 64] = C64
        cs[hh * 64:(hh + 1) * 64, 128 + hh * 64:128 + (hh + 1) * 64] = S64
    c['c_cs'] = cs.astype(np.float32)

    def dft(L, cols):
        l_ = np.arange(L)[:, None].astype(np.float64)
        lp = np.asarray(cols)[None, :].astype(np.float64)
        a = 2 * np.pi * ((l_ * lp) % L) / L
        s = 1.0 / math.sqrt(64.0 * L)
        return np.concatenate([s * np.cos(a), -s * np.sin(a)], axis=1).astype(np.float32)

    def poolm(L, cols):
        out = np.zeros((L, 4, len(cols)), np.float64)
        for g, w in enumerate((2, 4, 8, 16)):
            for ci, t in enumerate(cols):
                lo = min(max(t - w // 2, 0), L); hi = min(max(t + w // 2, 0), L)
                out[lo:hi, g, ci] += 1.0 / (hi - lo)
                out[t, g, ci] -= 1.0
        return out.reshape(L, 4 * len(cols)).astype(np.float32)

    c['c_dftp'] = dft(256, np.arange(256))
    c['c_poolp'] = poolm(256, list(range(256)))
    per_rank = []
    inv = 1.0 / (10000.0 ** (np.arange(0, 32, 2, dtype=np.float32) / 32.0))
    for r in range(4):
        cols = np.arange(r * 256, (r + 1) * 256)
        pr = {'c_dfts': dft(1024, cols), 'c_pools': poolm(1024, list(cols))}
        row = (cols // 64).astype(np.float32); col = (cols % 64).astype(np.float32)
        rc = np.zeros((128, 256), np.float32); rs = np.zeros((128, 256), np.float32)
        for p in range(128):
            dd = p % 64
            if dd < 32:
                a = row * inv[dd % 16]
            else:
                a = col * inv[(dd - 32) % 16]
            rc[p] = np.cos(a.astype(np.float32)); rs[p] = np.sin(a.astype(np.float32))
        pr['c_ropec'] = rc; pr['c_ropes'] = rs
        per_rank.append(pr)
    return c, per_rank


_NC_CACHE = {}


def kernel(**inputs):
    inp = {k: np.ascontiguousarray(np.asarray(v)) for k, v in inputs.items()}
    if 'nc' not in _NC_CACHE:
        _NC_CACHE['nc'] = build_nc()
    nc = _NC_CACHE['nc']
    consts, per_rank = _consts()
    shared = {}
    for nm in ('norm_g', 'ada_w', 'ada_b', 'ffn1_wi', 'ffn1_wo', 'ffn2_wi', 'ffn2_wo', 'w_in', 'w_out', 'q_norm_g', 'k_norm_g',
               'lam_q1', 'lam_k1', 'lam_q2', 'lam_k2', 'attn_out_g', 'pool_w', 'fnet_w', 'pool_scale'):
        shared[nm] = inp[nm].astype(np.float32, copy=False)
    shared.update(consts)
    in_maps = []
    for c in range(8):
        bs, r = c // 4, c % 4
        m = dict(shared)
        m['xp'] = inp['x_prompt'][4 * c:4 * c + 4].reshape(1024, D)
        m['xs'] = inp['x_sample'][bs, r * 256:(r + 1) * 256, :]
        m['ck'] = inp['cache_k'][bs].reshape(DEPTH, 256, 512)
        m['cv'] = inp['cache_v'][bs].reshape(DEPTH, 256, 512)
        m['cond'] = np.stack([inp['c_ctx'], inp['c'][bs]], axis=0)
        m.update(per_rank[r])
        in_maps.append({k: np.ascontiguousarray(v, dtype=np.float32) for k, v in m.items()})
    res = run_bass_kernel_spmd(nc, in_maps, core_ids=list(range(8)))
    R = res.results
    y_prompt = np.concatenate([np.asarray(R[c]['yp']).reshape(4, 256, D) for c in range(8)], axis=0)
    y_sample = np.stack([np.concatenate([np.asarray(R[bs * 4 + r]['ys']) for r in range(4)], axis=0) for bs in range(2)], axis=0)
    nk = np.concatenate([np.asarray(R[c]['nk']).reshape(DEPTH, 4, 256, 4, 128).transpose(1, 0, 2, 3, 4) for c in range(8)], axis=0)
    nv = np.concatenate([np.asarray(R[c]['nv']).reshape(DEPTH, 4, 256, 4, 128).transpose(1, 0, 2, 3, 4) for c in range(8)], axis=0)
    return (y_prompt.astype(np.float32), y_sample.astype(np.float32),
            np.ascontiguousarray(nk, dtype=np.float32), np.ascontiguousarray(nv, dtype=np.float32))
```

```python
import math
import numpy as np
from contextlib import ExitStack
import concourse.bass as bass
import concourse.mybir as mybir
from concourse.bass_utils import run_bass_kernel_spmd

F32 = mybir.dt.float32
BF16 = mybir.dt.bfloat16
AF = mybir.ActivationFunctionType
ALU = mybir.AluOpType
AX = mybir.AxisListType

D = 1024
DEPTH = 2
T = 1280
TILES = [(0, 512, 0), (512, 512, 0), (1024, 256, 1)]
DFF = 2816
NF = 22
EPS = 1e-6
SCALE = 64 ** -0.5
NBUF = 5
ENGS = ('sp', 'act', 'pool', 'dve', 'pe')
PAYC = 3584


class Prog:
    def __init__(self, nc, stack, dry=False):
        self.nc = nc
        self.stack = stack
        self.dry = dry
        self.thunks = {e: [] for e in ENGS}
        self.semh = {}
        self.val = {}
        self.seen = {e: {} for e in ENGS}

    def sem(self, name):
        if name not in self.val:
            if not self.dry:
                self.semh[name] = self.stack.enter_context(self.nc.semaphore(name))
            self.val[name] = 0
        return name

    def _waits(self, eng, toks):
        ws = []
        for t in toks:
            if t is None:
                continue
            name, v = t
            if self.seen[eng].get(name, 0) < v:
                self.seen[eng][name] = v
                ws.append((name, v))
        return ws

    def op(self, eng, fn, deps=(), sig=True):
        if eng in ('act', 'dve') and self.val.get('p_' + eng, 0) > 0:
            deps = list(deps) + [('p_' + eng, self.val['p_' + eng])]
        ws = self._waits(eng, deps)
        tok = None
        if sig:
            name = self.sem('p_' + eng)
            self.val[name] += 1
            tok = (name, self.val[name])
        self.thunks[eng].append((ws, fn, tok, 1))
        return tok

    def dma(self, eng, out, in_, chan, deps=()):
        ws = self._waits(eng, deps)
        name = self.sem(chan)
        self.val[name] += 16
        tok = (name, self.val[name])
        self.thunks[eng].append((ws, lambda e: e.dma_start(out=out, in_=in_), tok, 16))
        return tok

    def cc(self, fn, chan, deps=()):
        ws = self._waits('pool', deps)
        name = self.sem(chan)
        self.val[name] += 1
        tok = (name, self.val[name])
        self.thunks['pool'].append((ws, fn, tok, 'cc'))
        return tok

    def wait_only(self, eng, deps):
        ws = self._waits(eng, deps)
        if ws:
            self.thunks[eng].append((ws, None, None, 0))

    def barrier(self, skip=('out',), skip_w=False):
        if skip_w:
            skip = tuple(skip) + tuple('w%d' % i for i in range(NBUF))
        toks = [(n, v) for n, v in self.val.items() if v > 0 and n not in skip]
        for e in ENGS:
            self.wait_only(e, toks)

    def emit(self):
        with self.nc.Block() as block:
            def run(engname):
                def f(e):
                    for ws, fn, tok, inc in self.thunks[engname]:
                        for (n, v) in ws:
                            e.wait_ge(self.semh[n], v)
                        if fn is None:
                            continue
                        ins = fn(e)
                        if tok is not None:
                            if inc == 'cc':
                                ins.then_inc(self.semh[tok[0]])
                            else:
                                ins.then_inc(self.semh[tok[0]], inc)
                return f
            block.sync(run('sp'))
            block.scalar(run('act'))
            block.gpsimd(run('pool'))
            block.vector(run('dve'))
            block.tensor(run('pe'))


class Rot:
    def __init__(self, aps):
        self.aps = aps
        self.i = 0
        self.free = [[] for _ in aps]
        self.busy = {}

    def get(self, skip_busy=False):
        idx = self.i % len(self.aps)
        if skip_busy:
            for _ in range(len(self.aps)):
                if not self.busy.get(idx, False):
                    break
                self.i += 1
                idx = self.i % len(self.aps)
        self.i += 1
        assert not self.busy.get(idx, False), ('rotating buffer reused before release', idx)
        self.busy[idx] = True
        toks = self.free[idx]
        self.free[idx] = []
        return idx, self.aps[idx], toks

    def rel(self, idx, tok):
        self.free[idx].append(tok)
        self.busy[idx] = False


class WRing:
    def __init__(self, P, wbuf, plan=None):
        self.P = P
        self.wbuf = wbuf
        self.collect = plan is None
        self.plan = [] if plan is None else plan
        self.free = [[] for _ in range(NBUF)]
        self.rec = 0
        self.cur = 0
        self.tok = {}
        self.donef = {}

    def _record(self, j):
        s = j % NBUF
        key, dmas = self.plan[j]
        deps = self.free[s]
        self.free[s] = []
        tok = None
        for (dstf, src) in dmas:
            tok = self.P.dma('pool', dstf(self.wbuf[:, s, :]), src, 'w%d' % s, deps=deps)
        self.tok[j] = tok

    def _advance(self):
        while self.rec < len(self.plan) and (self.rec < NBUF or self.donef.get(self.rec - NBUF)):
            self._record(self.rec)
            self.rec += 1

    def use(self, key, dmas):
        i = self.cur
        self.cur += 1
        if self.collect:
            self.plan.append((key, dmas))
            return i, self.wbuf[:, i % NBUF, :], None
        assert self.plan[i][0] == key, (self.plan[i][0], key)
        self._advance()
        assert i < self.rec, (i, self.rec, key)
        return i, self.wbuf[:, i % NBUF, :], self.tok[i]

    def prestart(self):
        if not self.collect:
            self._advance()

    def done(self, i, tok):
        self.free[i % NBUF].append(tok)
        self.donef[i] = True
        if not self.collect:
            self._advance()


def build_nc():
    nc = bass.Bass("TRN2", target_bir_lowering=False)

    def din(name, shape):
        return nc.dram_tensor(name, list(shape), F32, kind="ExternalInput").ap()

    def dout(name, shape):
        return nc.dram_tensor(name, list(shape), F32, kind="ExternalOutput").ap()

    d = {}
    d['xp'] = din('xp', [1024, D]); d['xs'] = din('xs', [256, D])
    d['ck'] = din('ck', [DEPTH, 256, 512]); d['cv'] = din('cv', [DEPTH, 256, 512])
    d['cond'] = din('cond', [2, D])
    d['norm_g'] = din('norm_g', [DEPTH, 3, D])
    d['ada_w'] = din('ada_w', [DEPTH, D, 9 * D]); d['ada_b'] = din('ada_b', [DEPTH, 9 * D])
    d['ffn1_wi'] = din('ffn1_wi', [DEPTH, D, 2 * DFF]); d['ffn1_wo'] = din('ffn1_wo', [DEPTH, DFF, D])
    d['ffn2_wi'] = din('ffn2_wi', [DEPTH, D, 2 * DFF]); d['ffn2_wo'] = din('ffn2_wo', [DEPTH, DFF, D])
    d['w_in'] = din('w_in', [DEPTH, D, 2048]); d['w_out'] = din('w_out', [DEPTH, D, D])
    for nm in ('q_norm_g', 'k_norm_g', 'lam_q1', 'lam_k1', 'lam_q2', 'lam_k2'):
        d[nm] = din(nm, [DEPTH, 64])
    d['attn_out_g'] = din('attn_out_g', [DEPTH, 128])
    d['pool_w'] = din('pool_w', [DEPTH, 4, 64, 64]); d['fnet_w'] = din('fnet_w', [DEPTH, 4, 64, 64])
    d['pool_scale'] = din('pool_scale', [DEPTH, 256])
    d['ident'] = din('c_ident', [128, 128]); d['prot'] = din('c_prot', [128, 128])
    d['ropec'] = din('c_ropec', [128, 256]); d['ropes'] = din('c_ropes', [128, 256])
    d['cs'] = din('c_cs', [128, 256]); d['dftp'] = din('c_dftp', [256, 512]); d['poolp'] = din('c_poolp', [256, 1024])
    d['dfts'] = din('c_dfts', [1024, 512]); d['pools'] = din('c_pools', [1024, 1024])
    d['yp'] = dout('yp', [1024, D]); d['ys'] = dout('ys', [256, D])
    d['nk'] = dout('nk', [DEPTH, 1024, 512]); d['nv'] = dout('nv', [DEPTH, 1024, 512])
    ag_in = [nc.dram_tensor('ag_in%d' % l, [128, PAYC], BF16).ap() for l in range(DEPTH)]
    ag_out = [nc.dram_tensor('ag_out%d' % l, [512, PAYC], BF16).ap() for l in range(DEPTH)]

    with ExitStack() as st:
        def sb(name, shape, dt):
            return st.enter_context(nc.sbuf_tensor(name, shape, dt))

        xT = sb('xT', [128, 8, T], F32)
        xn = sb('xn', [128, 8, T], BF16)
        ARENA = 29776
        arena = sb('arena', [128, ARENA], BF16)
        wbuf = sb('wbuf', [128, NBUF, 4096], BF16)
        ident = sb('ident', [128, 128], F32)
        ones_f = sb('ones_f', [128, 128], F32)
        ones_b = sb('ones_b', [128, 128], BF16)
        bones_b = sb('bones_b', [128, 128], BF16)
        zeros_b = sb('zeros_b', [128, 128], BF16)
        prot = sb('prot', [128, 128], F32)
        ropec = sb('ropec', [128, 256], F32)
        ropes = sb('ropes', [128, 256], F32)
        csmat = sb('csmat', [128, 256], BF16)
        dftp = sb('dftp', [128, 2, 512], BF16)
        poolp = sb('poolp', [128, 2, 1024], BF16)
        pwbd = sb('pwbd', [128, DEPTH, 2, 128], BF16)
        fwbd = sb('fwbd', [128, DEPTH, 2, 128], BF16)
        vecrows = sb('vecrows', [128, DEPTH, 128], F32)
        vcols = sb('vcols', [128, DEPTH, 128], F32)
        condrows = sb('condrows', [16, 128], F32)
        condsil = sb('condsil', [16, 128], F32)
        scT = sb('scT', [128, 8, 2], BF16)
        modT = sb('modT', [128, DEPTH, 72, 2], F32)
        Atab = sb('Atab', [128, DEPTH, 3, 2, 8], F32)
        Gtab = sb('Gtab', [128, DEPTH, 3, 2, 8], F32)
        sctab = sb('sctab', [128, DEPTH, 8], F32)
        epsT = sb('epsT', [128, 1], F32)
        rstd_t = sb('rstd_t', [128, 2, 512], F32)
        sqc_t = sb('sqc_t', [128, 2, 512], BF16)
        sa_t = sb('sa_t', [128, 2, 512], F32)
        stg_t = sb('stg_t', [128, 2, 1024], F32)
        sm_t = sb('sm_t', [128, 2, 256], BF16)
        ot = sb('ot', [128, 6, 2, 128], F32)
        os_t = sb('os_t', [128, 6, 8], F32)
        rp_t = sb('rp_t', [128, 2, 2, 256], F32)
        lam_t = sb('lam_t', [128, 8], F32)

        psb = [st.enter_context(nc.psum_tensor('ps%d' % i, [128, 512], F32)) for i in range(8)]
        modps = psb[7]

        hv = arena[:, 0:NF * T].rearrange("p (f t) -> p f t", f=NF)
        SCR0 = ARENA - 12288
        tscr = arena[:, SCR0:SCR0 + 8192].bitcast(F32).rearrange("p (k t) -> p k t", k=8)
        sq = arena[:, SCR0 + 8192:SCR0 + 12288].rearrange("p (k t) -> p k t", k=8)
        qnT = arena[:, 0:5120].rearrange("p (h t) -> p h t", h=4)
        R0 = 5120
        knT = arena[:, R0:R0 + 4096].rearrange("p (h t) -> p h t", h=4)
        v_aug = arena[:, R0 + 4096:R0 + 8256].rearrange("p (i h e) -> p i h e", i=8, h=4)
        peo_p = arena[:, R0 + 8256:R0 + 12352].rearrange("p (i e j c) -> p i e j c", i=8, e=2, j=2)
        AB_p = arena[:, R0 + 12352:R0 + 16448].rearrange("p (i j c) -> p i j c", i=8, j=2)
        khat_r = arena[:, R0 + 16448:R0 + 18496].bitcast(F32).rearrange("p (r t) -> p r t", r=2)
        kT_all = arena[:, R0:R0 + 5120].rearrange("p (h t) -> p h t", h=4)
        v_all = arena[:, R0 + 5120:R0 + 10320].rearrange("p (i h e) -> p i h e", i=10, h=4)
        peo_s = arena[:, R0 + 10320:R0 + 14416].rearrange("p (i e j c) -> p i e j c", i=8, e=2, j=2)
        AB_s = arena[:, R0 + 14416:R0 + 18512].rearrange("p (i j c) -> p i j c", i=8, j=2)
        F0 = R0 + 18512
        fT = arena[:, F0:F0 + 2560].rearrange("p (j t) -> p j t", j=2)
        pay = arena[:, F0 + 2560:F0 + 2560 + PAYC]
        E_p = arena[:, F0:F0 + 4096].rearrange("p (r c x) -> p r c x", r=4, c=2)
        E_s = arena[:, F0:F0 + 5120].rearrange("p (r x) -> p r x", r=4)

        def emit_all(P, W):
            PS = Rot([psb[i] for i in range(8)])
            _psget = PS.get
            PS.get = lambda: _psget(skip_busy=True)
            ns = {'bank': [None, None, None], 'tok': [None, None, None], 'ready': False}
            SQC = Rot([sqc_t[:, i, :] for i in range(2)])
            STG = Rot([stg_t[:, i, :] for i in range(2)])
            SA = Rot([sa_t[:, i, :] for i in range(2)])
            RS = Rot([rstd_t[:, i, :] for i in range(2)])
            SM = Rot([sm_t[:, i, :] for i in range(2)])
            OT = Rot([(ot[:, i], os_t[:, i, :]) for i in range(6)])
            RP = Rot([rp_t[:, i] for i in range(2)])
            KH = Rot([khat_r[:, i, :] for i in range(2)])
            EP = Rot([E_p[:, i] for i in range(4)])
            ES = Rot([E_s[:, i, :] for i in range(4)])

            def mm(out, lhsT, rhs, start, stop, deps=(), sig=False):
                return P.op('pe', lambda e: e.matmul(out, lhsT=lhsT, rhs=rhs, start=start, stop=stop), deps=deps, sig=sig)

            def tr(out, in_, idn, deps=(), sig=True):
                return P.op('pe', lambda e: e.transpose(out=out, in_=in_, identity=idn), deps=deps, sig=sig)

            def act(out, in_, func, deps=(), bias=None, scale=None, accum=None):
                kw = {}
                if bias is not None:
                    kw['bias'] = bias
                if scale is not None:
                    kw['scale'] = scale
                if accum is not None:
                    kw['accum_out'] = accum
                return P.op('act', lambda e: e.activation(out=out, in_=in_, func=func, **kw), deps=deps)

            def amul(out, in_, m, deps=()):
                return P.op('act', lambda e: e.mul(out=out, in_=in_, mul=m), deps=deps)

            def acopy(out, in_, deps=()):
                return P.op('act', lambda e: e.copy(out=out, in_=in_), deps=deps)

            def vcopy(out, in_, deps=()):
                return P.op('dve', lambda e: e.tensor_copy(out=out, in_=in_), deps=deps)

            def tt(out, in0, in1, op, deps=()):
                return P.op('dve', lambda e: e.tensor_tensor(out=out, in0=in0, in1=in1, op=op), deps=deps)

            def tsc(out, in0, s1, op0, s2=None, op1=None, deps=()):
                if op1 is None:
                    return P.op('dve', lambda e: e.tensor_scalar(out=out, in0=in0, scalar1=s1, scalar2=None, op0=op0), deps=deps)
                return P.op('dve', lambda e: e.tensor_scalar(out=out, in0=in0, scalar1=s1, scalar2=s2, op0=op0, op1=op1), deps=deps)

            def stt(out, in0, scalar, in1, op0, op1, deps=()):
                return P.op('dve', lambda e: e.scalar_tensor_tensor(out=out, in0=in0, scalar=scalar, in1=in1, op0=op0, op1=op1), deps=deps)

            def recip(out, in_, deps=()):
                return P.op('dve', lambda e: e.reciprocal(out=out, in_=in_), deps=deps)

            def rsum(out, in_, deps=()):
                return P.op('dve', lambda e: e.reduce_sum(out=out, in_=in_, axis=AX.X), deps=deps)

            def memset(ap, v, deps=()):
                return P.op('dve', lambda e: e.memset(ap, v), deps=deps)

            cp_flip = [0]

            def evac(out, in_, deps=()):
                cp_flip[0] ^= 1
                return (acopy if cp_flip[0] else vcopy)(out, in_, deps)

            t_id = P.dma('sp', ident[:], d['ident'], 'cst0')
            xst = [arena[:, i * 2048:(i + 1) * 2048].bitcast(F32) for i in range(10)]
            t_lds = []
            for tti in range(10):
                src = d['xp'][tti * 128:(tti + 1) * 128, :] if tti < 8 else d['xs'][(tti - 8) * 128:(tti - 7) * 128, :]
                t_lds.append(P.dma('sp' if tti % 2 == 0 else 'act', xst[tti], src, 'xl%d' % tti))
            for tti in range(10):
                stg = xst[tti]
                for half in range(2):
                    pi, ps, pfr = PS.get()
                    for q in range(4):
                        kc = half * 4 + q
                        t = tr(ps[:, q * 128:(q + 1) * 128], stg[:, kc * 128:(kc + 1) * 128], ident[:],
                               deps=[t_lds[tti], t_id] + pfr, sig=(q == 3))
                    tcp = evac(xT[:, half * 4:half * 4 + 4, tti * 128:(tti + 1) * 128],
                               ps[:].rearrange("p (q t) -> p q t", q=4), deps=[t])
                    PS.rel(pi, tcp)

            P.dma('sp', prot[:], d['prot'], 'cst')
            P.dma('sp', ropec[:], d['ropec'], 'cst')
            P.dma('sp', ropes[:], d['ropes'], 'cst')
            P.dma('sp', condrows[:], d['cond'].rearrange("j (k c) -> (j k) c", c=128), 'cst')
            t_m = [memset(vecrows[:], 0.0), memset(pwbd[:], 0.0), memset(fwbd[:], 0.0)]
            memset(ones_f[:], 1.0); memset(ones_b[:], 1.0); memset(bones_b[:], 0.0); memset(zeros_b[:], 0.0)
            memset(bones_b[0:64, 0:64], 1.0); memset(bones_b[64:128, 64:128], 1.0)
            memset(epsT[:], EPS)
            P.dma('pool', csmat[:], d['cs'], 'cst2')
            P.dma('pool', dftp[:], d['dftp'].rearrange("(i p) c -> p i c", p=128), 'cst2')
            P.dma('pool', poolp[:], d['poolp'].rearrange("(i p) c -> p i c", p=128), 'cst2')
            P.wait_only('pool', t_m)
            P.wait_only('sp', t_m)
            for l in range(DEPTH):
                for g in range(4):
                    r0 = (g % 2) * 64
                    P.dma('pool', pwbd[r0:r0 + 64, l, g // 2, r0:r0 + 64], d['pool_w'][l, g], 'cst2')
                    P.dma('pool', fwbd[r0:r0 + 64, l, g // 2, r0:r0 + 64], d['fnet_w'][l, g], 'cst2')
            W.prestart()
            for l in range(DEPTH):
                P.dma('sp', vecrows[0:72, l, :], d['ada_b'][l].rearrange("(r c) -> r c", c=128), 'cst')
                P.dma('sp', vecrows[72:96, l, :], d['norm_g'][l].rearrange("s (k c) -> (s k) c", c=128), 'cst')
                P.dma('sp', vecrows[96:98, l, :], d['pool_scale'][l].rearrange("(r c) -> r c", c=128), 'cst')
                P.dma('sp', vecrows[98:99, l, 0:64], d['q_norm_g'][l:l + 1, :], 'cst')
                P.dma('sp', vecrows[98:99, l, 64:128], d['q_norm_g'][l:l + 1, :], 'cst')
                P.dma('sp', vecrows[99:100, l, 0:64], d['k_norm_g'][l:l + 1, :], 'cst')
                P.dma('sp', vecrows[99:100, l, 64:128], d['k_norm_g'][l:l + 1, :], 'cst')
                P.dma('sp', vecrows[100:101, l, :], d['attn_out_g'][l:l + 1, :], 'cst')
                for i, nm in enumerate(('lam_q1', 'lam_k1', 'lam_q2', 'lam_k2')):
                    P.dma('sp', vecrows[101 + i:102 + i, l, 0:64], d[nm][l:l + 1, :], 'cst')
            P.barrier(skip_w=True)
            t = act(condsil[:], condrows[:], AF.Silu)
            pi, ps, fr = PS.get()
            t = tr(ps[:, 0:16], condsil[:], ident[0:16, 0:16], deps=[t] + fr)
            t = vcopy(scT[:].rearrange("p k j -> p j k"), ps[:, 0:16].rearrange("p (j k) -> p j k", j=2), deps=[t])
            PS.rel(pi, t)
            for l in range(DEPTH):
                lam_init = 0.8 - 0.6 * math.exp(-0.3 * l)
                pi, ps, fr = PS.get()
                t = tr(ps[:, 0:128], vecrows[:, l, :], ident[:], deps=fr)
                t = vcopy(vcols[:, l, :], ps[:, 0:128], deps=[t])
                t_vc = t
                PS.rel(pi, t)
                t1 = tt(lam_t[:, 0:1], vcols[:, l, 101:102], vcols[:, l, 102:103], ALU.mult, deps=[t])
                t2 = tt(lam_t[:, 1:2], vcols[:, l, 103:104], vcols[:, l, 104:105], ALU.mult, deps=[t])
                pi, ps, fr = PS.get()
                t = mm(ps[:, 0:2], ones_f[:], lam_t[:, 0:2], True, True, deps=[t1, t2] + fr, sig=True)
                t = act(lam_t[:, 2:4], ps[:, 0:2], AF.Exp, deps=[t])
                PS.rel(pi, t)
                t = tsc(lam_t[:, 4:5], lam_t[:, 3:4], -lam_init, ALU.add, deps=[t])
                t = tt(sctab[:, l, 3:4], lam_t[:, 4:5], lam_t[:, 2:3], ALU.subtract, deps=[t])
                tsc(sctab[:, l, 2:3], vcols[:, l, 100:101], (1.0 - lam_init), ALU.mult, deps=[t_vc])
            P.barrier(skip_w=True)

            ada_state = {'next': [0, 0], 'rd': [None] * 18, 'tab': {}}

            def ada_block(l, b):
                i, ws, tw = W.use(('ada', l, b), [(lambda s: s.rearrange("p (k c) -> p k c", k=8),
                                                   d['ada_w'][l, :, b * 512:(b + 1) * 512].rearrange("(k p) c -> p k c", p=128))])
                wv = ws.rearrange("p (k c) -> p k c", k=8)
                t = None
                pi, ps, fr = PS.get()
                for q in range(4):
                    for k in range(8):
                        t = mm(ps[:, 2 * q:2 * q + 2], wv[:, k, q * 128:(q + 1) * 128], scT[:, k, :], k == 0, k == 7,
                               deps=[tw] + fr, sig=(q == 3 and k == 7))
                W.done(i, t)
                te = tt(modT[:, l, 4 * b:4 * b + 4, :], ps[:, 0:8].rearrange("p (c j) -> p c j", j=2),
                        vcols[:, l, 4 * b:4 * b + 4].unsqueeze(2).to_broadcast([128, 4, 2]), ALU.add, deps=[t])
                PS.rel(pi, te)
                ada_state['rd'][b] = te
                if b % 6 == 3:
                    s = b // 6
                    for cond in range(2):
                        te = stt(Atab[:, l, s, cond, :], modT[:, l, (3 * s + 1) * 8:(3 * s + 2) * 8, cond], 1.0,
                                 vcols[:, l, 72 + s * 8:72 + (s + 1) * 8], ALU.add, ALU.mult, deps=[te])
                    ada_state['tab'][(l, s)] = te
                if b % 6 == 5:
                    s = b // 6
                    for cond in range(2):
                        te = tsc(Gtab[:, l, s, cond, :], modT[:, l, (3 * s + 2) * 8:(3 * s + 3) * 8, cond],
                                 (1.0 if s == 1 else 0.5), ALU.mult, deps=[te])

            def pump_ada(l, upto):
                while ada_state['next'][l] < upto:
                    ada_block(l, ada_state['next'][l])
                    ada_state['next'][l] += 1

            def ns_accum(dc, ti, c0, n, t_u):
                si, sqb, sfr = SQC.get()
                t_q = act(sqb[:, :n], xT[:, dc, c0:c0 + n], AF.Square, deps=[t_u] + sfr)

                def later():
                    fr = []
                    if dc == 0:
                        bi, bap, fr = PS.get()
                        ns['bank'][ti] = (bi, bap)
                    t_m = mm(ns['bank'][ti][1][:, :n], ones_b[:], sqb[:, :n], dc == 0, dc == 7, deps=[t_q] + fr, sig=True)
                    SQC.rel(si, t_m)
                    if dc == 7:
                        ns['tok'][ti] = t_m
                return later

            def norm(l, s, t_in):
                if not ns['ready']:
                    t_ss = None
                    for ti, (c0, n, cond) in enumerate(TILES):
                        t_sq = act(sq[:, :, :n], xT[:, :, c0:c0 + n], AF.Square, deps=list(t_in) + [t_ss])
                        bi, bap, bfr = PS.get()
                        ns['bank'][ti] = (bi, bap)
                        for k in range(8):
                            t_ss = mm(bap[:, :n], ones_b[:], sq[:, k, :n], k == 0, k == 7, deps=[t_sq] + bfr, sig=(k == 7))
                        ns['tok'][ti] = t_ss
                pump_ada(l, 6 * s + 4)
                t_tab = ada_state['tab'][(l, s)]
                toks = []
                for ti, (c0, n, cond) in enumerate(TILES):
                    t_ss = ns['tok'][ti]
                    bi, bap = ns['bank'][ti]
                    ri, rs, rfr = RS.get()
                    t_sd = act(rs[:, :n], bap[:, :n], AF.Ln, deps=[t_ss] + rfr, bias=epsT[:, 0:1], scale=1.0 / D)
                    PS.rel(bi, t_sd)
                    t_r = act(rs[:, :n], rs[:, :n], AF.Exp, deps=[t_sd], scale=-0.5)
                    t_t = tt(tscr[:, :, :n], xT[:, :, c0:c0 + n], rs[:, :n].unsqueeze(1).to_broadcast([128, 8, n]), ALU.mult,
                             deps=[t_r] + list(t_in))
                    RS.rel(ri, t_t)
                    for k in range(5):
                        t_x = act(xn[:, k, c0:c0 + n], tscr[:, k, :n], AF.Identity, deps=[t_t, t_tab],
                                  scale=Atab[:, l, s, cond, k:k + 1], bias=modT[:, l, 3 * s * 8 + k, cond:cond + 1])
                    for k in range(5, 8):
                        t_y = tsc(xn[:, k, c0:c0 + n], tscr[:, k, :n], Atab[:, l, s, cond, k:k + 1], ALU.mult,
                                  s2=modT[:, l, 3 * s * 8 + k, cond:cond + 1], op1=ALU.add, deps=[t_t, t_tab])
                    toks.append(t_x)
                    toks.append(t_y)
                ns['ready'] = False
                return toks

            def ffn(l, s, t_in, ada_l=None, hoist=True):
                wi = d['ffn1_wi'] if s == 0 else d['ffn2_wi']
                wo = d['ffn1_wo'] if s == 0 else d['ffn2_wo']
                xtok = norm(l, s, t_in)
                t_h = None
                t_u = None
                for gi in range(11):
                    wsrc = wi[l].rearrange("(k p) (a f) -> p k a f", p=128, a=2)
                    dmas = []
                    for a in range(2):
                        dmas.append(((lambda s_, a=a: s_.rearrange("p (k a c) -> p k a c", k=8, a=2)[:, :, a, :]),
                                     wsrc[:, :, a, gi * 256:(gi + 1) * 256]))
                    i, ws, tw = W.use(('wi', l, s, gi), dmas)
                    wv = ws.rearrange("p (k a c) -> p k a c", k=8, a=2)
                    last = None
                    for q in range(2):
                        f = gi * 2 + q
                        for ti, (c0, n, cond) in enumerate(TILES):
                            ia, pa, fa = PS.get()
                            for k in range(8):
                                ta = mm(pa[:, :n], wv[:, k, 0, q * 128:(q + 1) * 128], xn[:, k, c0:c0 + n], k == 0, k == 7,
                                        deps=[tw, xtok[2 * ti], xtok[2 * ti + 1]] + fa, sig=(k == 7))
                            ib, pb, fb = PS.get()
                            for k in range(8):
                                tb = mm(pb[:, :n], wv[:, k, 1, q * 128:(q + 1) * 128], xn[:, k, c0:c0 + n], k == 0, k == 7,
                                        deps=[tw] + fb, sig=(k == 7))
                            si, sa, sfr = SA.get()
                            t_s = act(sa[:, :n], pa[:, :n], AF.Silu, deps=[ta] + sfr)
                            PS.rel(ia, t_s)
                            t_h = tt(hv[:, f, c0:c0 + n], sa[:, :n], pb[:, :n], ALU.mult, deps=[t_s, tb])
                            PS.rel(ib, t_h)
                            SA.rel(si, t_h)
                            last = tb
                    W.done(i, last)
                    if ada_l is not None:
                        ll, lo, hi = ada_l
                        pump_ada(ll, min(hi, ada_state['next'][ll] + 2))
                pend = [None]
                for dc in range(8):
                    wsrc = wo[l, :, dc * 128:(dc + 1) * 128].rearrange("(f p) c -> p f c", p=128)
                    dmas = []
                    for (f0, f1) in ((0, 8), (8, 15), (15, 22)):
                        dmas.append(((lambda s_, f0=f0, f1=f1: s_[:, 0:NF * 128].rearrange("p (f c) -> p f c", f=NF)[:, f0:f1, :]),
                                     wsrc[:, f0:f1, :]))
                    i, ws, tw = W.use(('wo', l, s, dc), dmas)
                    wv = ws[:, 0:NF * 128].rearrange("p (f c) -> p f c", f=NF)
                    t = None
                    for ti, (c0, n, cond) in enumerate(TILES):
                        ip, ps, fr = PS.get()
                        for f in range(NF):
                            t = mm(ps[:, :n], wv[:, f, :], hv[:, f, c0:c0 + n], f == 0, f == NF - 1, deps=[tw, t_h] + fr, sig=(f == NF - 1))
                        if pend[0] is not None:
                            pend[0]()
                            pend[0] = None
                        t_u = stt(xT[:, dc, c0:c0 + n], ps[:, :n], Gtab[:, l, s, cond, dc:dc + 1], xT[:, dc, c0:c0 + n],
                                  ALU.mult, ALU.add, deps=[t])
                        PS.rel(ip, t_u)
                        if hoist:
                            pend[0] = ns_accum(dc, ti, c0, n, t_u)
                    W.done(i, t)
                if pend[0] is not None:
                    pend[0]()
                if hoist:
                    ns['ready'] = True
                return [t_u]

            def qk_chain_a(ps, n, deps):
                si, sqb, sfr = SA.get()
                sqv = sqb.bitcast(BF16)[:, 0:n]
                t_q = act(sqv, ps[:, :n], AF.Square, deps=deps + sfr)
                pi2, ps2, fr2 = PS.get()
                t_m = mm(ps2[:, :n], bones_b[:], sqv, True, True, deps=[t_q] + fr2, sig=True)
                SA.rel(si, t_m)
                return pi2, ps2, t_m

            def qk_chain_b(ps, n, gcol, out_ap, pi2, ps2, t_m, deps):
                ri, rs, rfr = RS.get()
                t_sd = act(rs[:, :n], ps2[:, :n], AF.Ln, deps=[t_m] + rfr, bias=epsT[:, 0:1], scale=1.0 / 64)
                PS.rel(pi2, t_sd)
                t_r = act(rs[:, :n], rs[:, :n], AF.Exp, deps=[t_sd], scale=-0.5)
                t_o = stt(out_ap, ps[:, :n], gcol, rs[:, :n], ALU.mult, ALU.mult, deps=[t_r] + deps)
                RS.rel(ri, t_o)
                return t_o

            def rope(src, out_ap, deps):
                pi, ps, fr = PS.get()
                t_p = mm(ps[:, 0:256], prot[:], src, True, True, deps=deps + fr, sig=True)
                ri, rp, rfr = RP.get()
                t1 = tt(rp[:, 0, :], src, ropec[:], ALU.mult, deps=deps + rfr)
                t2 = tt(rp[:, 1, :], ps[:, 0:256], ropes[:], ALU.mult, deps=[t_p])
                PS.rel(pi, t2)
                t3 = tt(out_ap, rp[:, 0, :], rp[:, 1, :], ALU.add, deps=[t1, t2])
                RP.rel(ri, t3)
                return t3

            def pipeline(units):
                n = len(units)
                ns = max(len(u) for u in units)
                for step in range(n + ns - 1):
                    for sg in range(ns):
                        ui = step - sg
                        if 0 <= ui < n and sg < len(units[ui]):
                            units[ui][sg]()

            def attn_scores(kT, kbase, nkc, qbase, h, Eslots):
                E = []
                for c in range(2):
                    lo = c * 64
                    ei, ev, efr = Eslots[c]
                    tE = None
                    for g0 in range(0, nkc, 2):
                        pi, ps, fr = PS.get()
                        for j in range(2):
                            kc = g0 + j
                            t = mm(ps[:, j * 256:(j + 1) * 256], kT[lo:lo + 64, h, kbase + kc * 128:kbase + (kc + 1) * 128],
                                   qnT[lo:lo + 64, h, qbase:qbase + 256], True, True, deps=fr, sig=(j == 1))
                        tE = act(ev[:, g0 * 256:(g0 + 2) * 256], ps[:, 0:512], AF.Exp, deps=[t] + efr, scale=SCALE)
                        PS.rel(pi, tE)
                    E.append((ev, tE))
                return E

            def attn_pv_a(l, nkc, vsrc, E):
                qs = []
                t_pv = None
                for qt in range(2):
                    pi, ps, fr = PS.get()
                    for c in range(2):
                        ev, tE = E[c]
                        for kc in range(nkc):
                            t_pv = mm(ps[:, c * 256:c * 256 + 129], ev[:, kc * 256 + qt * 128:kc * 256 + (qt + 1) * 128],
                                      vsrc(kc), kc == 0, kc == nkc - 1, deps=[tE] + fr, sig=(c == 1 and kc == nkc - 1))
                    oi, (ob, osb), ofr = OT.get()
                    qs.append({'pi': pi, 'ps': ps, 'oi': oi, 'ob': ob, 'osb': osb, 'ofr': ofr, 'tpv': t_pv})
                for q in qs:
                    psv = q['ps'][:].rearrange("p (c x) -> p c x", c=2)
                    q['t'] = recip(q['osb'][:, 0:2], psv[:, :, 128], deps=[q['tpv']] + q['ofr'])
                for q in qs:
                    psv2 = q['ps'][:].rearrange("p (c x) -> p c x", c=2)[:, :, 0:128]
                    q['t1'] = tt(q['ob'][:, 0:2, :], psv2, q['osb'][:, 0:2].unsqueeze(2).to_broadcast([128, 2, 128]), ALU.mult, deps=[q['t']])
                    PS.rel(q['pi'], q['t1'])
                if len(qs) == 2 and qs[1]['oi'] == qs[0]['oi'] + 1:
                    o0 = qs[0]['oi']
                    t = stt(ot[:, o0:o0 + 2, 1, :], ot[:, o0:o0 + 2, 1, :], sctab[:, l, 3:4], ot[:, o0:o0 + 2, 0, :], ALU.mult, ALU.add,
                            deps=[qs[0]['t1'], qs[1]['t1']])
                    for q in qs:
                        q['t'] = t
                else:
                    for q in qs:
                        q['t'] = stt(q['ob'][:, 1, :], q['ob'][:, 1, :], sctab[:, l, 3:4], q['ob'][:, 0, :], ALU.mult, ALU.add, deps=[q['t1']])
                return t_pv, qs

            def attn_pv_b(qs):
                for q in qs:
                    q['t'] = act(q['ob'][:, 0, :], q['ob'][:, 1, :], AF.Square, deps=[q['t']], accum=q['osb'][:, 3:4])
                if len(qs) == 2 and qs[1]['oi'] == qs[0]['oi'] + 1:
                    o2 = os_t[:, qs[0]['oi']:qs[0]['oi'] + 2, :]
                    t = act(o2[:, :, 4:5], o2[:, :, 3:4], AF.Ln, deps=[qs[0]['t'], qs[1]['t']], bias=epsT[:, 0:1], scale=1.0 / 128)
                    t = act(o2[:, :, 5:6], o2[:, :, 4:5], AF.Exp, deps=[t], scale=-0.5)
                    o0 = qs[0]['oi']
                    t = tt(ot[:, o0:o0 + 2, 0, :], ot[:, o0:o0 + 2, 1, :], o2[:, :, 5:6].to_broadcast([128, 2, 128]), ALU.mult, deps=[t])
                    for q in qs:
                        q['t'] = t
                else:
                    for q in qs:
                        q['t'] = act(q['osb'][:, 4:5], q['osb'][:, 3:4], AF.Ln, deps=[q['t']], bias=epsT[:, 0:1], scale=1.0 / 128)
                    for q in qs:
                        q['t'] = act(q['osb'][:, 5:6], q['osb'][:, 4:5], AF.Exp, deps=[q['t']], scale=-0.5)
                    for q in qs:
                        q['t'] = tsc(q['ob'][:, 0, :], q['ob'][:, 1, :], q['osb'][:, 5:6], ALU.mult, deps=[q['t']])
                return [(q['oi'], q['ob'], q['t']) for q in qs]

            def attn_finish(l, h, mix_c0, res):
                nq = len(res)
                pi2, ps2, fr2 = PS.get()
                t = None
                for qt, (oi, ob, t_on) in enumerate(res):
                    t = tr(ps2[:, qt * 128:(qt + 1) * 128], ob[:, 0, :], ident[:], deps=[t_on] + fr2, sig=True)
                    OT.rel(oi, t)
                t = amul(xn[:, 4 + h, mix_c0:mix_c0 + nq * 128], ps2[:, 0:nq * 128], sctab[:, l, 2:3], deps=[t])
                PS.rel(pi2, t)

            def mixing(l, t_in):
                xtok = norm(l, 1, t_in)
                P.wait_only('pe', xtok)
                lnk = d['nk'][l]
                lnv = d['nv'][l]
                t_z = []
                for tti in range(8):
                    t_z.append(vcopy(v_aug[:, tti, :, 128:130], ones_b[:, 0:8].rearrange("p (h e) -> p h e", h=4)))
                    t_z.append(vcopy(peo_p[:, tti, 0, :, 64:128], zeros_b[:, 0:128].rearrange("p (j c) -> p j c", j=2)))
                    t_z.append(vcopy(peo_p[:, tti, 1, :, 0:64], zeros_b[:, 0:128].rearrange("p (j c) -> p j c", j=2)))
                pt = {}
                i, ws, tw = W.use(('win', l, 0), [(lambda s_: s_.rearrange("p (k c) -> p k c", k=8),
                                                   d['w_in'][l, :, 0:512].rearrange("(k p) c -> p k c", p=128))])
                wv = ws.rearrange("p (k c) -> p k c", k=8)
                for tti in range(10):
                    pi, ps, fr = PS.get()
                    for k in range(8):
                        t = mm(ps[:, 0:256], xn[:, k, tti * 128:(tti + 1) * 128], wv[:, k, 0:256], k == 0, k == 7, deps=[tw] + fr, sig=(k == 7))
                    if tti < 8:
                        psv = ps[:, 0:256].rearrange("p (j e c) -> p j e c", j=2, e=2)
                        t1 = vcopy(peo_p[:, tti, 0, :, 0:64], psv[:, :, 0, :], deps=[t] + t_z)
                        t2 = vcopy(peo_p[:, tti, 1, :, 64:128], psv[:, :, 1, :], deps=[t] + t_z)
                        PS.rel(pi, t1); PS.rel(pi, t2)
                    else:
                        t1 = evac(pay[:, 2048 + (tti - 8) * 256:2048 + (tti - 7) * 256], ps[:, 0:256], deps=[t])
                        PS.rel(pi, t1)
                last = t
                for c2 in range(2):
                    for (c0, n, cond) in TILES:
                        pi, ps, fr = PS.get()
                        for k in range(8):
                            t = mm(ps[:, :n], wv[:, k, 256 + c2 * 128:256 + (c2 + 1) * 128], xn[:, k, c0:c0 + n], k == 0, k == 7,
                                   deps=[tw] + fr, sig=(k == 7))
                        t1 = evac(fT[:, c2, c0:c0 + n], ps[:, :n], deps=[t])
                        PS.rel(pi, t1)
                W.done(i, t)
                t_fT = t1
                cgw = {}
                for cg in (1, 2):
                    i, ws, tw = W.use(('win', l, cg), [(lambda s_: s_.rearrange("p (k c) -> p k c", k=8),
                                                        d['w_in'][l, :, cg * 512:(cg + 1) * 512].rearrange("(k p) c -> p k c", p=128))])
                    cgw[cg] = {'i': i, 'wv': ws.rearrange("p (k c) -> p k c", k=8), 'tw': tw, 'last': None}

                def qk_unit(cg, c0, n, cond, h):
                    u = {}
                    gcol = vcols[:, l, 98:99] if cg == 1 else vcols[:, l, 99:100]
                    wv = cgw[cg]['wv']
                    tw = cgw[cg]['tw']

                    def s0():
                        pi, ps, fr = PS.get()
                        for k in range(8):
                            t = mm(ps[:, :n], wv[:, k, h * 128:(h + 1) * 128], xn[:, k, c0:c0 + n], k == 0, k == 7,
                                   deps=[tw] + fr, sig=(k == 7))
                        cgw[cg]['last'] = t
                        u['pi'], u['ps'], u['t'] = pi, ps, t

                    def s1():
                        u['c'] = qk_chain_a(u['ps'], n, [u['t']])

                    def s1b():
                        pi, ps = u['pi'], u['ps']
                        pi2, ps2, t_m = u['c']
                        if cond == 0 and cg == 1:
                            t_o = qk_chain_b(ps, n, gcol, qnT[:, h, c0:c0 + n], pi2, ps2, t_m, [])
                            PS.rel(pi, t_o)
                        else:
                            ki, kh, kfr = KH.get()
                            t_o = qk_chain_b(ps, n, gcol, kh[:, :n], pi2, ps2, t_m, kfr)
                            PS.rel(pi, t_o)
                            u['ki'], u['kh'], u['t_o'] = ki, kh, t_o
                            if cond == 0:
                                u['t_c'] = vcopy(knT[:, h, c0:c0 + n], kh[:, :n], deps=[t_o])

                    def s2():
                        if cond == 0 and cg == 1:
                            return
                        ki, kh, t_o = u['ki'], u['kh'], u['t_o']
                        if cond == 0:
                            pi2, ps2, fr2 = PS.get()
                            for sub in range(4):
                                t_t = tr(ps2[:, sub * 128:(sub + 1) * 128], kh[:, sub * 128:(sub + 1) * 128], ident[:],
                                         deps=[t_o] + fr2, sig=(sub == 3))
                            KH.rel(ki, t_t); KH.rel(ki, u['t_c'])
                            si, stg, sfr = STG.get()
                            t_e = vcopy(stg[:, 0:512], ps2[:, 0:512], deps=[t_t] + sfr)
                            PS.rel(pi2, t_e)
                            t_d = P.dma('sp', lnk[c0:c0 + 512, h * 128:(h + 1) * 128].rearrange("(s p) e -> p s e", p=128),
                                        stg[:, 0:512].rearrange("p (s e) -> p s e", s=4), 'st%d' % si, deps=[t_e])
                            STG.rel(si, t_d)
                        else:
                            dst = qnT[:, h, c0:c0 + n] if cg == 1 else pay[:, h * 256:(h + 1) * 256]
                            t_r = rope(kh[:, :n], dst, [t_o])
                            KH.rel(ki, t_r)
                    return [s0, s1, s1b, s2]

                units = []
                for cg in (1, 2):
                    for (c0, n, cond) in TILES:
                        for h in range(4):
                            units.append(qk_unit(cg, c0, n, cond, h))
                pipeline(units)
                for cg in (1, 2):
                    W.done(cgw[cg]['i'], cgw[cg]['last'])
                i, ws, tw = W.use(('win', l, 3), [(lambda s_: s_.rearrange("p (k c) -> p k c", k=8),
                                                   d['w_in'][l, :, 1536:2048].rearrange("(k p) c -> p k c", p=128))])
                wv = ws.rearrange("p (k c) -> p k c", k=8)
                for tti in range(10):
                    pi, ps, fr = PS.get()
                    for k in range(8):
                        t = mm(ps[:, 0:512], xn[:, k, tti * 128:(tti + 1) * 128], wv[:, k, :], k == 0, k == 7, deps=[tw] + fr, sig=(k == 7))
                    if tti < 8:
                        si, stg, sfr = STG.get()
                        t1 = acopy(stg[:, 0:512], ps[:, 0:512], deps=[t] + sfr)
                        t2 = acopy(v_aug[:, tti, :, 0:128], ps[:, 0:512].rearrange("p (h e) -> p h e", h=4), deps=[t])
                        PS.rel(pi, t1); PS.rel(pi, t2)
                        t_d = P.dma('sp', lnv[tti * 128:(tti + 1) * 128, :], stg[:, 0:512], 'st%d' % si, deps=[t1])
                        STG.rel(si, t_d)
                        pt['out'] = pt.get('out', []) + [t_d]
                    else:
                        t1 = evac(pay[:, 1024 + (tti - 8) * 512:1024 + (tti - 7) * 512], ps[:, 0:512], deps=[t])
                        PS.rel(pi, t1)
                W.done(i, t)
                for tti in range(10):
                    for j in range(2):
                        pi, ps, fr = PS.get()
                        t = mm(ps[:, 0:256], fT[:, j, tti * 128:(tti + 1) * 128], csmat[:], True, True, deps=[t_fT] + fr, sig=True)
                        if tti < 8:
                            t1 = evac(AB_p[:, tti, j, :], ps[:, 0:256], deps=[t])
                        else:
                            o0 = 2560 + ((tti - 8) * 2 + j) * 256
                            t1 = evac(pay[:, o0:o0 + 256], ps[:, 0:256], deps=[t])
                        PS.rel(pi, t1)
                P.barrier()
                t_pd = P.dma('pool', ag_in[l], pay, 'ccin')
                t_cc = P.cc(lambda e: e.collective_compute("AllGather", ALU.bypass, replica_groups=[[0, 1, 2, 3], [4, 5, 6, 7]],
                                                           ins=[ag_in[l].opt()], outs=[ag_out[l].opt()]), 'cc', deps=[t_pd])
                P.wait_only('act', [t_pd])
                P.wait_only('dve', [t_pd])


                def mix_pool_fourier(l, ntile, peo, AB, Mv, Dv, c0):
                    for j in range(2):
                        pi, ps, fr = PS.get()
                        cnt = 0
                        for i_ in range(ntile):
                            for eo in range(2):
                                t = mm(ps[:, 0:256], peo[:, i_, eo, j, :], Mv(i_, 2 * j + eo), cnt == 0, cnt == 2 * ntile - 1,
                                       deps=fr, sig=(cnt == 2 * ntile - 1))
                                cnt += 1
                        si, sm, sfr = SM.get()
                        t1 = vcopy(sm[:, 0:256], ps[:, 0:256], deps=[t] + sfr)
                        PS.rel(pi, t1)
                        pi2, ps2, fr2 = PS.get()
                        t = mm(ps2[:, 0:256], pwbd[:, l, j, :], sm[:, 0:256], True, True, deps=[t1] + fr2, sig=True)
                        SM.rel(si, t)
                        t1 = amul(xn[:, j, c0:c0 + 256], ps2[:, 0:256], vcols[:, l, 96 + j:97 + j], deps=[t])
                        PS.rel(pi2, t1)
                    for j in range(2):
                        pi, ps, fr = PS.get()
                        cnt = 0
                        for i_ in range(ntile):
                            for cs_ in range(2):
                                t = mm(ps[:, 0:256], AB[:, i_, j, cs_ * 128:(cs_ + 1) * 128], Dv(i_)[:, cs_ * 256:(cs_ + 1) * 256],
                                       cnt == 0, cnt == 2 * ntile - 1, deps=fr, sig=(cnt == 2 * ntile - 1))
                                cnt += 1
                        si, sm, sfr = SM.get()
                        t1 = acopy(sm[:, 0:256], ps[:, 0:256], deps=[t] + sfr)
                        PS.rel(pi, t1)
                        pi2, ps2, fr2 = PS.get()
                        t = mm(ps2[:, 0:256], fwbd[:, l, j, :], sm[:, 0:256], True, True, deps=[t1] + fr2, sig=True)
                        SM.rel(si, t)
                        t1 = vcopy(xn[:, 2 + j, c0:c0 + 256], ps2[:, 0:256], deps=[t])
                        PS.rel(pi2, t1)
                    return t

                units = []
                for b in range(4):
                    c0 = b * 256
                    mix_pool_fourier(l, 2, peo_p[:, 2 * b:2 * b + 2], AB_p[:, 2 * b:2 * b + 2],
                                     lambda i_, g: poolp[:, i_, g * 256:(g + 1) * 256], lambda i_: dftp[:, i_, :], c0)
                for b in range(4):
                    for h in range(4):
                        def mk(b=b, h=h):
                            u = {}
                            c0 = b * 256

                            def s0():
                                ei, ev, efr = EP.get()
                                u['ei'] = ei
                                u['E'] = attn_scores(knT, c0, 2, c0, h, [(ei, ev[:, c, :], efr) for c in range(2)])

                            def s1():
                                tpv, qs = attn_pv_a(l, 2, lambda kc: v_aug[:, 2 * b + kc, h, 0:129], u['E'])
                                EP.rel(u['ei'], tpv)
                                u['qs'] = qs

                            def s1b():
                                u['res'] = attn_pv_b(u['qs'])

                            def s2():
                                attn_finish(l, h, c0, u['res'])
                            return [s0, s1, s1b, s2]
                        units.append(mk())
                pipeline(units)
                P.barrier()
                t_z2 = []
                for tti in range(10):
                    t_on2 = vcopy(v_all[:, tti, :, 128:130], ones_b[:, 0:8].rearrange("p (h e) -> p h e", h=4))
                    t_z2.append(t_on2)
                for tti in range(8):
                    t_z2.append(vcopy(peo_s[:, tti, 0, :, 64:128], zeros_b[:, 0:128].rearrange("p (j c) -> p j c", j=2)))
                    t_z2.append(vcopy(peo_s[:, tti, 1, :, 0:64], zeros_b[:, 0:128].rearrange("p (j c) -> p j c", j=2)))
                P.wait_only('sp', [t_on2] + t_z2 + [t_cc])
                P.wait_only('pool', [t_on2])
                for i_ in range(2):
                    t_cv = P.dma('pool', v_all[:, i_, :, 0:128], d['cv'][l][i_ * 128:(i_ + 1) * 128, :].rearrange("p (h e) -> p h e", h=4), 'ccin')
                si, stg, sfr = STG.get()
                t_ck = P.dma('sp', stg.rearrange("p (i c) -> p i c", i=2), d['ck'][l].rearrange("(i p) c -> p i c", p=128), 'st%d' % si, deps=sfr)
                for i_ in range(2):
                    pi, ps, fr = PS.get()
                    for h in range(4):
                        t = tr(ps[:, h * 128:(h + 1) * 128], stg[:, i_ * 512 + h * 128:i_ * 512 + (h + 1) * 128], ident[:],
                               deps=[t_ck] + fr, sig=(h == 3))
                    t1 = evac(kT_all[:, :, i_ * 128:(i_ + 1) * 128], ps[:].rearrange("p (h t) -> p h t", h=4), deps=[t])
                    PS.rel(pi, t1)
                STG.rel(si, t)
                gl = []
                ago = ag_out[l]
                P.wait_only('act', [t_on2] + t_z2 + [t_cc])
                for r in range(4):
                    rows = ago[r * 128:(r + 1) * 128, :]
                    gq = 'sp' if r % 2 == 0 else 'act'
                    gl.append(P.dma(gq, kT_all[:, :, 256 + r * 256:256 + (r + 1) * 256],
                                    rows[:, 0:1024].rearrange("p (h t) -> p h t", h=4), 'gld_' + gq))
                    gl.append(P.dma(gq, AB_s[:, 2 * r:2 * r + 2, :, :].rearrange("p i j c -> p (i j c)"), rows[:, 2560:3584], 'gld_' + gq))
                    for i_ in range(2):
                        gl.append(P.dma(gq, v_all[:, 2 + 2 * r + i_, :, 0:128],
                                        rows[:, 1024 + i_ * 512:1024 + (i_ + 1) * 512].rearrange("p (h e) -> p h e", h=4), 'gld_' + gq))
                        pv_ = rows[:, 2048 + i_ * 256:2048 + (i_ + 1) * 256].rearrange("p (j e c) -> p j e c", j=2, e=2)
                        gl.append(P.dma(gq, peo_s[:, 2 * r + i_, 0, :, 0:64], pv_[:, :, 0, :], 'gld_' + gq))
                        gl.append(P.dma(gq, peo_s[:, 2 * r + i_, 1, :, 64:128], pv_[:, :, 1, :], 'gld_' + gq))
                pend = [None]
                wo_st = {'i': [], 'wv': [], 'tw': [], 'last': None}
                for hf in range(2):
                    i, ws, tw = W.use(('wout', l, hf), [(lambda s_: s_.rearrange("p (k c) -> p k c", k=8),
                                                         d['w_out'][l, :, hf * 512:(hf + 1) * 512].rearrange("(k p) c -> p k c", p=128))])
                    wo_st['i'].append(i); wo_st['wv'].append(ws.rearrange("p (k c) -> p k c", k=8)); wo_st['tw'].append(tw)

                def w_out_part(tis):
                    t_u = None
                    for hf in range(2):
                        wv = wo_st['wv'][hf]
                        tw = wo_st['tw'][hf]
                        for q in range(4):
                            dc = hf * 4 + q
                            for ti in tis:
                                c0, n, cond = TILES[ti]
                                pi, ps, fr = PS.get()
                                for k in range(8):
                                    t = mm(ps[:, :n], wv[:, k, q * 128:(q + 1) * 128], xn[:, k, c0:c0 + n], k == 0, k == 7, deps=[tw] + fr, sig=(k == 7))
                                wo_st['last'] = t
                                if pend[0] is not None:
                                    pend[0]()
                                    pend[0] = None
                                t_u = stt(xT[:, dc, c0:c0 + n], ps[:, :n], Gtab[:, l, 1, cond, dc:dc + 1], xT[:, dc, c0:c0 + n],
                                          ALU.mult, ALU.add, deps=[t])
                                PS.rel(pi, t_u)
                                pend[0] = ns_accum(dc, ti, c0, n, t_u)
                    return t_u

                w_out_part([0, 1])
                P.barrier()
                i0, ws0, tw0 = W.use(('pools', l, 0), [(lambda s_: s_.rearrange("p (k c) -> p k c", k=8),
                                                        d['pools'][:, 0:512].rearrange("(k p) c -> p k c", p=128))])
                i1, ws1, tw1 = W.use(('pools', l, 1), [(lambda s_: s_.rearrange("p (k c) -> p k c", k=8),
                                                        d['pools'][:, 512:1024].rearrange("(k p) c -> p k c", p=128))])
                i2, ws2, tw2 = W.use(('dfts', l), [(lambda s_: s_.rearrange("p (k c) -> p k c", k=8),
                                                    d['dfts'].rearrange("(k p) c -> p k c", p=128))])
                P.wait_only('pe', [tw0, tw1, tw2])
                pm = [ws0.rearrange("p (k c) -> p k c", k=8), ws1.rearrange("p (k c) -> p k c", k=8)]
                dm = ws2.rearrange("p (k c) -> p k c", k=8)
                tl = mix_pool_fourier(l, 8, peo_s, AB_s, lambda i_, g: pm[g // 2][:, i_, (g % 2) * 256:(g % 2 + 1) * 256],
                                      lambda i_: dm[:, i_, :], 1024)
                W.done(i0, tl); W.done(i1, tl); W.done(i2, tl)
                def samp_unit(h, qt):
                    u = {}
                    q0 = 1024 + qt * 128

                    def s0():
                        E = []
                        rels = []
                        for c in range(2):
                            lo = c * 64
                            ei, ev, efr = ES.get()
                            rels.append(ei)
                            tE = None
                            for g0 in (0, 4, 8):
                                nk = min(4, 10 - g0)
                                pi, ps, fr = PS.get()
                                for j in range(nk):
                                    kc = g0 + j
                                    t = mm(ps[:, j * 128:(j + 1) * 128], kT_all[lo:lo + 64, h, kc * 128:(kc + 1) * 128],
                                           qnT[lo:lo + 64, h, q0:q0 + 128], True, True, deps=fr, sig=(j == nk - 1))
                                tE = act(ev[:, g0 * 128:(g0 + nk) * 128], ps[:, 0:nk * 128], AF.Exp, deps=[t] + efr, scale=SCALE)
                                PS.rel(pi, tE)
                            E.append((ev, tE))
                        u['E'], u['rels'] = E, rels

                    def s1():
                        pi, ps, fr = PS.get()
                        t_pv = None
                        for c in range(2):
                            ev, tE = u['E'][c]
                            for kc in range(10):
                                t_pv = mm(ps[:, c * 256:c * 256 + 129], ev[:, kc * 128:(kc + 1) * 128], v_all[:, kc, h, 0:129],
                                          kc == 0, kc == 9, deps=[tE] + fr, sig=(c == 1 and kc == 9))
                        for ei in u['rels']:
                            ES.rel(ei, t_pv)
                        oi, (ob, osb), ofr = OT.get()
                        q = {'pi': pi, 'ps': ps, 'oi': oi, 'ob': ob, 'osb': osb}
                        psv = ps[:].rearrange("p (c x) -> p c x", c=2)
                        q['t'] = recip(osb[:, 0:2], psv[:, :, 128], deps=[t_pv] + ofr)
                        q['t1'] = tt(ob[:, 0:2, :], psv[:, :, 0:128], osb[:, 0:2].unsqueeze(2).to_broadcast([128, 2, 128]), ALU.mult, deps=[q['t']])
                        PS.rel(pi, q['t1'])
                        q['t'] = stt(ob[:, 1, :], ob[:, 1, :], sctab[:, l, 3:4], ob[:, 0, :], ALU.mult, ALU.add, deps=[q['t1']])
                        u['qs'] = [q]

                    def s1b():
                        u['res'] = attn_pv_b(u['qs'])

                    def s2():
                        attn_finish(l, h, q0, u['res'])
                    return [s0, s1, s1b, s2]

                units = []
                for h in range(4):
                    for qt in range(2):
                        units.append(samp_unit(h, qt))
                pipeline(units)
                P.barrier()
                t_u = w_out_part([2])
                for hf in range(2):
                    W.done(wo_st['i'][hf], wo_st['last'])
                if pend[0] is not None:
                    pend[0]()
                    pend[0] = None
                ns['ready'] = True
                return [t_u]

            P.barrier(skip_w=True)
            pump_ada(0, 4)
            t_in = []
            for l in range(DEPTH):
                t_in = ffn(l, 0, t_in, ada_l=(l, 6, 18))
                pump_ada(l, 18)
                t_in = mixing(l, t_in)
                t_in = ffn(l, 2, t_in, ada_l=((l + 1, 0, 18) if l + 1 < DEPTH else None), hoist=(l + 1 < DEPTH))
                if l + 1 < DEPTH:
                    pump_ada(l + 1, 18)
            P.barrier()
            outs = []
            for tti in range(10):
                dst = d['yp'][tti * 128:(tti + 1) * 128, :] if tti < 8 else d['ys'][(tti - 8) * 128:(tti - 7) * 128, :]
                si, stg, sfr = STG.get()
                te = None
                for half in range(2):
                    pi, ps, pfr = PS.get()
                    for q in range(4):
                        kc = half * 4 + q
                        t = tr(ps[:, q * 128:(q + 1) * 128], xT[:, kc, tti * 128:(tti + 1) * 128], ident[:], deps=pfr, sig=(q == 3))
                    te = evac(stg[:, half * 512:(half + 1) * 512], ps[:, 0:512], deps=[t] + sfr)
                    PS.rel(pi, te)
                    outs.append(te)
                t_d = P.dma('sp', dst, stg, 'st%d' % si, deps=outs[-2:])
                STG.rel(si, t_d)
            P.barrier()

        P1 = Prog(nc, st, dry=True)
        W1 = WRing(P1, wbuf, None)
        emit_all(P1, W1)
        P2 = Prog(nc, st, dry=False)
        W2 = WRing(P2, wbuf, W1.plan)
        emit_all(P2, W2)
        P2.emit()
    return nc


def _consts():
    c = {}
    c['c_ident'] = np.eye(128, dtype=np.float32)
    pm = np.zeros((128, 128), np.float32)
    for i in range(128):
        if (i % 32) < 16:
            pm[i, i + 16] = -1.0
        else:
            pm[i, i - 16] = 1.0
    c['c_prot'] = np.ascontiguousarray(pm.T)
    cc = np.arange(64)
    ang = 2 * np.pi * np.outer(cc, cc) / 64.0
    C64 = np.cos(ang); S64 = np.sin(ang)
    cs = np.zeros((128, 256), np.float64)
    for hh in range(2):
        cs[hh * 64:(hh + 1) * 64, hh * 64:(hh + 1) * 64] = C64
        cs[hh * 64:(hh + 1) * 64, 128 + hh * 64:128 + (hh + 1) * 64] = S64
    c['c_cs'] = cs.astype(np.float32)

    def dft(L, cols):
        l_ = np.arange(L)[:, None].astype(np.float64)
        lp = np.asarray(cols)[None, :].astype(np.float64)
        a = 2 * np.pi * ((l_ * lp) % L) / L
        s = 1.0 / math.sqrt(64.0 * L)
        return np.concatenate([s * np.cos(a), -s * np.sin(a)], axis=1).astype(np.float32)

    def poolm(L, cols):
        out = np.zeros((L, 4, len(cols)), np.float64)
        for g, w in enumerate((2, 4, 8, 16)):
            for ci, t in enumerate(cols):
                lo = min(max(t - w // 2, 0), L); hi = min(max(t + w // 2, 0), L)
                out[lo:hi, g, ci] += 1.0 / (hi - lo)
                out[t, g, ci] -= 1.0
        return out.reshape(L, 4 * len(cols)).astype(np.float32)

    c['c_dftp'] = dft(256, np.arange(256))
    c['c_poolp'] = poolm(256, list(range(256)))
    per_rank = []
    inv = 1.0 / (10000.0 ** (np.arange(0, 32, 2, dtype=np.float32) / 32.0))
    for r in range(4):
        cols = np.arange(r * 256, (r + 1) * 256)
        pr = {'c_dfts': dft(1024, cols), 'c_pools': poolm(1024, list(cols))}
        row = (cols // 64).astype(np.float32); col = (cols % 64).astype(np.float32)
        rc = np.zeros((128, 256), np.float32); rs = np.zeros((128, 256), np.float32)
        for p in range(128):
            dd = p % 64
            if dd < 32:
                a = row * inv[dd % 16]
            else:
                a = col * inv[(dd - 32) % 16]
            rc[p] = np.cos(a.astype(np.float32)); rs[p] = np.sin(a.astype(np.float32))
        pr['c_ropec'] = rc; pr['c_ropes'] = rs
        per_rank.append(pr)
    return c, per_rank


_NC_CACHE = {}


def kernel(**inputs):
    inp = {k: np.ascontiguousarray(np.asarray(v)) for k, v in inputs.items()}
    if 'nc' not in _NC_CACHE:
        _NC_CACHE['nc'] = build_nc()
    nc = _NC_CACHE['nc']
    consts, per_rank = _consts()
    shared = {}
    for nm in ('norm_g', 'ada_w', 'ada_b', 'ffn1_wi', 'ffn1_wo', 'ffn2_wi', 'ffn2_wo', 'w_in', 'w_out', 'q_norm_g', 'k_norm_g',
               'lam_q1', 'lam_k1', 'lam_q2', 'lam_k2', 'attn_out_g', 'pool_w', 'fnet_w', 'pool_scale'):
        shared[nm] = inp[nm].astype(np.float32, copy=False)
    shared.update(consts)
    in_maps = []
    for c in range(8):
        bs, r = c // 4, c % 4
        m = dict(shared)
        m['xp'] = inp['x_prompt'][4 * c:4 * c + 4].reshape(1024, D)
        m['xs'] = inp['x_sample'][bs, r * 256:(r + 1) * 256, :]
        m['ck'] = inp['cache_k'][bs].reshape(DEPTH, 256, 512)
        m['cv'] = inp['cache_v'][bs].reshape(DEPTH, 256, 512)
        m['cond'] = np.stack([inp['c_ctx'], inp['c'][bs]], axis=0)
        m.update(per_rank[r])
        in_maps.append({k: np.ascontiguousarray(v, dtype=np.float32) for k, v in m.items()})
    res = run_bass_kernel_spmd(nc, in_maps, core_ids=list(range(8)))
    R = res.results
    y_prompt = np.concatenate([np.asarray(R[c]['yp']).reshape(4, 256, D) for c in range(8)], axis=0)
    y_sample = np.stack([np.concatenate([np.asarray(R[bs * 4 + r]['ys']) for r in range(4)], axis=0) for bs in range(2)], axis=0)
    nk = np.concatenate([np.asarray(R[c]['nk']).reshape(DEPTH, 4, 256, 4, 128).transpose(1, 0, 2, 3, 4) for c in range(8)], axis=0)
    nv = np.concatenate([np.asarray(R[c]['nv']).reshape(DEPTH, 4, 256, 4, 128).transpose(1, 0, 2, 3, 4) for c in range(8)], axis=0)
    return (y_prompt.astype(np.float32), y_sample.astype(np.float32),
            np.ascontiguousarray(nk, dtype=np.float32), np.ascontiguousarray(nv, dtype=np.float32))
```
